# Optimizing a Trainium2 kernel written in Bass

```python
import math
import jax, jax.numpy as jnp
from jax import lax
import numpy as np

D_MODEL = 1024
BATCH = 4
SEQ = 8192
DEPTH = 1

PLE_DIM = 256
NORM_EPS = 1e-6
DA_HEADS = 4
DA_HEAD_DIM = 64
DA_V_DIM = 2 * DA_HEAD_DIM
DA_QK = DA_HEADS * 2 * DA_HEAD_DIM
DA_WIDTH = DA_HEADS * DA_V_DIM
DA_IN_WIDTH = 2 * DA_QK + DA_WIDTH
ROPE_THETA = 500000.0
ROT_DIM = DA_HEAD_DIM // 4
Q_BLOCK = 128
RW_HEAD = 64
RW_WIDTH = D_MODEL - DA_WIDTH
RW_HEADS = RW_WIDTH // RW_HEAD
DECAY_LORA = 64
AAA_LORA = 64
GATE_LORA = 128
RW_IN_WIDTH = 3 * RW_WIDTH + DECAY_LORA + AAA_LORA + GATE_LORA
RW_GN_EPS = 64e-5
IN_WIDTH = DA_IN_WIDTH + RW_IN_WIDTH
PEER_HEADS = 8
N_KEYS = 128
N_EXPERTS = N_KEYS * N_KEYS
PEER_TOPK = 16
PEER_QDIM = 256
PEER_HALF = PEER_QDIM // 2
TOKEN_BLOCK = 128

kernel_name = "hymba_diffattn_rwkv7_peer_block"


def rms_norm(x, g, eps=NORM_EPS):
    xf = x.astype(jnp.float32)
    y = xf * lax.rsqrt(jnp.mean(xf * xf, axis=-1, keepdims=True) + eps)
    return (y * g.astype(jnp.float32)).astype(x.dtype)


def partial_rope(t, positions):
    half = ROT_DIM // 2
    inv_freq = ROPE_THETA ** (-jnp.arange(half, dtype=jnp.float32) * 2.0 / ROT_DIM)
    ang = positions.astype(jnp.float32)[..., None] * inv_freq
    cos = jnp.cos(ang)[:, :, None, None, :]
    sin = jnp.sin(ang)[:, :, None, None, :]
    tf = t.astype(jnp.float32)
    t1 = tf[..., :half]
    t2 = tf[..., half:ROT_DIM]
    out = jnp.concatenate([t1 * cos - t2 * sin, t2 * cos + t1 * sin, tf[..., ROT_DIM:]], axis=-1)
    return out.astype(t.dtype)


def diff_attention(z_da, positions, lam_q1, lam_k1, lam_q2, lam_k2, subln_g, lam_init):
    B, S, _ = z_da.shape
    q, k, v = jnp.split(z_da, [DA_QK, 2 * DA_QK], axis=-1)
    q = partial_rope(q.reshape(B, S, DA_HEADS, 2, DA_HEAD_DIM), positions)
    k = partial_rope(k.reshape(B, S, DA_HEADS, 2, DA_HEAD_DIM), positions)
    q = (q * DA_HEAD_DIM ** -0.5).transpose(0, 2, 3, 1, 4)
    k = k.transpose(0, 2, 3, 1, 4)
    v = v.reshape(B, S, DA_HEADS, DA_V_DIM).transpose(0, 2, 1, 3)
    f32 = jnp.float32
    lam = (jnp.exp(jnp.sum(lam_q1.astype(f32) * lam_k1.astype(f32)))
           - jnp.exp(jnp.sum(lam_q2.astype(f32) * lam_k2.astype(f32))) + lam_init)
    key_pos = jnp.arange(S)

    def block(bi):
        start = bi * Q_BLOCK
        qb = lax.dynamic_slice_in_dim(q, start, Q_BLOCK, axis=3)
        s = jnp.einsum('bhcqd,bhckd->bhcqk', qb, k).astype(f32)
        causal = key_pos[None, :] <= (start + jnp.arange(Q_BLOCK))[:, None]
        pr = jax.nn.softmax(jnp.where(causal, s, -jnp.inf), axis=-1)
        a = pr[:, :, 0] - lam * pr[:, :, 1]
        return jnp.einsum('bhqk,bhkv->bhqv', a.astype(v.dtype), v)

    o = lax.map(block, jnp.arange(S // Q_BLOCK))
    o = o.transpose(1, 0, 3, 2, 4).reshape(B, S, DA_HEADS, DA_V_DIM)
    o = rms_norm(o, subln_g) * (1.0 - lam_init)
    return o.reshape(B, S, DA_WIDTH).astype(z_da.dtype)


def rwkv7_mix(z_rw, mu, w0, w_up, a0, a_up, g_up, k_k, k_a, r_k, ln_g, ln_b):
    B, S, _ = z_rw.shape
    f32 = jnp.float32
    prev = jnp.pad(z_rw[:, :-1], ((0, 0), (1, 0), (0, 0)))
    zs = z_rw + (prev - z_rw) * mu
    o1 = 3 * RW_WIDTH
    r, k, v, xw, xa, xg = jnp.split(
        zs, [RW_WIDTH, 2 * RW_WIDTH, o1, o1 + DECAY_LORA, o1 + DECAY_LORA + AAA_LORA], axis=-1)
    w = -jax.nn.softplus(-(w0 + jnp.tanh(xw) @ w_up)) - 0.5
    a = jax.nn.sigmoid(a0 + xa @ a_up)
    g = jax.nn.sigmoid(xg) @ g_up
    kk = k * k_k
    k = k * (1.0 + (a - 1.0) * k_a)

    def heads(t):
        return t.reshape(B, S, RW_HEADS, RW_HEAD).astype(f32)

    kk = heads(kk)
    kk = kk / jnp.maximum(jnp.sqrt(jnp.sum(kk * kk, axis=-1, keepdims=True)), 1e-12)
    r, k, v, a = heads(r), heads(k), heads(v), heads(a)
    decay = jnp.exp(-jnp.exp(heads(w)))

    def step(state, inp):
        r_t, d_t, k_t, v_t, kk_t, a_t = inp
        sa = jnp.einsum('bhij,bhj->bhi', state, -kk_t)
        state = (state * d_t[:, :, None, :] + sa[..., None] * (kk_t * a_t)[:, :, None, :]
                 + v_t[..., None] * k_t[:, :, None, :])
        return state, jnp.einsum('bhij,bhj->bhi', state, r_t)

    def tm(t):
        return jnp.moveaxis(t, 1, 0)

    s0 = jnp.zeros((B, RW_HEADS, RW_HEAD, RW_HEAD), f32)
    _, y = lax.scan(step, s0, (tm(r), tm(decay), tm(k), tm(v), tm(kk), tm(a)))
    y = jnp.moveaxis(y, 0, 1)
    mean = jnp.mean(y, axis=-1, keepdims=True)
    var = jnp.mean(jnp.square(y - mean), axis=-1, keepdims=True)
    y = ((y - mean) * lax.rsqrt(var + RW_GN_EPS) * ln_g.astype(f32).reshape(RW_HEADS, RW_HEAD)
         + ln_b.astype(f32).reshape(RW_HEADS, RW_HEAD))
    y = y + jnp.sum(r * k * r_k.astype(f32), axis=-1, keepdims=True) * v
    y = y.reshape(B, S, RW_WIDTH) * g.astype(f32)
    return y.astype(z_rw.dtype)


def peer_ffn(u, w_q, sub_keys, exp_u, exp_v):
    B, S, D = u.shape
    f32 = jnp.float32
    tokens = u.reshape(-1, TOKEN_BLOCK, D)

    def block(xc):
        q = (xc @ w_q).reshape(TOKEN_BLOCK, PEER_HEADS, 2, PEER_HALF)
        s = jnp.einsum('chpd,hpnd->chpn', q, sub_keys).astype(f32)
        s_top, i_top = lax.top_k(s, PEER_TOPK)
        cand = (s_top[:, :, 0, :, None] + s_top[:, :, 1, None, :]).reshape(
            TOKEN_BLOCK, PEER_HEADS, PEER_TOPK * PEER_TOPK)
        cand_idx = (i_top[:, :, 0, :, None] * N_KEYS + i_top[:, :, 1, None, :]).reshape(
            TOKEN_BLOCK, PEER_HEADS, PEER_TOPK * PEER_TOPK)
        best, pos = lax.top_k(cand, PEER_TOPK)
        idx = jnp.take_along_axis(cand_idx, pos, axis=-1)
        gate = jax.nn.softmax(best, axis=-1)
        uu = jnp.take(exp_u, idx, axis=0)
        vv = jnp.take(exp_v, idx, axis=0)
        hid = jax.nn.gelu(jnp.einsum('chkd,cd->chk', uu, xc).astype(f32), approximate=False)
        return jnp.einsum('chk,chkd->cd', (gate * hid).astype(vv.dtype), vv)

    out = lax.map(block, tokens)
    return out.reshape(B, S, D).astype(u.dtype)


def setup_inputs(seed: int = 0) -> dict:
    key = jax.random.key(seed)
    ks = iter(jax.random.split(key, 40))
    f32 = jnp.float32
    L, D = DEPTH, D_MODEL

    def nrm(shape, scale):
        return jax.random.normal(next(ks), shape, f32) * scale

    def gain(shape):
        return 1.0 + nrm(shape, 0.02)

    return {
        "x": nrm((BATCH, SEQ, D), 1.0),
        "p": nrm((DEPTH, BATCH, SEQ, PLE_DIM), 1.0),
        "positions": jnp.broadcast_to(jnp.arange(SEQ, dtype=jnp.int32), (BATCH, SEQ)),
        "norm_mix_g": gain((L, D)),
        "w_in": nrm((L, D, IN_WIDTH), D ** -0.5),
        "lam_q1": nrm((L, DA_HEAD_DIM), 0.1),
        "lam_k1": nrm((L, DA_HEAD_DIM), 0.1),
        "lam_q2": nrm((L, DA_HEAD_DIM), 0.1),
        "lam_k2": nrm((L, DA_HEAD_DIM), 0.1),
        "da_subln_g": gain((L, DA_V_DIM)),
        "rw_mu": jax.random.uniform(next(ks), (L, RW_IN_WIDTH), f32),
        "rw_w0": jax.random.uniform(next(ks), (L, RW_WIDTH), f32, -4.0, 1.0),
        "rw_w_up": nrm((L, DECAY_LORA, RW_WIDTH), 0.1 * DECAY_LORA ** -0.5),
        "rw_a0": nrm((L, RW_WIDTH), 0.1),
        "rw_a_up": nrm((L, AAA_LORA, RW_WIDTH), 0.5 * AAA_LORA ** -0.5),
        "rw_g_up": nrm((L, GATE_LORA, RW_WIDTH), GATE_LORA ** -0.5),
        "rw_k_k": 0.85 + nrm((L, RW_WIDTH), 0.05),
        "rw_k_a": 1.0 + nrm((L, RW_WIDTH), 0.05),
        "rw_r_k": nrm((L, RW_HEADS, RW_HEAD), 0.1),
        "rw_ln_g": gain((L, RW_WIDTH)),
        "rw_ln_b": nrm((L, RW_WIDTH), 0.02),
        "w_out": nrm((L, D, D), D ** -0.5),
        "norm_ffn_g": gain((L, D)),
        "peer_w_q": nrm((L, D, PEER_HEADS * PEER_QDIM), D ** -0.5),
        "peer_sub_keys": nrm((L, PEER_HEADS, 2, N_KEYS, PEER_HALF), PEER_HALF ** -0.5),
        "peer_u": nrm((L, N_EXPERTS, D), D ** -0.5),
        "peer_v": nrm((L, N_EXPERTS, D), 0.5 * PEER_HEADS ** -0.5),
        "norm_ple_g": gain((L, D)),
        "ple_gate_w": nrm((L, D, D), D ** -0.5),
        "ple_proj_w": nrm((L, PLE_DIM, D), PLE_DIM ** -0.5),
        "norm_final_g": gain((D,)),
    }


def reference(x, p, positions, norm_mix_g, w_in, lam_q1, lam_k1, lam_q2, lam_k2, da_subln_g,
              rw_mu, rw_w0, rw_w_up, rw_a0, rw_a_up, rw_g_up, rw_k_k, rw_k_a, rw_r_k,
              rw_ln_g, rw_ln_b, w_out, norm_ffn_g, peer_w_q, peer_sub_keys, peer_u, peer_v,
              norm_ple_g, ple_gate_w, ple_proj_w, norm_final_g):
    h = x
    for i in range(DEPTH):
        u = rms_norm(h, norm_mix_g[i])
        z = u @ w_in[i]
        lam_init = 0.8 - 0.6 * math.exp(-0.3 * i)
        o_da = diff_attention(z[..., :DA_IN_WIDTH], positions, lam_q1[i], lam_k1[i],
                              lam_q2[i], lam_k2[i], da_subln_g[i], lam_init)
        o_rw = rwkv7_mix(z[..., DA_IN_WIDTH:], rw_mu[i], rw_w0[i], rw_w_up[i], rw_a0[i],
                         rw_a_up[i], rw_g_up[i], rw_k_k[i], rw_k_a[i], rw_r_k[i],
                         rw_ln_g[i], rw_ln_b[i])
        h = h + jnp.concatenate([o_da, o_rw], axis=-1) @ w_out[i]
        h = h + peer_ffn(rms_norm(h, norm_ffn_g[i]), peer_w_q[i], peer_sub_keys[i],
                         peer_u[i], peer_v[i])
        gate = jax.nn.sigmoid(rms_norm(h, norm_ple_g[i]) @ ple_gate_w[i])
        h = h + gate * (p[i] @ ple_proj_w[i])
    return rms_norm(h, norm_final_g)
```

```python
import contextlib
import numpy as np
import concourse.bass as bass
import concourse.mybir as mybir
from concourse.bass_utils import run_bass_kernel_spmd

F32 = mybir.dt.float32
BF16 = mybir.dt.bfloat16
I32 = mybir.dt.int32
U32 = mybir.dt.uint32
AF = mybir.ActivationFunctionType
ALU = mybir.AluOpType
AX = mybir.AxisListType

D = 1024
SEQ = 8192
NOWN = 4096
EPS = 1e-6
ENGS = ("pe", "act", "dve", "pool", "sp")
NDSEM = 8


class Tok:
    __slots__ = ("w", "r", "psum")

    def __init__(self):
        self.w = None
        self.r = []
        self.psum = False


class Prog:
    def __init__(self, nc):
        self.nc = nc
        self.ops = {e: [] for e in ENGS}
        self.ndma = {e: 0 for e in ENGS}
        self.barrier_deps = set()

    def barrier(self):
        deps = set()
        for e in ENGS:
            n = len(self.ops[e])
            for i in range(n - 1, -1, -1):
                if self.ops[e][i]["me"][0] == "eng":
                    deps.add(("eng", e, i))
                    break
            for j in range(max(0, self.ndma[e] - NDSEM), self.ndma[e]):
                deps.add(("dma", e, j))
        self.barrier_deps = deps

    def _add(self, eng, fn, reads, writes, dma):
        deps = set()
        for t in reads:
            if t.w is not None:
                deps.add(t.w)
            if t.psum:
                for d in t.r:
                    if d[1] != eng:
                        deps.add(d)
        for t in writes:
            if t.w is not None:
                deps.add(t.w)
            for d in t.r:
                deps.add(d)
        deps |= self.barrier_deps
        idx = len(self.ops[eng])
        if dma:
            j = self.ndma[eng]
            self.ndma[eng] += 1
            me = ("dma", eng, j)
            if j >= NDSEM:
                deps.add(("dma", eng, j - NDSEM))
        else:
            me = ("eng", eng, idx)
        if eng == "pe":
            deps = {d for d in deps if not (d[0] == "eng" and d[1] == "pe")}
        deps.discard(me)
        self.ops[eng].append(dict(fn=fn, deps=deps, me=me))
        for t in reads:
            t.r.append(me)
        for t in writes:
            t.w = me
            t.r = []
        return me

    def op(self, eng, fn, reads=(), writes=()):
        return self._add(eng, fn, list(reads), list(writes), False)

    def dma(self, eng, fn, reads=(), writes=()):
        return self._add(eng, fn, list(reads), list(writes), True)

    def finish(self, toks):
        deps = set()
        for t in toks:
            if t.w is not None:
                deps.add(t.w)
        self.ops["sp"].append(dict(fn=lambda e: e.nop(), deps=deps,
                                   me=("eng", "sp", len(self.ops["sp"]))))

    def emit(self):
        nc = self.nc
        needed = {e: set() for e in ENGS}
        for e in ENGS:
            for o in self.ops[e]:
                for d in o["deps"]:
                    if d[0] == "eng":
                        needed[d[1]].add(d[2])
        sigcount = {}
        for e in ENGS:
            c = 0
            for i, o in enumerate(self.ops[e]):
                if i in needed[e] and o["me"][0] == "eng":
                    c += 1
                    sigcount[(e, i)] = c
        with contextlib.ExitStack() as st:
            esem = {e: st.enter_context(nc.semaphore(f"s_{e}")) for e in ENGS}
            dsem = {e: [st.enter_context(nc.semaphore(f"d_{e}{k}")) for k in range(NDSEM)]
                    for e in ("sp", "act", "pool")}
            block = st.enter_context(nc.Block())

            def run(eng_name, eng):
                waited = {}
                for i, o in enumerate(self.ops[eng_name]):
                    for d in sorted(o["deps"]):
                        if d[0] == "eng":
                            sem = esem[d[1]]
                            val = sigcount[(d[1], d[2])]
                            key = ("e", d[1])
                        else:
                            sem = dsem[d[1]][d[2] % NDSEM]
                            val = 16 * (d[2] // NDSEM + 1)
                            key = ("d", d[1], d[2] % NDSEM)
                        if waited.get(key, 0) >= val:
                            continue
                        eng.wait_ge(sem, val)
                        waited[key] = val
                    ins = o["fn"](eng)
                    me = o["me"]
                    if me[0] == "dma":
                        ins.then_inc(dsem[me[1]][me[2] % NDSEM], 16)
                    elif (eng_name, i) in sigcount:
                        ins.then_inc(esem[eng_name], 1)

            @block.tensor
            def _(e):
                run("pe", e)

            @block.scalar
            def _(e):
                run("act", e)

            @block.vector
            def _(e):
                run("dve", e)

            @block.gpsimd
            def _(e):
                run("pool", e)

            @block.sync
            def _(e):
                run("sp", e)


class TT:
    _used = {}

    def __init__(self, st, nc, name, shape, dtype, psum=False):
        k_ = (id(nc), name)
        n_ = TT._used.get(k_, 0)
        TT._used[k_] = n_ + 1
        if n_:
            name = f"{name}_v{n_}"
        if psum:
            self.t = st.enter_context(nc.psum_tensor("P_" + name, shape, dtype))
        else:
            self.t = st.enter_context(nc.sbuf_tensor("S_" + name, shape, dtype))
        self.k = Tok()
        self.k.psum = psum

    def __getitem__(self, idx):
        return self.t[idx]


INVF = [float(500000.0 ** (-(i * 2.0) / 16.0)) for i in range(8)]
TWO_PI = 6.283185307179586
C1 = 6.28125
C2 = TWO_PI - C1
LAM_INIT = 0.2
DEBUG = False


class _Stop(Exception):
    pass

DBG_DONE = []
DBG_TOK = Tok()


def build(do_da=True, do_rw=True, do_peer=True, mini=0, rw_tiles=0, rw_stop=0, c_tiles=0, peer_slots=128, cvt_blocks=128):
    nc = bass.Bass("TRN2", target_bir_lowering=False)
    P = Prog(nc)

    def din(name, shape, dt=F32):
        return nc.dram_tensor(name, list(shape), dt, kind="ExternalInput").ap()

    x_full = din("x_full", [SEQ, D])
    pos_full = din("pos_full", [1, SEQ], I32)
    x_own = din("x_own", [NOWN, D])
    pos_own = din("pos_own", [1, NOWN], I32)
    p_own = din("p_own", [NOWN, 256])
    ident_d = din("ident", [128, 128])
    cvec_d = din("cvec", [128, 4])
    mask_d = din("maskT", [128, 8 * 512])
    norm_mix_g = din("norm_mix_g", [1, D])
    w_in = din("w_in", [D, 3328])
    lamv = din("lamv", [1, 256])
    da_subln_g = din("da_subln_g", [128, 1])
    w_out = din("w_out", [D, D])
    norm_ple_g = din("norm_ple_g", [1, D])
    norm_final_g = din("norm_final_g", [1, D])
    ple_gate_w = din("ple_gate_w", [D, D])
    ple_proj_w = din("ple_proj_w", [256, D])
    norm_ffn_g = din("norm_ffn_g", [1, D])
    peer_w_q = din("peer_w_q", [D, 2048])
    peer_keys = din("peer_sub_keys", [16, 128, 128])
    peer_u = din("peer_u", [16384, D])
    peer_v = din("peer_v", [16384, D])
    iota_d = din("iota16", [128, 16])
    rwc_d = din("rwc", [128, 1280])
    sel_d = din("sel", [128, 1])
    rw_mu = din("rw_mu", [1, 1792])
    rw_w0 = din("rw_w0", [1, 512])
    rw_w_up = din("rw_w_up", [64, 512])
    rw_a0 = din("rw_a0", [1, 512])
    rw_a_up = din("rw_a_up", [64, 512])
    rw_g_up = din("rw_g_up", [128, 512])
    rw_k_k = din("rw_k_k", [1, 512])
    rw_k_a = din("rw_k_a", [1, 512])
    rw_r_k = din("rw_r_k", [1, 512])
    rw_ln_g = din("rw_ln_g", [1, 512])
    rw_ln_b = din("rw_ln_b", [1, 512])
    out_d = nc.dram_tensor("out", [NOWN, D], F32, kind="ExternalOutput").ap()
    out_tok = Tok()

    with contextlib.ExitStack() as st0:
        def sb0(name, shape, dt=F32):
            return TT(st0, nc, name, shape, dt)

        ident = sb0("ident", [128, 128])
        identb = sb0("identb", [128, 128], BF16)
        onesb = sb0("onesb", [128, 128], BF16)
        cvec = sb0("cvec", [128, 4])
        mhalf = sb0("mhalf", [128, 1])
        P.dma("sp", lambda e: e.dma_start(out=ident[:], in_=ident_d), [], [ident.k])
        P.dma("sp", lambda e: e.dma_start(out=cvec[:], in_=cvec_d), [], [cvec.k])
        P.op("dve", lambda e: e.tensor_copy(identb[:], ident[:]), [ident.k], [identb.k])
        P.op("pool", lambda e: e.memset(onesb[:], 1.0), [], [onesb.k])
        P.op("pool", lambda e: e.memset(mhalf[:], -0.5), [], [mhalf.k])

        stage = sb0("stage", [128, 1024])

        phase_reads = []

        def load_bf16(dst, dst_view, src_view, eng="dve"):
            a, b_ = src_view.shape[1], src_view.shape[2]
            sv = stage[:, 0:a * b_].rearrange("p (a b) -> p a b", a=a)
            P.dma("sp", lambda e: e.dma_start(out=sv, in_=src_view), [], [stage.k])
            P.op(eng, lambda e: e.tensor_copy(dst_view, sv), [stage.k] + phase_reads, [dst.k])

        def load_w(dst, src2d, ncols):
            kch = src2d.shape[0] // 128
            step = 1024 // kch
            for c0 in range(0, ncols, step):
                load_bf16(dst, dst[:, :, c0:c0 + step],
                          src2d[:, c0:c0 + step].rearrange("(k p) n -> p k n", p=128))

        ps = [TT(st0, nc, f"ps{i}", [128, 512], F32, psum=True) for i in range(7)]
        pst = TT(st0, nc, "pst", [128, 1024], BF16, psum=True)

        o_rw = sb0("o_rw", [128, NOWN // 128, 512], BF16)

        def rmsnorm(src, gtile, dst, jk, ss, col, d=D):
            P.op("dve", lambda e: e.scalar_tensor_tensor(jk[:], src[:], 1.0, src[:], ALU.mult, ALU.mult,
                                                         accum_out=ss[:, col:col + 1]), [src.k], [jk.k, ss.k])
            P.op("dve", lambda e: e.tensor_scalar(ss[:, col:col + 1], ss[:, col:col + 1], 1.0 / d, EPS,
                                                  ALU.mult, ALU.add), [ss.k], [ss.k])
            P.op("act", lambda e: e.activation(out=ss[:, col:col + 1], in_=ss[:, col:col + 1], func=AF.Sqrt),
                 [ss.k], [ss.k])
            P.op("dve", lambda e: e.reciprocal(ss[:, col:col + 1], ss[:, col:col + 1]), [ss.k], [ss.k])
            P.op("dve", lambda e: e.scalar_tensor_tensor(dst[:], src[:], ss[:, col:col + 1], gtile[:],
                                                         ALU.mult, ALU.mult),
                 [src.k, ss.k, gtile.k], [dst.k])

        def transpose_to(src, nchunks, dst_ap_fn, dst_tok, extra_reads=()):
            for c in range(nchunks):
                P.op("pe", lambda e, c=c: e.transpose(pst[:, c * 128:(c + 1) * 128],
                                                      src[:, c * 128:(c + 1) * 128], identb[:]),
                     [src.k, identb.k], [pst.k])
            P.op("act", lambda e: e.copy(dst_ap_fn(), pst[:, 0:nchunks * 128].rearrange("p (k t) -> p k t", k=nchunks)),
                 [pst.k], [dst_tok])

        uvb = nc.dram_tensor("uvb_scratch", [16384, 2 * D], BF16, kind="Internal").ap()
        uvb_tok = Tok()

        def _phase_cvt():
            with contextlib.ExitStack() as st:
                def sb(name, shape, dt=F32):
                    return TT(st, nc, name, shape, dt)
                NCB = 4
                cin = [sb(f"cin{i}", [128, 2, D]) for i in range(NCB)]
                cou = [sb(f"cou{i}", [128, 2, D], BF16) for i in range(NCB)]
                for blk in range(cvt_blocks):
                    i_ = blk % NCB
                    rows = slice(blk * 128, (blk + 1) * 128)
                    P.dma("sp", lambda e, i_=i_, rows=rows: e.dma_start(out=cin[i_][:, 0, :], in_=peer_u[rows, :]), [], [cin[i_].k])
                    P.dma("sp", lambda e, i_=i_, rows=rows: e.dma_start(out=cin[i_][:, 1, :], in_=peer_v[rows, :]), [], [cin[i_].k])
                    if blk % 2 == 0:
                        P.op("act", lambda e, i_=i_: e.copy(cou[i_][:], cin[i_][:]), [cin[i_].k], [cou[i_].k])
                    else:
                        P.op("dve", lambda e, i_=i_: e.tensor_copy(cou[i_][:], cin[i_][:]), [cin[i_].k], [cou[i_].k])
                    P.dma("sp", lambda e, i_=i_, rows=rows: e.dma_start(
                        out=uvb[rows, :].rearrange("r (a d) -> r a d", a=2), in_=cou[i_][:]), [cou[i_].k], [uvb_tok])
            P.barrier()
        if do_peer:
            _phase_cvt()

        def _phase_rw():
            with contextlib.ExitStack() as st:
                def sb(name, shape, dt=F32):
                    return TT(st, nc, name, shape, dt)

                NTILE = rw_tiles if rw_tiles else SEQ // 128
                gmix = sb("gmix", [128, D])
                P.dma("sp", lambda e: e.dma_start(out=gmix[:], in_=norm_mix_g.partition_broadcast(128)), [], [gmix.k])
                rwc = sb("rwc", [128, 1280])
                P.dma("sp", lambda e: e.dma_start(out=rwc[:], in_=rwc_d), [], [rwc.k])
                maskUN = lambda: rwc[:, 0:512]
                maskSL2 = lambda: rwc[:, 512:768]
                triBD = lambda: rwc[:, 768:896]
                onesBD = lambda: rwc[:, 896:1024]
                I2 = lambda: rwc[:, 1024:1088]
                selt = sb("selt", [128, 1])
                P.dma("sp", lambda e: e.dma_start(out=selt[:], in_=sel_d), [], [selt.k])
                prm = sb("prm", [128, 7, 512])
                for n_, src in enumerate((rw_w0, rw_a0, rw_k_k, rw_k_a, rw_r_k, rw_ln_g, rw_ln_b)):
                    P.dma("sp", lambda e, n_=n_, src=src: e.dma_start(out=prm[:, n_, :], in_=src.partition_broadcast(128)),
                          [], [prm.k])
                lup = sb("lup", [128, 512])
                gup = sb("gup", [128, 512])
                P.dma("sp", lambda e: e.dma_start(out=lup[0:64, :], in_=rw_w_up), [], [lup.k])
                P.dma("sp", lambda e: e.dma_start(out=lup[64:128, :], in_=rw_a_up), [], [lup.k])
                P.dma("sp", lambda e: e.dma_start(out=gup[:], in_=rw_g_up), [], [gup.k])
                Wa = sb("Wa", [128, 8, 1792], BF16)
                Wb = sb("Wb", [128, 8, 1792], BF16)
                mut = sb("mut", [128, 128])
                for c0 in range(0, 1792, 128):
                    sv = stage[:, 0:1024].rearrange("p (a b) -> p a b", a=8)
                    P.dma("sp", lambda e, c0=c0: e.dma_start(
                        out=sv, in_=w_in[:, 1536 + c0:1536 + c0 + 128].rearrange("(k p) n -> p k n", p=128)),
                        [], [stage.k])
                    P.dma("sp", lambda e, c0=c0: e.dma_start(out=mut[:], in_=rw_mu[:, c0:c0 + 128].partition_broadcast(128)),
                          [], [mut.k])
                    mub = lambda: mut[:].unsqueeze(1).to_broadcast([128, 8, 128])
                    P.op("dve", lambda e, c0=c0: e.tensor_tensor(Wb[:, :, c0:c0 + 128], sv, mub(), ALU.mult),
                         [stage.k, mut.k], [Wb.k])
                    P.op("dve", lambda e: e.tensor_scalar(mut[:], mut[:], -1.0, 1.0, ALU.mult, ALU.add), [mut.k], [mut.k])
                    P.op("dve", lambda e, c0=c0: e.tensor_tensor(Wa[:, :, c0:c0 + 128], sv, mub(), ALU.mult),
                         [stage.k, mut.k], [Wa.k])

                xt = sb("xt", [128, D])
                jk = sb("jk", [128, D])
                ssx = sb("ssx", [128, 2])
                ub = sb("ub", [128, D], BF16)
                uTe = [sb(f"uTe{i}", [128, 8, 129], BF16) for i in range(2)]
                P.op("pool", lambda e: e.memset(uTe[0][:], 0.0), [], [uTe[0].k])
                P.op("pool", lambda e: e.memset(uTe[1][:], 0.0), [], [uTe[1].k])
                r_t = sb("r_t", [128, 512])
                k_t = sb("k_t", [128, 512])
                v_t = sb("v_t", [128, 512])
                lora = sb("lora", [128, 256])
                loraT = sb("loraT", [128, 2, 128])
                lw = sb("lw", [128, 512])
                a_t = sb("a_t", [128, 512])
                kk = sb("kk", [128, 512])
                km = sb("km", [128, 512])
                ba = sb("ba", [128, 512])
                cl = sb("cl", [128, 512])
                clC = sb("clC", [128, 512])
                ex = sb("ex", [128, 512])
                tmp = sb("tmp", [128, 512])
                st8 = sb("st8", [128, 16])
                Ab = sb("Ab", [128, 512])
                Rb = sb("Rb", [128, 512])
                Bt = sb("Bt", [128, 512])
                Kt = sb("Kt", [128, 512])
                Bh = sb("Bh", [128, 512])
                Kh = sb("Kh", [128, 512])
                PC = sb("PC", [128, 512])
                g_t = sb("g_t", [128, 512])
                Y = sb("Y", [128, 512])
                T4 = sb("T4", [128, 4, 128])
                PCT = sb("PCT", [128, 2])
                UN = sb("UN", [128, 2, 256])
                MN = sb("MN", [128, 2, 256])
                Mq = [sb(f"Mq{i}", [128, 2, 128]) for i in range(2)]
                Uq = [sb(f"Uq{i}", [128, 2, 128]) for i in range(2)]
                TTt = [sb(f"TTt{i}", [128, 2, 128]) for i in range(2)]
                X0 = sb("X0", [128, 128])
                W0 = sb("W0", [128, 128])
                ApPad = sb("ApPad", [128, 2, 128])
                P.op("pool", lambda e: e.memset(ApPad[:], 0.0), [], [ApPad.k])
                W0pad = sb("W0pad", [128, 2, 128])
                Vpad = sb("Vpad", [128, 2, 128])
                P.op("pool", lambda e: e.memset(W0pad[:], 0.0), [], [W0pad.k])
                P.op("pool", lambda e: e.memset(Vpad[:], 0.0), [], [Vpad.k])
                R0 = sb("R0", [128, 128])
                R1 = sb("R1", [128, 128])
                P.op("pool", lambda e: e.memset(R0[:], 0.0), [], [R0.k])
                P.op("pool", lambda e: e.memset(R1[:], 0.0), [], [R1.k])
                GTbd = [sb(f"GTbd{i}", [128, 128]) for i in range(2)]
                for i in range(2):
                    P.op("pool", lambda e, i=i: e.memset(GTbd[i][:], 0.0), [], [GTbd[i].k])
                Sbd = [[sb(f"Sbd{p_}_{i}", [128, 128]) for i in range(2)] for p_ in range(4)]
                for p_ in range(4):
                    for i in range(2):
                        P.op("pool", lambda e, p_=p_, i=i: e.memset(Sbd[p_][i][:], 0.0), [], [Sbd[p_][i].k])
                scur = [0, 0, 0, 0]
                o_t = sb("o_t", [128, 512])
                o_b = sb("o_b", [128, 512], BF16)

                def ev(eng, out_fn, in_fn, reads, writes):
                    if eng == "act":
                        P.op("act", lambda e: e.copy(out_fn(), in_fn()), reads, writes)
                    else:
                        P.op("dve", lambda e: e.tensor_copy(out_fn(), in_fn()), reads, writes)

                for it in range(NTILE):
                    ue = uTe[it % 2]
                    un = uTe[(it + 1) % 2]
                    rows = slice(it * 128, (it + 1) * 128)
                    P.dma("sp", lambda e, rows=rows: e.dma_start(out=xt[:], in_=x_full[rows, :]), [], [xt.k])
                    rmsnorm(xt, gmix, ub, jk, ssx, 0)
                    transpose_to(ub, 8, lambda ue=ue: ue[:, :, 1:129], ue.k)
                    P.op("dve", lambda e, ue=ue, un=un: e.tensor_copy(un[:, :, 0:1], ue[:, :, 128:129]), [ue.k], [un.k])
                    groups = [(0, 512, r_t, 0), (512, 512, k_t, 0), (1024, 512, v_t, 0), (1536, 256, lora, 0)]
                    for gi, (c0, w_, dst, _) in enumerate(groups):
                        pz = ps[gi % 2]
                        for kc in range(8):
                            P.op("pe", lambda e, kc=kc, c0=c0, w_=w_, pz=pz, ue=ue: e.matmul(
                                pz[:, 0:w_], ue[:, kc, 1:129], Wa[:, kc, c0:c0 + w_], start=(kc == 0), stop=False),
                                [ue.k, Wa.k], [pz.k])
                        for kc in range(8):
                            P.op("pe", lambda e, kc=kc, c0=c0, w_=w_, pz=pz, ue=ue: e.matmul(
                                pz[:, 0:w_], ue[:, kc, 0:128], Wb[:, kc, c0:c0 + w_], start=False, stop=(kc == 7)),
                                [ue.k, Wb.k], [pz.k])
                        P.op("act", lambda e, dst=dst, w_=w_, pz=pz: e.copy(dst[:, 0:w_], pz[:, 0:w_]), [pz.k], [dst.k])
                    if rw_stop == 1:
                        return
                    for c in range(2):
                        P.op("pe", lambda e, c=c: e.transpose(ps[2][:, c * 128:(c + 1) * 128],
                                                              lora[:, c * 128:(c + 1) * 128], ident[:]),
                             [lora.k, ident.k], [ps[2].k])
                    P.op("act", lambda e: e.activation(out=loraT[0:64, 0, :], in_=ps[2][0:64, 0:128], func=AF.Tanh),
                         [ps[2].k], [loraT.k])
                    P.op("act", lambda e: e.copy(loraT[64:128, 0, :], ps[2][64:128, 0:128]), [ps[2].k], [loraT.k])
                    P.op("act", lambda e: e.activation(out=loraT[:, 1, :], in_=ps[2][:, 128:256], func=AF.Sigmoid),
                         [ps[2].k], [loraT.k])
                    P.op("pe", lambda e: e.matmul(ps[3][:], loraT[0:64, 0, :], lup[0:64, :], start=True, stop=True),
                         [loraT.k, lup.k], [ps[3].k])
                    P.op("pe", lambda e: e.matmul(ps[4][:], loraT[64:128, 0, :], lup[64:128, :], start=True, stop=True),
                         [loraT.k, lup.k], [ps[4].k])
                    P.op("pe", lambda e: e.matmul(ps[5][:], loraT[:, 1, :], gup[:], start=True, stop=True),
                         [loraT.k, gup.k], [ps[5].k])
                    if rw_stop == 2:
                        return
                    P.op("dve", lambda e: e.tensor_tensor(tmp[:], ps[3][:], prm[:, 0, :], ALU.add), [ps[3].k, prm.k], [tmp.k])
                    P.op("act", lambda e: e.activation(out=lw[:], in_=tmp[:], func=AF.Sigmoid), [tmp.k], [lw.k])
                    P.op("dve", lambda e: e.tensor_scalar(lw[:], lw[:], -0.6065306597126334, None, ALU.mult), [lw.k], [lw.k])
                    P.op("dve", lambda e: e.tensor_tensor(tmp[:], ps[4][:], prm[:, 1, :], ALU.add), [ps[4].k, prm.k], [tmp.k])
                    P.op("act", lambda e: e.activation(out=a_t[:], in_=tmp[:], func=AF.Sigmoid), [tmp.k], [a_t.k])
                    P.op("act", lambda e: e.copy(g_t[:], ps[5][:]), [ps[5].k], [g_t.k])
                    P.op("dve", lambda e: e.tensor_tensor(kk[:], k_t[:], prm[:, 2, :], ALU.mult), [k_t.k, prm.k], [kk.k])
                    P.op("dve", lambda e: e.tensor_tensor(tmp[:], kk[:], kk[:], ALU.mult), [kk.k], [tmp.k])
                    P.op("dve", lambda e: e.tensor_reduce(st8[:, 0:8], tmp[:].rearrange("p (h j) -> p h j", h=8), AX.X, ALU.add),
                         [tmp.k], [st8.k])
                    P.op("act", lambda e: e.activation(out=st8[:, 0:8], in_=st8[:, 0:8], func=AF.Sqrt), [st8.k], [st8.k])
                    P.op("dve", lambda e: e.tensor_scalar(st8[:, 0:8], st8[:, 0:8], 1e-12, None, ALU.max), [st8.k], [st8.k])
                    P.op("dve", lambda e: e.reciprocal(st8[:, 0:8], st8[:, 0:8]), [st8.k], [st8.k])
                    P.op("dve", lambda e: e.tensor_tensor(kk[:].rearrange("p (h j) -> p h j", h=8),
                                                          kk[:].rearrange("p (h j) -> p h j", h=8),
                                                          st8[:, 0:8].unsqueeze(2).to_broadcast([128, 8, 64]), ALU.mult),
                         [kk.k, st8.k], [kk.k])
                    P.op("dve", lambda e: e.scalar_tensor_tensor(tmp[:], a_t[:], -1.0, prm[:, 3, :], ALU.add, ALU.mult),
                         [a_t.k, prm.k], [tmp.k])
                    P.op("dve", lambda e: e.scalar_tensor_tensor(km[:], tmp[:], 1.0, k_t[:], ALU.add, ALU.mult),
                         [tmp.k, k_t.k], [km.k])
                    P.op("dve", lambda e: e.tensor_tensor(ba[:], kk[:], a_t[:], ALU.mult), [kk.k, a_t.k], [ba.k])
                    if rw_stop == 3:
                        return
                    P.op("pe", lambda e: e.matmul(ps[3][:], triBD(), lw[:], start=True, stop=True), [rwc.k, lw.k], [ps[3].k])
                    P.op("pe", lambda e: e.matmul(ps[4][:], onesBD(), lw[:], start=True, stop=True), [rwc.k, lw.k], [ps[4].k])
                    P.op("act", lambda e: e.copy(cl[:], ps[3][:]), [ps[3].k], [cl.k])
                    P.op("act", lambda e: e.copy(clC[:], ps[4][:]), [ps[4].k], [clC.k])
                    P.op("dve", lambda e: e.tensor_tensor(tmp[:], cl[:], lw[:], ALU.subtract), [cl.k, lw.k], [tmp.k])
                    P.op("act", lambda e: e.activation(out=ex[:], in_=tmp[:], func=AF.Exp), [tmp.k], [ex.k])
                    P.op("dve", lambda e: e.scalar_tensor_tensor(Ab[:], kk[:], -1.0, ex[:], ALU.mult, ALU.mult),
                         [kk.k, ex.k], [Ab.k])
                    P.op("act", lambda e: e.activation(out=ex[:], in_=cl[:], func=AF.Exp), [cl.k], [ex.k])
                    P.op("dve", lambda e: e.tensor_tensor(Rb[:], r_t[:], ex[:], ALU.mult), [r_t.k, ex.k], [Rb.k])
                    P.op("act", lambda e: e.activation(out=ex[:], in_=cl[:], func=AF.Exp, scale=-1.0), [cl.k], [ex.k])
                    P.op("dve", lambda e: e.tensor_tensor(Bt[:], ba[:], ex[:], ALU.mult), [ba.k, ex.k], [Bt.k])
                    P.op("dve", lambda e: e.tensor_tensor(Kt[:], km[:], ex[:], ALU.mult), [km.k, ex.k], [Kt.k])
                    P.op("dve", lambda e: e.tensor_tensor(tmp[:], clC[:], cl[:], ALU.subtract), [clC.k, cl.k], [tmp.k])
                    P.op("act", lambda e: e.activation(out=ex[:], in_=tmp[:], func=AF.Exp), [tmp.k], [ex.k])
                    P.op("dve", lambda e: e.tensor_tensor(Bh[:], ba[:], ex[:], ALU.mult), [ba.k, ex.k], [Bh.k])
                    P.op("dve", lambda e: e.tensor_tensor(Kh[:], km[:], ex[:], ALU.mult), [km.k, ex.k], [Kh.k])
                    P.op("act", lambda e: e.activation(out=PC[:], in_=clC[:], func=AF.Exp), [clC.k], [PC.k])

                    if rw_stop == 4:
                        return
                    for hp2 in range(4):
                        pc = slice(hp2 * 128, (hp2 + 1) * 128)
                        hc = [slice(hp2 * 128 + h * 64, hp2 * 128 + (h + 1) * 64) for h in range(2)]
                        pb_ = [slice(0, 64), slice(64, 128)]
                        for n_, src in enumerate((Ab, Rb, Bt, Kt)):
                            P.op("pe", lambda e, n_=n_, src=src, pc=pc: e.transpose(ps[2][:, n_ * 128:(n_ + 1) * 128],
                                                                                    src[:, pc], ident[:]),
                                 [src.k, ident.k], [ps[2].k])
                        P.op("pe", lambda e, pc=pc: e.transpose(ps[3][:, 0:128], PC[:, pc], ident[:]),
                             [PC.k, ident.k], [ps[3].k])
                        P.op("act", lambda e: e.copy(T4[:].rearrange("p a t -> p (a t)"), ps[2][:]), [ps[2].k], [T4.k])
                        P.op("dve", lambda e: e.tensor_copy(PCT[:], ps[3][:, 0:128].rearrange("p (c t) -> p c t", c=2)[:, :, 0]),
                             [ps[3].k], [PCT.k])
                        if rw_stop == 5:
                            return
                        for h in range(2):
                            P.op("pe", lambda e, h=h: e.matmul(ps[4][:, h * 256:(h + 1) * 256], T4[pb_[h], 2, :],
                                                               T4[pb_[h], 0:2, :].rearrange("p a t -> p (a t)"), start=True, stop=True),
                                 [T4.k], [ps[4].k])
                            P.op("pe", lambda e, h=h: e.matmul(ps[5][:, h * 256:(h + 1) * 256], T4[pb_[h], 3, :],
                                                               T4[pb_[h], 0:2, :].rearrange("p a t -> p (a t)"), start=True, stop=True),
                                 [T4.k], [ps[5].k])
                            P.op("pe", lambda e, h=h: e.matmul(ps[6][:, h * 128:(h + 1) * 128], T4[pb_[h], 0, :],
                                                               T4[pb_[h], 2, :], start=True, stop=True),
                                 [T4.k], [ps[6].k])
                        P.op("dve", lambda e: e.tensor_tensor(UN[:].rearrange("p h c -> p (h c)"), ps[4][:], maskUN(), ALU.mult),
                             [ps[4].k, rwc.k], [UN.k])
                        P.op("dve", lambda e: e.tensor_tensor(MN[:].rearrange("p h c -> p (h c)"), ps[5][:], maskUN(), ALU.mult),
                             [ps[5].k, rwc.k], [MN.k])
                        P.op("dve", lambda e: e.tensor_tensor(Mq[0][:].rearrange("p h c -> p (h c)"), ps[6][:, 0:256], maskSL2(), ALU.mult),
                             [ps[6].k, rwc.k], [Mq[0].k])
                        if rw_stop == 6:
                            return
                        P.op("act", lambda e: e.copy(Uq[0][:], UN[:, :, 0:128]), [UN.k], [Uq[0].k])
                        for h in range(2):
                            P.op("dve", lambda e, h=h: e.tensor_tensor(TTt[0][:, h, :], UN[:, h, 0:128], ident[:], ALU.add),
                                 [UN.k, ident.k], [TTt[0].k])
                        cu = 0
                        for lev in range(5):
                            nu = 1 - cu
                            for h in range(2):
                                P.op("pe", lambda e, h=h, cu=cu: e.matmul(ps[4][:, h * 128:(h + 1) * 128], Uq[cu][:, h, :], Mq[cu][:, h, :],
                                                                          start=True, stop=True), [Uq[cu].k, Mq[cu].k], [ps[4].k])
                                if lev < 4:
                                    P.op("pe", lambda e, h=h, cu=cu: e.matmul(ps[5][:, h * 128:(h + 1) * 128], Mq[cu][:, h, :], Uq[cu][:, h, :],
                                                                              start=True, stop=True), [Uq[cu].k, Mq[cu].k], [ps[5].k])
                            P.op("act", lambda e, nu=nu: e.copy(Mq[nu][:].rearrange("p h c -> p (h c)"), ps[4][:, 0:256]),
                                 [ps[4].k], [Mq[nu].k])
                            if lev < 4:
                                P.op("dve", lambda e, nu=nu: e.tensor_copy(Uq[nu][:].rearrange("p h c -> p (h c)"), ps[5][:, 0:256]),
                                     [ps[5].k], [Uq[nu].k])
                            for h in range(2):
                                P.op("pe", lambda e, h=h, nu=nu, cu=cu: e.matmul(ps[6][:, h * 128:(h + 1) * 128], Mq[nu][:, h, :], TTt[cu][:, h, :],
                                                                                 start=True, stop=True), [Mq[nu].k, TTt[cu].k], [ps[6].k])
                            P.op("dve", lambda e, nu=nu, cu=cu: e.tensor_tensor(TTt[nu][:].rearrange("p h c -> p (h c)"),
                                                                                TTt[cu][:].rearrange("p h c -> p (h c)"),
                                                                                ps[6][:, 0:256], ALU.add),
                                 [TTt[cu].k, ps[6].k], [TTt[nu].k])
                            cu = nu
                        TTf = TTt[cu]
                        if cu != 0:
                            pass
                        if rw_stop == 7:
                            return
                        for h in range(2):
                            P.op("pe", lambda e, hc=hc, h=h: e.matmul(ps[4][:, h * 64:(h + 1) * 64], MN[:, h, 0:128], v_t[:, hc[h]],
                                                               start=True, stop=True), [MN.k, v_t.k], [ps[4].k])
                        if rw_stop == 70:
                            return
                        P.op("act", lambda e: e.copy(X0[:], ps[4][:, 0:128]), [ps[4].k], [X0.k])
                        for h in range(2):
                            P.op("pe", lambda e, hc=hc, h=h, TTf=TTf: e.matmul(ps[5][:, h * 64:(h + 1) * 64], TTf[:, h, :], X0[:, h * 64:(h + 1) * 64],
                                                                        start=True, stop=True), [TTf.k, X0.k], [ps[5].k])
                            P.op("pe", lambda e, hc=hc, h=h, TTf=TTf: e.matmul(ps[5][:, 128 + h * 64:128 + (h + 1) * 64], TTf[:, h, :], Ab[:, hc[h]],
                                                                        start=True, stop=True), [TTf.k, Ab.k], [ps[5].k])
                        if rw_stop == 72:
                            return
                        P.op("act", lambda e: e.copy(W0[:], ps[5][:, 0:128]), [ps[5].k], [W0.k])
                        if rw_stop == 721:
                            return
                        for h in range(2):
                            P.op("act", lambda e, hc=hc, h=h: e.copy(W0pad[:, h, h * 64:(h + 1) * 64], ps[5][:, h * 64:(h + 1) * 64]),
                                 [ps[5].k], [W0pad.k])
                            P.op("dve", lambda e, hc=hc, h=h: e.tensor_copy(Vpad[:, h, h * 64:(h + 1) * 64], v_t[:, hc[h]]),
                                 [v_t.k], [Vpad.k])
                        if rw_stop == 722:
                            return
                        for h in range(2):
                            P.op("dve", lambda e, h=h: e.tensor_copy(ApPad[:, h, h * 64:(h + 1) * 64], ps[5][:, 128 + h * 64:128 + (h + 1) * 64]),
                                 [ps[5].k], [ApPad.k])
                        if rw_stop == 73:
                            return
                        for h in range(2):
                            P.op("pe", lambda e, h=h: e.matmul(ps[6][:, 0:128], ApPad[:, h, :], UN[:, h, 128:256],
                                                               start=(h == 0), stop=(h == 1)), [ApPad.k, UN.k], [ps[6].k])
                        if rw_stop == 74:
                            return
                        P.op("dve", lambda e: e.tensor_tensor(R0[:, 0:64], ps[6][:, 0:64], T4[:, 1, 0:64], ALU.add),
                             [ps[6].k, T4.k], [R0.k])
                        P.op("dve", lambda e: e.tensor_tensor(R1[:, 64:128], ps[6][:, 64:128], T4[:, 1, 64:128], ALU.add),
                             [ps[6].k, T4.k], [R1.k])
                        if rw_stop == 8:
                            return
                        for c in range(2):
                            cp = slice(c * 64, (c + 1) * 64)
                            for h in range(2):
                                P.op("pe", lambda e, hc=hc, h=h, cp=cp, c=c: e.matmul(ps[4][:, c * 64:(c + 1) * 64], ApPad[cp, h, :], Bh[cp, hc[h]],
                                                                               start=(h == 0), stop=(h == 1)), [ApPad.k, Bh.k], [ps[4].k])
                            for h in range(2):
                                P.op("dve", lambda e, h=h, c=c: e.scalar_tensor_tensor(
                                    GTbd[c][pb_[h], h * 64:(h + 1) * 64], I2()[pb_[h], :], PCT[pb_[h], c:c + 1],
                                    ps[4][pb_[h], c * 64:(c + 1) * 64], ALU.mult, ALU.add),
                                    [rwc.k, PCT.k, ps[4].k], [GTbd[c].k])
                        Sc0 = Sbd[hp2][scur[hp2]]
                        Sc1 = Sbd[hp2][1 - scur[hp2]]
                        for (c, Sin, Sout) in ((0, Sc0, Sc1), (1, Sc1, Sc0)):
                            cp = slice(c * 64, (c + 1) * 64)
                            P.op("pe", lambda e, cp=cp, pc=pc: e.matmul(ps[5][:, 0:128], Bh[cp, pc], W0[cp, :], start=True, stop=False),
                                 [Bh.k, W0.k], [ps[5].k])
                            P.op("pe", lambda e, cp=cp, pc=pc: e.matmul(ps[5][:, 0:128], Kh[cp, pc], v_t[cp, pc], start=False, stop=False),
                                 [Kh.k, v_t.k], [ps[5].k])
                            P.op("pe", lambda e, c=c, Sin=Sin: e.matmul(ps[5][:, 0:128], GTbd[c][:], Sin[:], start=False, stop=True),
                                 [GTbd[c].k, Sin.k], [ps[5].k])
                            if c == 0:
                                for h in range(2):
                                    P.op("pe", lambda e, h=h: e.matmul(ps[6][:, 128:256], UN[:, h, 128:256], W0pad[:, h, :],
                                                                       start=(h == 0), stop=False), [UN.k, W0pad.k], [ps[6].k])
                                    P.op("pe", lambda e, h=h: e.matmul(ps[6][:, 128:256], MN[:, h, 128:256], Vpad[:, h, :],
                                                                       start=False, stop=False), [MN.k, Vpad.k], [ps[6].k])
                                P.op("pe", lambda e, Sin=Sin: e.matmul(ps[6][:, 128:256], R0[:], Sin[:], start=False, stop=False),
                                     [R0.k, Sin.k], [ps[6].k])
                            for h in range(2):
                                P.op("dve" if h == 0 else "act",
                                     (lambda e, h=h, Sout=Sout: e.tensor_copy(Sout[pb_[h], h * 64:(h + 1) * 64], ps[5][pb_[h], h * 64:(h + 1) * 64]))
                                     if h == 0 else
                                     (lambda e, h=h, Sout=Sout: e.copy(Sout[pb_[h], h * 64:(h + 1) * 64], ps[5][pb_[h], h * 64:(h + 1) * 64])),
                                     [ps[5].k], [Sout.k])
                            if c == 0:
                                P.op("pe", lambda e, Sout=Sout: e.matmul(ps[6][:, 128:256], R1[:], Sout[:], start=False, stop=True),
                                     [R1.k, Sout.k], [ps[6].k])
                                P.op("act", lambda e, pc=pc: e.copy(Y[:, pc], ps[6][:, 128:256]), [ps[6].k], [Y.k])

                    if rw_stop == 9:
                        return
                    Y3 = lambda: Y[:].rearrange("p (h j) -> p h j", h=8)
                    P.op("dve", lambda e: e.tensor_reduce(st8[:, 0:8], Y3(), AX.X, ALU.add), [Y.k], [st8.k])
                    P.op("dve", lambda e: e.tensor_scalar(st8[:, 0:8], st8[:, 0:8], 1.0 / 64, None, ALU.mult), [st8.k], [st8.k])
                    P.op("dve", lambda e: e.tensor_tensor(Y3(), Y3(), st8[:, 0:8].unsqueeze(2).to_broadcast([128, 8, 64]), ALU.subtract),
                         [Y.k, st8.k], [Y.k])
                    P.op("dve", lambda e: e.tensor_tensor(tmp[:], Y[:], Y[:], ALU.mult), [Y.k], [tmp.k])
                    P.op("dve", lambda e: e.tensor_reduce(st8[:, 8:16], tmp[:].rearrange("p (h j) -> p h j", h=8), AX.X, ALU.add),
                         [tmp.k], [st8.k])
                    P.op("dve", lambda e: e.tensor_scalar(st8[:, 8:16], st8[:, 8:16], 1.0 / 64, 64e-5, ALU.mult, ALU.add), [st8.k], [st8.k])
                    P.op("act", lambda e: e.activation(out=st8[:, 8:16], in_=st8[:, 8:16], func=AF.Sqrt), [st8.k], [st8.k])
                    P.op("dve", lambda e: e.reciprocal(st8[:, 8:16], st8[:, 8:16]), [st8.k], [st8.k])
                    P.op("dve", lambda e: e.tensor_tensor(Y3(), Y3(), st8[:, 8:16].unsqueeze(2).to_broadcast([128, 8, 64]), ALU.mult),
                         [Y.k, st8.k], [Y.k])
                    P.op("dve", lambda e: e.tensor_tensor(Y[:], Y[:], prm[:, 5, :], ALU.mult), [Y.k, prm.k], [Y.k])
                    P.op("dve", lambda e: e.tensor_tensor(Y[:], Y[:], prm[:, 6, :], ALU.add), [Y.k, prm.k], [Y.k])
                    P.op("dve", lambda e: e.tensor_tensor(tmp[:], r_t[:], km[:], ALU.mult), [r_t.k, km.k], [tmp.k])
                    P.op("dve", lambda e: e.tensor_tensor(tmp[:], tmp[:], prm[:, 4, :], ALU.mult), [tmp.k, prm.k], [tmp.k])
                    P.op("dve", lambda e: e.tensor_reduce(st8[:, 0:8], tmp[:].rearrange("p (h j) -> p h j", h=8), AX.X, ALU.add),
                         [tmp.k], [st8.k])
                    P.op("dve", lambda e: e.tensor_tensor(tmp[:].rearrange("p (h j) -> p h j", h=8),
                                                          v_t[:].rearrange("p (h j) -> p h j", h=8),
                                                          st8[:, 0:8].unsqueeze(2).to_broadcast([128, 8, 64]), ALU.mult),
                         [v_t.k, st8.k], [tmp.k])
                    P.op("dve", lambda e: e.tensor_tensor(Y[:], Y[:], tmp[:], ALU.add), [Y.k, tmp.k], [Y.k])
                    P.op("dve", lambda e: e.tensor_tensor(o_t[:], Y[:], g_t[:], ALU.mult), [Y.k, g_t.k], [o_t.k])
                    sc_i, tin = it // 8, it % 8
                    slot = sc_i * 4 + (tin % 4)
                    if tin < 4:
                        P.op("dve", lambda e, slot=slot: e.tensor_copy(o_rw[:, slot, :], o_t[:]), [o_t.k], [o_rw.k])
                    else:
                        P.op("dve", lambda e, slot=slot: e.tensor_tensor(o_t[:], o_t[:], o_rw[:, slot, :], ALU.subtract),
                             [o_t.k, o_rw.k], [o_t.k])
                        P.op("dve", lambda e, slot=slot: e.scalar_tensor_tensor(o_rw[:, slot, :], o_t[:], selt[:], o_rw[:, slot, :],
                                                                                ALU.mult, ALU.add),
                             [o_t.k, selt.k, o_rw.k], [o_rw.k])
            P.barrier()
        if do_rw:
            _phase_rw()
            P.barrier()
        if DEBUG == 2:
            dbg_d = nc.dram_tensor("dbg", [128, 4 * 512], F32, kind="ExternalOutput").ap()
            for s_ in range(min(4, rw_tiles or 4)):
                P.op("dve", lambda e, s_=s_: e.tensor_copy(stage[:, 0:512], o_rw[:, s_, :]), [o_rw.k, stage.k], [stage.k])
                P.dma("sp", lambda e, s_=s_: e.dma_start(out=dbg_d[:, s_ * 512:(s_ + 1) * 512], in_=stage[:, 0:512]), [stage.k], [DBG_TOK])
        oT_da = sb0("oT_da", [128, 4, NOWN], BF16) if not rw_stop else None
        def _phase_da():
            with contextlib.ExitStack() as st:
                def sb(name, shape, dt=F32):
                    return TT(st, nc, name, shape, dt)

                gmix = sb("gmix", [128, D])
                P.dma("sp", lambda e: e.dma_start(out=gmix[:], in_=norm_mix_g.partition_broadcast(128)), [], [gmix.k])
                maskT = sb("maskT", [128, 8, 512], BF16)
                for mh in range(4):
                    load_bf16(maskT, maskT[:, mh * 2:(mh + 1) * 2, :],
                              mask_d[:, mh * 1024:(mh + 1) * 1024].rearrange("p (j f) -> p j f", j=2))
                lv = sb("lv", [128, 256])
                lt = sb("lt", [128, 128])
                lsc = sb("lsc", [128, 4])
                P.dma("sp", lambda e: e.dma_start(out=lv[:], in_=lamv.partition_broadcast(128)), [], [lv.k])
                P.op("dve", lambda e: e.tensor_tensor(lt[:].rearrange("p (a b) -> p a b", a=2),
                                                      lv[:].rearrange("p (a c b) -> p a c b", a=2, c=2)[:, :, 0, :],
                                                      lv[:].rearrange("p (a c b) -> p a c b", a=2, c=2)[:, :, 1, :],
                                                      ALU.mult), [lv.k], [lt.k])
                P.op("dve", lambda e: e.tensor_reduce(lsc[:, 0:2], lt[:].rearrange("p (a b) -> p a b", a=2),
                                                      AX.X, ALU.add), [lt.k], [lsc.k])
                P.op("act", lambda e: e.activation(out=lsc[:, 0:2], in_=lsc[:, 0:2], func=AF.Exp), [lsc.k], [lsc.k])
                P.op("dve", lambda e: e.tensor_tensor(lsc[:, 2:3], lsc[:, 1:2], lsc[:, 0:1], ALU.subtract), [lsc.k], [lsc.k])
                P.op("dve", lambda e: e.tensor_scalar(lsc[:, 2:3], lsc[:, 2:3], -LAM_INIT, None, ALU.add), [lsc.k], [lsc.k])
                gsub = sb("gsub", [128, 1])
                P.dma("sp", lambda e: e.dma_start(out=gsub[:], in_=da_subln_g), [], [gsub.k])
                P.op("dve", lambda e: e.tensor_scalar(gsub[:], gsub[:], 1.0 - LAM_INIT, None, ALU.mult), [gsub.k], [gsub.k])

                kT = sb("kT", [128, 2, SEQ], BF16)
                Vt = sb("Vt", [128, SEQ // 128, 256], BF16)
                uT = [sb("uT0", [128, 8, 512], BF16)] * 2
                xt = [sb(f"xt{i}", [128, D]) for i in range(2)]
                jk = stage
                ssx = [sb("ssx0", [128, 2])] * 2
                ub = [sb("ub0", [128, D], BF16)] * 2
                wk = sb("wk", [128, 8, 256], BF16)
                wkp = sb("wkp", [128, 8, 256], BF16)
                wq, wqp = wk, wkp
                wv = sb("wv", [128, 8, 256], BF16)
                posi = sb("posi", [128, 512], I32)
                ang = sb("ang", [128, 512])
                kq = sb("kq", [128, 512])
                kqi = sb("kqi", [128, 512], I32)
                msk = sb("msk", [128, 512])
                tC = sb("tC", [128, 512])
                tS = sb("tS", [128, 512])
                t1 = sb("t1", [128, 512])
                t2 = sb("t2", [128, 512])
                qT = sb("qT", [128, 2, 512], BF16)
                pT = [sb(f"pT{i}", [128, 512], BF16) for i in range(3)]
                rr = [sb(f"rr{i}", [128, 512]) for i in range(2)]
                oo = sb("oo", [128, 512])
                sqb = sb("sqb", [128, 512], BF16)

                def sin_table(dst, shift, scale_col, mul):
                    P.op("dve", lambda e: e.tensor_scalar(kq[:], ang[:], shift, 1.0 / TWO_PI, ALU.add, ALU.mult),
                         [ang.k], [kq.k])
                    P.op("dve", lambda e: e.tensor_copy(kqi[:], kq[:]), [kq.k], [kqi.k])
                    P.op("dve", lambda e: e.tensor_copy(kq[:], kqi[:]), [kqi.k], [kq.k])
                    P.op("dve", lambda e: e.scalar_tensor_tensor(msk[:], kq[:], -C1, ang[:], ALU.mult, ALU.add),
                         [kq.k, ang.k], [msk.k])
                    P.op("dve", lambda e: e.scalar_tensor_tensor(msk[:], kq[:], -C2, msk[:], ALU.mult, ALU.add),
                         [kq.k, msk.k], [msk.k])
                    P.op("dve", lambda e: e.tensor_scalar(msk[:], msk[:], shift, None, ALU.add), [msk.k], [msk.k])
                    P.op("dve", lambda e: e.tensor_scalar(kq[:], msk[:], 3.141592653589793, -TWO_PI, ALU.is_gt, ALU.mult),
                         [msk.k], [kq.k])
                    P.op("dve", lambda e: e.tensor_tensor(msk[:], msk[:], kq[:], ALU.add), [msk.k, kq.k], [msk.k])
                    P.op("dve", lambda e: e.tensor_scalar(kq[:], msk[:], -3.141592653589793, TWO_PI, ALU.is_lt, ALU.mult),
                         [msk.k], [kq.k])
                    P.op("dve", lambda e: e.tensor_tensor(msk[:], msk[:], kq[:], ALU.add), [msk.k, kq.k], [msk.k])
                    P.op("act", lambda e: e.activation(out=dst[:], in_=msk[:], func=AF.Sin), [msk.k], [dst.k])
                    if scale_col is not None:
                        P.op("dve", lambda e: e.tensor_scalar(dst[:], dst[:], cvec[:, scale_col:scale_col + 1], mul,
                                                              ALU.mult, ALU.mult), [dst.k, cvec.k], [dst.k])
                    elif mul != 1.0:
                        P.op("dve", lambda e: e.tensor_scalar(dst[:], dst[:], mul, None, ALU.mult), [dst.k], [dst.k])

                def rope_tables(pos_ap, mul):
                    P.dma("sp", lambda e: e.dma_start(out=posi[:], in_=pos_ap.partition_broadcast(128)), [], [posi.k])
                    P.op("dve", lambda e: e.tensor_copy(ang[:], posi[:]), [posi.k], [ang.k])
                    P.op("dve", lambda e: e.tensor_scalar(ang[:], ang[:], cvec[:, 0:1], None, ALU.mult),
                         [ang.k, cvec.k], [ang.k])
                    sin_table(tC, 1.5707963267948966, None, mul)
                    sin_table(tS, 0.0, 1, mul)

                def load_uT(src_dram, row0, dst):
                    for tt in range(4):
                        b = tt % 2
                        rows = slice(row0 + tt * 128, row0 + (tt + 1) * 128)
                        P.dma("sp", lambda e, b=b, rows=rows: e.dma_start(out=xt[b][:], in_=src_dram[rows, :]),
                              [], [xt[b].k])
                        rmsnorm(xt[b], gmix, ub[b], jk, ssx[b], 0)
                        transpose_to(ub[b], 8, lambda tt=tt: dst[:, :, tt * 128:(tt + 1) * 128], dst.k)

                def proj_rope(w_a, w_b, h, src_uT, dst_ap_fn, dst_tok):
                    pa, pb_ = ps[0], ps[1]
                    for kc in range(8):
                        P.op("pe", lambda e, kc=kc: e.matmul(pa[:], w_a[:, kc, h * 128:(h + 1) * 128],
                                                             src_uT[:, kc, :], start=(kc == 0), stop=(kc == 7)),
                             [w_a.k, src_uT.k], [pa.k])
                    for kc in range(8):
                        P.op("pe", lambda e, kc=kc: e.matmul(pb_[:], w_b[:, kc, h * 128:(h + 1) * 128],
                                                             src_uT[:, kc, :], start=(kc == 0), stop=(kc == 7)),
                             [w_b.k, src_uT.k], [pb_.k])
                    P.op("dve", lambda e: e.tensor_tensor(t1[:], pa[:], tC[:], ALU.mult), [pa.k, tC.k], [t1.k])
                    P.op("dve", lambda e: e.tensor_tensor(t2[:], pb_[:], tS[:], ALU.mult), [pb_.k, tS.k], [t2.k])
                    P.op("dve", lambda e: e.tensor_tensor(dst_ap_fn(), t1[:], t2[:], ALU.add), [t1.k, t2.k], [dst_tok])

                for hp in range(1 if mini in (1, 2) else 2):
                    def wload(dst, col0):
                        load_w(dst, w_in[:, col0:col0 + 256], 256)
                    def wperm(wsrc, wdst):
                        P.op("pool", lambda e, wdst=wdst: e.memset(wdst[:], 0.0), [], [wdst.k])
                        P.op("dve", lambda e, wsrc=wsrc, wdst=wdst: e.tensor_copy(
                            wdst[:].rearrange("p k (b d) -> p k b d", d=64)[:, :, :, 0:8],
                            wsrc[:].rearrange("p k (b d) -> p k b d", d=64)[:, :, :, 8:16]), [wsrc.k, wdst.k], [wdst.k])
                        P.op("dve", lambda e, wsrc=wsrc, wdst=wdst: e.tensor_copy(
                            wdst[:].rearrange("p k (b d) -> p k b d", d=64)[:, :, :, 8:16],
                            wsrc[:].rearrange("p k (b d) -> p k b d", d=64)[:, :, :, 0:8]), [wsrc.k, wdst.k], [wdst.k])
                    wload(wk, 512 + hp * 256)
                    wload(wv, 1024 + hp * 256)
                    wperm(wk, wkp)
                    for (wsrc, wdst) in ():
                        P.op("pool", lambda e, wdst=wdst: e.memset(wdst[:], 0.0), [], [wdst.k])
                        P.op("dve", lambda e, wsrc=wsrc, wdst=wdst: e.tensor_copy(
                            wdst[:].rearrange("p k (b d) -> p k b d", d=64)[:, :, :, 0:8],
                            wsrc[:].rearrange("p k (b d) -> p k b d", d=64)[:, :, :, 8:16]), [wsrc.k, wdst.k], [wdst.k])
                        P.op("dve", lambda e, wsrc=wsrc, wdst=wdst: e.tensor_copy(
                            wdst[:].rearrange("p k (b d) -> p k b d", d=64)[:, :, :, 8:16],
                            wsrc[:].rearrange("p k (b d) -> p k b d", d=64)[:, :, :, 0:8]), [wsrc.k, wdst.k], [wdst.k])
                    for g in range(2 if mini else SEQ // 512):
                        u = uT[g % 2]
                        load_uT(x_full, g * 512, u)
                        if mini == 2:
                            P.op('pool', lambda e: e.memset(tC[:], 1.0), [], [tC.k])
                            P.op('pool', lambda e: e.memset(tS[:], 0.0), [], [tS.k])
                        else:
                            rope_tables(pos_full[:, g * 512:(g + 1) * 512], 1.0)
                        for h in range(2):
                            proj_rope(wk, wkp, h, u, lambda h=h, g=g: kT[:, h, g * 512:(g + 1) * 512], kT.k)
                        for tt in range(4):
                            pv = ps[2 + tt % 2]
                            for kc in range(8):
                                P.op("pe", lambda e, kc=kc, tt=tt, pv=pv, u=u: e.matmul(
                                    pv[:, 0:256], u[:, kc, tt * 128:(tt + 1) * 128], wv[:, kc, :],
                                    start=(kc == 0), stop=(kc == 7)), [u.k, wv.k], [pv.k])
                            P.op("act", lambda e, tt=tt, pv=pv, g=g: e.copy(Vt[:, g * 4 + tt, :], pv[:, 0:256]),
                                 [pv.k], [Vt.k])
                    if mini not in (1, 2):
                        wload(wq, hp * 256)
                        wperm(wq, wqp)
                    for i in range({0: 8, 1: 0, 2: 0, 3: 1}[mini]):
                        u = uT[i % 2]
                        load_uT(x_own, i * 512, u)
                        rope_tables(pos_own[:, i * 512:(i + 1) * 512], 0.125)
                        for h in range(2):
                            proj_rope(wq, wqp, h, u, lambda h=h: qT[:, h, :], qT.k)
                        nkb = 8 * i + 8
                        for h in range(2):
                            acc = [ps[2], ps[3], ps[4], ps[5]]
                            pairs = [(j, c) for j in range(nkb) for c in range(2)]

                            def qk(n):
                                j, c = pairs[n]
                                sc = ps[n % 2]
                                pr = slice(c * 64, (c + 1) * 64)
                                P.op("pe", lambda e, sc=sc, pr=pr, j=j, h=h: e.matmul(
                                    sc[:], kT[pr, h, j * 128:(j + 1) * 128], qT[pr, h, :], start=True, stop=True),
                                    [kT.k, qT.k], [sc.k])
                            qk(0)
                            for n in range(len(pairs)):
                                j, c = pairs[n]
                                sc = ps[n % 2]
                                pt = pT[n % 3]
                                P.op("act", lambda e, sc=sc, pt=pt: e.activation(out=pt[:], in_=sc[:], func=AF.Exp),
                                     [sc.k], [pt.k])
                                if n + 1 < len(pairs):
                                    qk(n + 1)
                                if j >= 8 * i:
                                    jj = j - 8 * i
                                    P.op("dve", lambda e, pt=pt, jj=jj: e.tensor_tensor(pt[:], pt[:], maskT[:, jj, :], ALU.mult),
                                         [pt.k, maskT.k], [pt.k])
                                P.op("pe", lambda e, pt=pt, j=j, h=h, c=c, nkb=nkb: e.matmul(
                                    acc[2 * c][:], Vt[:, j, h * 128:(h + 1) * 128], pt[:],
                                    start=(j == 0), stop=(j == nkb - 1)), [Vt.k, pt.k], [acc[2 * c].k])
                                P.op("pe", lambda e, pt=pt, j=j, c=c, nkb=nkb: e.matmul(
                                    acc[2 * c + 1][:], onesb[:], pt[:],
                                    start=(j == 0), stop=(j == nkb - 1)), [onesb.k, pt.k], [acc[2 * c + 1].k])
                            for c in range(2):
                                P.op("dve", lambda e, c=c: e.reciprocal(rr[c][:], acc[2 * c + 1][:]), [acc[2 * c + 1].k], [rr[c].k])
                                P.op("dve", lambda e, c=c: e.tensor_tensor(rr[c][:], rr[c][:], acc[2 * c][:], ALU.mult),
                                     [rr[c].k, acc[2 * c].k], [rr[c].k])
                            P.op("dve", lambda e: e.scalar_tensor_tensor(oo[:], rr[1][:], lsc[:, 2:3], rr[0][:], ALU.mult, ALU.add),
                                 [rr[0].k, rr[1].k, lsc.k], [oo.k])
                            P.op("act", lambda e: e.activation(out=sqb[:], in_=oo[:], func=AF.Square), [oo.k], [sqb.k])
                            P.op("pe", lambda e: e.matmul(ps[6][:], onesb[:], sqb[:], start=True, stop=True),
                                 [onesb.k, sqb.k], [ps[6].k])
                            P.op("dve", lambda e: e.tensor_scalar(rr[0][:], ps[6][:], 1.0 / 128, EPS, ALU.mult, ALU.add),
                                 [ps[6].k], [rr[0].k])
                            P.op("pool", lambda e: e.tensor_tensor(rr[0][:], rr[0][:], mhalf[:].to_broadcast([128, 512]), ALU.pow),
                                 [rr[0].k, mhalf.k], [rr[0].k])
                            P.op("dve", lambda e, h=h, i=i, hp=hp: e.scalar_tensor_tensor(
                                oT_da[:, 2 * hp + h, i * 512:(i + 1) * 512], oo[:], gsub[:], rr[0][:], ALU.mult, ALU.mult),
                                [oo.k, gsub.k, rr[0].k], [oT_da.k])

        if do_da:
            _phase_da()
        P.barrier()
        def _phase_c():
            with contextlib.ExitStack() as st:
                def sb(name, shape, dt=F32):
                    return TT(st, nc, name, shape, dt)

                gple = sb("gple", [128, D])
                gfin = sb("gfin", [128, D])
                P.dma("sp", lambda e: e.dma_start(out=gple[:], in_=norm_ple_g.partition_broadcast(128)), [], [gple.k])
                P.dma("sp", lambda e: e.dma_start(out=gfin[:], in_=norm_final_g.partition_broadcast(128)), [], [gfin.k])
                wo = sb("wo", [128, 8, D], BF16)
                wg = sb("wg", [128, 8, D], BF16)
                wp = sb("wp", [128, 2, D], BF16)
                load_w(wo, w_out, D)
                load_w(wg, ple_gate_w, D)
                load_w(wp, ple_proj_w, D)
                if do_peer:
                    gffn = sb("gffn", [128, D])
                    P.dma("sp", lambda e: e.dma_start(out=gffn[:], in_=norm_ffn_g.partition_broadcast(128)), [], [gffn.k])
                    wq = sb("wq", [128, 8, 2048], BF16)
                    load_w(wq, peer_w_q, 2048)
                    iota16 = sb("iota16", [128, 16])
                    P.dma("sp", lambda e: e.dma_start(out=iota16[:], in_=iota_d), [], [iota16.k])
                    keysT = sb("keysT", [128, 16, 128], BF16)
                    kst = stage
                    for g4 in range(4):
                        P.dma("sp", lambda e, g4=g4: e.dma_start(
                            out=kst[:, 0:512].rearrange("p (a d) -> p a d", a=4), in_=peer_keys[g4 * 4:(g4 + 1) * 4].rearrange("h n d -> n h d")), [], [kst.k])
                        for q in range(4):
                            P.op("pe", lambda e, q=q: e.transpose(ps[0][:, q * 128:(q + 1) * 128], kst[:, q * 128:(q + 1) * 128], ident[:]),
                                 [kst.k, ident.k], [ps[0].k])
                        P.op("act", lambda e, g4=g4: e.copy(keysT[:, g4 * 4:(g4 + 1) * 4, :].rearrange("p a n -> p (a n)"), ps[0][:]),
                             [ps[0].k], [keysT.k])
                    qTg = sb("qTg", [128, 4, 128], BF16)
                    s4 = sb("s4", [128, 4, 128])
                    s4b = sb("s4b", [128, 128])
                    tv = sb("tv", [128, 16, 16])
                    ti = sb("ti", [128, 16, 16], U32)
                    tif = sb("tif", [128, 16, 16])
                    cand = sb("cand", [128, 256])
                    cand2 = sb("cand2", [128, 256])
                    best = sb("best", [128, 8, 16])
                    pos = sb("pos", [128, 8, 16], U32)
                    pij = sb("pij", [128, 2, 128], I32)
                    pijf = sb("pijf", [128, 2, 128])
                    oh = sb("oh", [128, 8, 16, 16], BF16)
                    e01 = sb("e01", [128, 2, 128])
                    idxf = sb("idxf", [128, 128])
                    idxi = sb("idxi", [128, 128], I32)
                    gsm = sb("gsm", [128, 8, 16])
                    gss = sb("gss", [128, 8])
                    hid = sb("hid", [128, 128])
                    wgt = sb("wgt", [128, 128])
                    NG = 4
                    UV = [sb(f"UV{i}", [128, 2, D], BF16) for i in range(NG)]
                    dgs = [sb(f"dg{i}", [128, 128], BF16) for i in range(2)]

                NB = 1
                hbuf = [sb(f"h{i}", [128, D]) for i in range(NB)]
                junk = sb("junk", [128, D])
                ssb = [sb(f"ss{i}", [128, 4]) for i in range(NB)]
                nb = [sb(f"n{i}", [128, D], BF16) for i in range(NB)]
                nT = [sb(f"nT{i}", [128, 8, 128], BF16) for i in range(NB)]
                orT = [sb(f"orT{i}", [128, 4, 128], BF16) for i in range(NB)]
                pin = [sb(f"pin{i}", [128, 256]) for i in range(NB)]
                pb = [sb(f"pb{i}", [128, 256], BF16) for i in range(NB)]
                pT2 = [sb(f"pT2{i}", [128, 2, 128], BF16) for i in range(NB)]
                gate = [sb(f"gate{i}", [128, D]) for i in range(NB)]
                h3 = gate
                ob = hbuf
                prod, xnb, xnT, xn = junk, nb[0], nT[0], gate[0]

                NT = c_tiles if c_tiles else {0: NOWN // 128, 1: 0, 2: 0, 3: 2}[mini]
                for it in range(NT):
                    b = it % NB
                    rows = slice(it * 128, (it + 1) * 128)
                    h = hbuf[b]
                    P.dma("sp", lambda e, h=h, rows=rows: e.dma_start(out=h[:], in_=x_own[rows, :]), [], [h.k])
                    P.dma("sp", lambda e, b=b, rows=rows: e.dma_start(out=pin[b][:], in_=p_own[rows, :]), [], [pin[b].k])
                    if do_da or do_rw:
                        if do_rw:
                            for c in range(4):
                                P.op("pe", lambda e, c=c, it=it: e.transpose(pst[:, c * 128:(c + 1) * 128],
                                                                             o_rw[:, it, c * 128:(c + 1) * 128], identb[:]),
                                     [o_rw.k, identb.k], [pst.k])
                            P.op("act", lambda e, b=b: e.copy(orT[b][:], pst[:, 0:512].rearrange("p (k t) -> p k t", k=4)),
                                 [pst.k], [orT[b].k])
                        for half in range(2):
                            cs = slice(half * 512, (half + 1) * 512)
                            pg = ps[4 + half]
                            kcs = ([0, 1, 2, 3] if do_da else []) + ([4, 5, 6, 7] if do_rw else [])
                            for n_, kc in enumerate(kcs):
                                if kc < 4:
                                    lhs = (lambda kc=kc, it=it: oT_da[:, kc, it * 128:(it + 1) * 128])
                                    rk = oT_da.k
                                else:
                                    lhs = (lambda kc=kc, b=b: orT[b][:, kc - 4, :])
                                    rk = orT[b].k
                                P.op("pe", lambda e, lhs=lhs, kc=kc, cs=cs, pg=pg, n_=n_, kcs=kcs: e.matmul(
                                    pg[:], lhs(), wo[:, kc, cs], start=(n_ == 0), stop=(n_ == len(kcs) - 1)),
                                    [rk, wo.k], [pg.k])
                            P.op("dve", lambda e, h=h, cs=cs, pg=pg: e.tensor_tensor(h[:, cs], h[:, cs], pg[:], ALU.add),
                                 [h.k, pg.k], [h.k])
                    if do_peer:
                        rmsnorm(h, gffn, xn, junk, ssb[b], 2)
                        P.op("act", lambda e: e.copy(xnb[:], xn[:]), [xn.k], [xnb.k])
                        transpose_to(xnb, 8, lambda: xnT[:], xnT.k)
                        for g4 in range(4):
                            for q in range(4):
                                hp_ = g4 * 4 + q
                                for kc in range(8):
                                    P.op("pe", lambda e, q=q, kc=kc, hp_=hp_: e.matmul(
                                        ps[0][:, q * 128:(q + 1) * 128], wq[:, kc, hp_ * 128:(hp_ + 1) * 128], xnT[:, kc, :],
                                        start=(kc == 0), stop=(kc == 7)), [wq.k, xnT.k], [ps[0].k])
                            P.op("act", lambda e: e.copy(qTg[:].rearrange("p a t -> p (a t)"), ps[0][:]), [ps[0].k], [qTg.k])
                            for q in range(4):
                                hp_ = g4 * 4 + q
                                P.op("pe", lambda e, q=q, hp_=hp_: e.matmul(ps[1][:, q * 128:(q + 1) * 128], qTg[:, q, :], keysT[:, hp_, :],
                                                                          start=True, stop=True), [qTg.k, keysT.k], [ps[1].k])
                            P.op("act", lambda e: e.copy(s4[:].rearrange("p a n -> p (a n)"), ps[1][:]), [ps[1].k], [s4.k])
                            for q in range(4):
                                hp_ = g4 * 4 + q
                                P.op("dve", lambda e, q=q, hp_=hp_: e.max(out=tv[:, hp_, 0:8], in_=s4[:, q, :]), [s4.k], [tv.k])
                                P.op("dve", lambda e, q=q, hp_=hp_: e.max_index(out=ti[:, hp_, 0:8], in_max=tv[:, hp_, 0:8], in_values=s4[:, q, :]),
                                     [s4.k, tv.k], [ti.k])
                                P.op("dve", lambda e, q=q, hp_=hp_: e.match_replace(out=s4b[:], in_to_replace=tv[:, hp_, 0:8], in_values=s4[:, q, :],
                                                                                    imm_value=-1e30), [s4.k, tv.k], [s4b.k])
                                P.op("dve", lambda e, hp_=hp_: e.max(out=tv[:, hp_, 8:16], in_=s4b[:]), [s4b.k], [tv.k])
                                P.op("dve", lambda e, hp_=hp_: e.max_index(out=ti[:, hp_, 8:16], in_max=tv[:, hp_, 8:16], in_values=s4b[:]),
                                     [s4b.k, tv.k], [ti.k])
                        P.op("dve", lambda e: e.tensor_copy(tif[:], ti[:]), [ti.k], [tif.k])
                        for hh_ in range(8):
                            c3 = lambda: cand[:].rearrange("p (i j) -> p i j", i=16)
                            P.op("dve", lambda e, hh_=hh_, c3=c3: e.tensor_tensor(
                                c3(), tv[:, 2 * hh_, :].unsqueeze(2).to_broadcast([128, 16, 16]),
                                tv[:, 2 * hh_ + 1, :].unsqueeze(1).to_broadcast([128, 16, 16]), ALU.add), [tv.k], [cand.k])
                            P.op("dve", lambda e, hh_=hh_: e.max(out=best[:, hh_, 0:8], in_=cand[:]), [cand.k], [best.k])
                            P.op("dve", lambda e, hh_=hh_: e.max_index(out=pos[:, hh_, 0:8], in_max=best[:, hh_, 0:8], in_values=cand[:]),
                                 [cand.k, best.k], [pos.k])
                            P.op("dve", lambda e, hh_=hh_: e.match_replace(out=cand2[:], in_to_replace=best[:, hh_, 0:8], in_values=cand[:],
                                                                           imm_value=-1e30), [cand.k, best.k], [cand2.k])
                            P.op("dve", lambda e, hh_=hh_: e.max(out=best[:, hh_, 8:16], in_=cand2[:]), [cand2.k], [best.k])
                            P.op("dve", lambda e, hh_=hh_: e.max_index(out=pos[:, hh_, 8:16], in_max=best[:, hh_, 8:16], in_values=cand2[:]),
                                 [cand2.k, best.k], [pos.k])
                        posf = lambda: pos[:].rearrange("p h k -> p (h k)")
                        P.op("dve", lambda e: e.tensor_copy(idxf[:], posf()), [pos.k], [idxf.k])
                        P.op("dve", lambda e: e.tensor_scalar(pijf[:, 1, :], idxf[:], 0.0625, None, ALU.mult), [idxf.k], [pijf.k])
                        P.op("dve", lambda e: e.tensor_copy(pij[:, 0, :], pijf[:, 1, :]), [pijf.k], [pij.k])
                        P.op("dve", lambda e: e.tensor_copy(pijf[:, 0, :], pij[:, 0, :]), [pij.k], [pijf.k])
                        P.op("dve", lambda e: e.tensor_scalar(pijf[:, 1, :], pijf[:, 0, :], 16.0, None, ALU.mult), [pijf.k], [pijf.k])
                        P.op("dve", lambda e: e.tensor_tensor(pijf[:, 1, :], pijf[:, 1, :], idxf[:], ALU.is_gt), [pijf.k, idxf.k], [pijf.k])
                        P.op("dve", lambda e: e.tensor_tensor(pijf[:, 0, :], pijf[:, 0, :], pijf[:, 1, :], ALU.subtract), [pijf.k], [pijf.k])
                        P.op("dve", lambda e: e.scalar_tensor_tensor(pijf[:, 1, :], pijf[:, 0, :], -16.0, idxf[:], ALU.mult, ALU.add),
                             [pijf.k, idxf.k], [pijf.k])
                        tif4 = lambda: tif[:].rearrange("p (h two) k -> p h two k", two=2)
                        for pp_ in range(2):
                            P.op("dve", lambda e, pp_=pp_: e.tensor_tensor(
                                oh[:], pijf[:, pp_, :].rearrange("p (h k) -> p h k", h=8).unsqueeze(3).to_broadcast([128, 8, 16, 16]),
                                iota16[:].unsqueeze(1).unsqueeze(1).to_broadcast([128, 8, 16, 16]), ALU.is_equal),
                                [pijf.k, iota16.k], [oh.k])
                            P.op("dve", lambda e, pp_=pp_: e.tensor_tensor(
                                oh[:], oh[:], tif4()[:, :, pp_, :].unsqueeze(2).to_broadcast([128, 8, 16, 16]), ALU.mult),
                                [oh.k, tif.k], [oh.k])
                            P.op("dve", lambda e, pp_=pp_: e.tensor_reduce(e01[:, pp_, :], oh[:].rearrange("p h k i -> p (h k) i"), AX.X, ALU.add),
                                 [oh.k], [e01.k])
                        P.op("dve", lambda e: e.scalar_tensor_tensor(idxf[:], e01[:, 0, :], 128.0, e01[:, 1, :], ALU.mult, ALU.add),
                             [e01.k], [idxf.k])
                        P.op("dve", lambda e: e.tensor_copy(idxi[:], idxf[:]), [idxf.k], [idxi.k])
                        P.op("dve", lambda e: e.tensor_tensor(gsm[:], best[:], best[:, :, 0:1].to_broadcast([128, 8, 16]), ALU.subtract),
                             [best.k], [gsm.k])
                        P.op("act", lambda e: e.activation(out=gsm[:], in_=gsm[:], func=AF.Exp), [gsm.k], [gsm.k])
                        P.op("dve", lambda e: e.tensor_reduce(gss[:], gsm[:], AX.X, ALU.add), [gsm.k], [gss.k])
                        P.op("dve", lambda e: e.reciprocal(gss[:], gss[:]), [gss.k], [gss.k])
                        P.op("dve", lambda e: e.tensor_tensor(gsm[:], gsm[:], gss[:].unsqueeze(2).to_broadcast([128, 8, 16]), ALU.mult),
                             [gsm.k, gss.k], [gsm.k])
                        NS = peer_slots
                        gflat = lambda: gsm[:].rearrange("p h k -> p (h k)")
                        for s_ in range(NS + 1):
                            if s_ < NS:
                                uv = UV[s_ % NG]
                                P.dma("pool", lambda e, s_=s_, uv=uv: e.indirect_dma_start(
                                    out=uv[:].rearrange("p a d -> p (a d)"), out_offset=None, in_=uvb,
                                    in_offset=bass.IndirectOffsetOnAxis(ap=idxi[:, s_:s_ + 1], axis=0)), [idxi.k, uvb_tok], [uv.k])
                                P.op("dve", lambda e, s_=s_, uv=uv: e.scalar_tensor_tensor(
                                    prod[:], uv[:, 0, :], 1.0, xn[:], ALU.mult, ALU.mult, accum_out=hid[:, s_:s_ + 1]),
                                    [uv.k, xn.k], [prod.k, hid.k])
                            if s_ >= 1:
                                t_ = s_ - 1
                                uvp = UV[t_ % NG]
                                P.op("act", lambda e, t_=t_: e.activation(out=wgt[:, t_:t_ + 1], in_=hid[:, t_:t_ + 1], func=AF.Gelu),
                                     [hid.k], [wgt.k])
                                P.op("dve", lambda e, t_=t_: e.tensor_tensor(wgt[:, t_:t_ + 1], wgt[:, t_:t_ + 1], gflat()[:, t_:t_ + 1], ALU.mult),
                                     [wgt.k, gsm.k], [wgt.k])
                                dg = dgs[t_ % 2]
                                P.op("act", lambda e, t_=t_, dg=dg: e.activation(out=dg[:], in_=identb[:], func=AF.Copy, scale=wgt[:, t_:t_ + 1]),
                                     [identb.k, wgt.k], [dg.k])
                                for hf in range(2):
                                    P.op("pe", lambda e, t_=t_, dg=dg, uvp=uvp, hf=hf, NS=NS: e.matmul(
                                        ps[2 + hf][:], dg[:], uvp[:, 1, hf * 512:(hf + 1) * 512], start=(t_ == 0), stop=(t_ == NS - 1)),
                                        [dg.k, uvp.k], [ps[2 + hf].k])
                        for hf in range(2):
                            P.op("dve", lambda e, hf=hf, h=h: e.tensor_tensor(h[:, hf * 512:(hf + 1) * 512], h[:, hf * 512:(hf + 1) * 512], ps[2 + hf][:], ALU.add),
                                 [h.k, ps[2 + hf].k], [h.k])
                    rmsnorm(h, gple, nb[b], junk, ssb[b], 0)
                    transpose_to(nb[b], 8, lambda b=b: nT[b][:], nT[b].k)
                    P.op("dve", lambda e, b=b: e.tensor_copy(pb[b][:], pin[b][:]), [pin[b].k], [pb[b].k])
                    transpose_to(pb[b], 2, lambda b=b: pT2[b][:], pT2[b].k)
                    for half in range(2):
                        cs = slice(half * 512, (half + 1) * 512)
                        pg = ps[half]
                        for kc in range(8):
                            P.op("pe", lambda e, b=b, kc=kc, cs=cs, pg=pg: e.matmul(
                                pg[:], nT[b][:, kc, :], wg[:, kc, cs], start=(kc == 0), stop=(kc == 7)),
                                [nT[b].k, wg.k], [pg.k])
                        P.op("act", lambda e, b=b, cs=cs, pg=pg: e.activation(out=gate[b][:, cs], in_=pg[:],
                                                                              func=AF.Sigmoid),
                             [pg.k], [gate[b].k])
                        pp = ps[2 + half]
                        for kc in range(2):
                            P.op("pe", lambda e, b=b, kc=kc, cs=cs, pp=pp: e.matmul(
                                pp[:], pT2[b][:, kc, :], wp[:, kc, cs], start=(kc == 0), stop=(kc == 1)),
                                [pT2[b].k, wp.k], [pp.k])
                        P.op("dve", lambda e, b=b, cs=cs, pp=pp: e.tensor_tensor(gate[b][:, cs], gate[b][:, cs], pp[:],
                                                                                 ALU.mult),
                             [gate[b].k, pp.k], [gate[b].k])
                    P.op("dve", lambda e, b=b, h=h: e.tensor_tensor(h3[b][:], h[:], gate[b][:], ALU.add),
                         [h.k, gate[b].k], [h3[b].k])
                    rmsnorm(h3[b], gfin, ob[b], junk, ssb[b], 1)
                    P.dma("sp", lambda e, b=b, rows=rows: e.dma_start(out=out_d[rows, :], in_=ob[b][:]),
                          [ob[b].k], [out_tok])
        if not rw_stop:
            _phase_c()
        P.finish([out_tok, DBG_TOK])
        P.emit()
    return nc


_NC_CACHE = {}


def _masks(hh):
    p = np.arange(128)[:, None, None]
    jj = np.arange(8)[None, :, None]
    f = np.arange(512)[None, None, :]
    return ((128 * jj + p) <= (512 * hh + f)).astype(np.float32).reshape(128, 8 * 512)


def _rw_consts():
    s = np.arange(128)[:, None]
    t = np.arange(128)[None, :]
    same = (s // 64) == (t // 64)
    su = (same & (s < t)).astype(np.float32)
    ui = (same & (s <= t)).astype(np.float32)
    sl = (same & (s > t)).astype(np.float32)
    c = np.zeros((128, 1280), np.float32)
    c[:, 0:128] = su; c[:, 128:256] = ui; c[:, 256:384] = su; c[:, 384:512] = ui
    c[:, 512:640] = sl; c[:, 640:768] = sl
    c[:, 768:896] = ui
    c[:, 896:1024] = same.astype(np.float32)
    c[:, 1024:1088] = (np.arange(128)[:, None] % 64 == np.arange(64)[None, :]).astype(np.float32)
    return c


def kernel(**inputs):
    f32 = lambda a: np.ascontiguousarray(np.asarray(a, dtype=np.float32))
    x = f32(inputs["x"])
    p = f32(inputs["p"])[0]
    pos = np.ascontiguousarray(np.asarray(inputs["positions"], dtype=np.int32))
    B = x.shape[0]
    key = "nc"
    if key not in _NC_CACHE:
        _NC_CACHE[key] = build(**_BUILD_FLAGS)
    nc = _NC_CACHE[key]
    ident = np.eye(128, dtype=np.float32)
    cvec = np.zeros((128, 4), np.float32)
    for q in range(128):
        d = q % 64
        cvec[q, 0] = INVF[d % 8] if d < 16 else 0.0
        cvec[q, 1] = -1.0 if d < 8 else 1.0
    lamv = np.concatenate([f32(inputs["lam_q1"])[0], f32(inputs["lam_k1"])[0],
                           f32(inputs["lam_q2"])[0], f32(inputs["lam_k2"])[0]]).reshape(1, 256)
    shared = {
        "ident": ident, "cvec": cvec,
        "norm_mix_g": f32(inputs["norm_mix_g"]).reshape(1, D),
        "w_in": f32(inputs["w_in"][0]),
        "lamv": lamv,
        "da_subln_g": f32(inputs["da_subln_g"]).reshape(128, 1),
        "w_out": f32(inputs["w_out"][0]),
        "norm_ple_g": f32(inputs["norm_ple_g"]).reshape(1, D),
        "norm_final_g": f32(inputs["norm_final_g"]).reshape(1, D),
        "ple_gate_w": f32(inputs["ple_gate_w"][0]),
        "ple_proj_w": f32(inputs["ple_proj_w"][0]),
    }
    masks = [_masks(0), _masks(1)]
    for nm in ("rw_mu", "rw_w0", "rw_a0", "rw_k_k", "rw_k_a", "rw_ln_g", "rw_ln_b"):
        shared[nm] = f32(inputs[nm]).reshape(1, -1)
    shared["rw_r_k"] = f32(inputs["rw_r_k"]).reshape(1, 512)
    shared["rw_w_up"] = f32(inputs["rw_w_up"][0])
    shared["rw_a_up"] = f32(inputs["rw_a_up"][0])
    shared["rw_g_up"] = f32(inputs["rw_g_up"][0])
    shared["rwc"] = _rw_consts()
    shared["norm_ffn_g"] = f32(inputs["norm_ffn_g"]).reshape(1, D)
    shared["peer_w_q"] = f32(inputs["peer_w_q"][0])
    shared["peer_sub_keys"] = f32(inputs["peer_sub_keys"][0]).reshape(16, 128, 128)
    shared["peer_u"] = f32(inputs["peer_u"][0])
    shared["peer_v"] = f32(inputs["peer_v"][0])
    shared["iota16"] = np.tile(np.arange(16, dtype=np.float32)[None, :], (128, 1))
    in_maps = []
    for c in range(8):
        b, hh = c // 2, c % 2
        m = dict(shared)
        m["x_full"] = x[b]
        m["pos_full"] = pos[b].reshape(1, SEQ)
        m["x_own"] = np.ascontiguousarray(x[b].reshape(8, 2, 512, D)[:, hh].reshape(NOWN, D))
        m["pos_own"] = np.ascontiguousarray(pos[b].reshape(8, 2, 512)[:, hh].reshape(1, NOWN))
        m["p_own"] = np.ascontiguousarray(p[b].reshape(8, 2, 512, 256)[:, hh].reshape(NOWN, 256))
        m["maskT"] = masks[hh]
        m["sel"] = np.full((128, 1), float(hh), np.float32)
        in_maps.append(m)
    res = run_bass_kernel_spmd(nc, in_maps, core_ids=list(range(8)))
    out = np.empty((B, SEQ, D), dtype=np.float32)
    for c in range(8):
        b, hh = c // 2, c % 2
        out[b].reshape(8, 2, 512, D)[:, hh] = np.asarray(res.results[c]["out"]).reshape(8, 512, D)
    return out


_BUILD_FLAGS = dict(do_da=True, do_rw=True, do_peer=True)
```

```python
import contextlib
import numpy as np
import concourse.bass as bass
import concourse.mybir as mybir
from concourse.bass_utils import run_bass_kernel_spmd

F32 = mybir.dt.float32
BF16 = mybir.dt.bfloat16
I32 = mybir.dt.int32
U32 = mybir.dt.uint32
AF = mybir.ActivationFunctionType
ALU = mybir.AluOpType
AX = mybir.AxisListType

D = 1024
SEQ = 8192
NOWN = 4096
EPS = 1e-6
ENGS = ("pe", "act", "dve", "pool", "sp")
NDSEM = 8


class Tok:
    __slots__ = ("w", "r", "psum")

    def __init__(self):
        self.w = None
        self.r = []
        self.psum = False


class Prog:
    def __init__(self, nc):
        self.nc = nc
        self.ops = {e: [] for e in ENGS}
        self.ndma = {e: 0 for e in ENGS}
        self.barrier_deps = set()

    def barrier(self):
        deps = set()
        for e in ENGS:
            n = len(self.ops[e])
            for i in range(n - 1, -1, -1):
                if self.ops[e][i]["me"][0] == "eng":
                    deps.add(("eng", e, i))
                    break
            for j in range(max(0, self.ndma[e] - NDSEM), self.ndma[e]):
                deps.add(("dma", e, j))
        self.barrier_deps = deps

    def _add(self, eng, fn, reads, writes, dma):
        deps = set()
        for t in reads:
            if t.w is not None:
                deps.add(t.w)
            if t.psum:
                for d in t.r:
                    if d[1] != eng:
                        deps.add(d)
        for t in writes:
            if t.w is not None:
                deps.add(t.w)
            for d in t.r:
                deps.add(d)
        deps |= self.barrier_deps
        idx = len(self.ops[eng])
        if dma:
            j = self.ndma[eng]
            self.ndma[eng] += 1
            me = ("dma", eng, j)
            if j >= NDSEM:
                deps.add(("dma", eng, j - NDSEM))
        else:
            me = ("eng", eng, idx)
        if eng == "pe":
            deps = {d for d in deps if not (d[0] == "eng" and d[1] == "pe")}
        deps.discard(me)
        self.ops[eng].append(dict(fn=fn, deps=deps, me=me))
        for t in reads:
            t.r.append(me)
        for t in writes:
            t.w = me
            t.r = []
        return me

    def op(self, eng, fn, reads=(), writes=()):
        return self._add(eng, fn, list(reads), list(writes), False)

    def dma(self, eng, fn, reads=(), writes=()):
        return self._add(eng, fn, list(reads), list(writes), True)

    def finish(self, toks):
        deps = set()
        for t in toks:
            if t.w is not None:
                deps.add(t.w)
        self.ops["sp"].append(dict(fn=lambda e: e.nop(), deps=deps,
                                   me=("eng", "sp", len(self.ops["sp"]))))

    def emit(self):
        nc = self.nc
        needed = {e: set() for e in ENGS}
        for e in ENGS:
            for o in self.ops[e]:
                for d in o["deps"]:
                    if d[0] == "eng":
                        needed[d[1]].add(d[2])
        sigcount = {}
        for e in ENGS:
            c = 0
            for i, o in enumerate(self.ops[e]):
                if i in needed[e] and o["me"][0] == "eng":
                    c += 1
                    sigcount[(e, i)] = c
        with contextlib.ExitStack() as st:
            esem = {e: st.enter_context(nc.semaphore(f"s_{e}")) for e in ENGS}
            dsem = {e: [st.enter_context(nc.semaphore(f"d_{e}{k}")) for k in range(NDSEM)]
                    for e in ("sp", "act", "pool")}
            block = st.enter_context(nc.Block())

            def run(eng_name, eng):
                waited = {}
                for i, o in enumerate(self.ops[eng_name]):
                    for d in sorted(o["deps"]):
                        if d[0] == "eng":
                            sem = esem[d[1]]
                            val = sigcount[(d[1], d[2])]
                            key = ("e", d[1])
                        else:
                            sem = dsem[d[1]][d[2] % NDSEM]
                            val = 16 * (d[2] // NDSEM + 1)
                            key = ("d", d[1], d[2] % NDSEM)
                        if waited.get(key, 0) >= val:
                            continue
                        eng.wait_ge(sem, val)
                        waited[key] = val
                    ins = o["fn"](eng)
                    me = o["me"]
                    if me[0] == "dma":
                        ins.then_inc(dsem[me[1]][me[2] % NDSEM], 16)
                    elif (eng_name, i) in sigcount:
                        ins.then_inc(esem[eng_name], 1)

            @block.tensor
            def _(e):
                run("pe", e)

            @block.scalar
            def _(e):
                run("act", e)

            @block.vector
            def _(e):
                run("dve", e)

            @block.gpsimd
            def _(e):
                run("pool", e)

            @block.sync
            def _(e):
                run("sp", e)


class TT:
    _used = {}

    def __init__(self, st, nc, name, shape, dtype, psum=False):
        k_ = (id(nc), name)
        n_ = TT._used.get(k_, 0)
        TT._used[k_] = n_ + 1
        if n_:
            name = f"{name}_v{n_}"
        if psum:
            self.t = st.enter_context(nc.psum_tensor("P_" + name, shape, dtype))
        else:
            self.t = st.enter_context(nc.sbuf_tensor("S_" + name, shape, dtype))
        self.k = Tok()
        self.k.psum = psum

    def __getitem__(self, idx):
        return self.t[idx]


INVF = [float(500000.0 ** (-(i * 2.0) / 16.0)) for i in range(8)]
TWO_PI = 6.283185307179586
C1 = 6.28125
C2 = TWO_PI - C1
LAM_INIT = 0.2
DEBUG = False


class _Stop(Exception):
    pass

DBG_DONE = []
DBG_TOK = Tok()


def build(do_da=True, do_rw=True, do_peer=True, mini=0, rw_tiles=0, rw_stop=0, c_tiles=0, peer_slots=128, cvt_blocks=128):
    nc = bass.Bass("TRN2", target_bir_lowering=False)
    P = Prog(nc)

    def din(name, shape, dt=F32):
        return nc.dram_tensor(name, list(shape), dt, kind="ExternalInput").ap()

    x_full = din("x_full", [SEQ, D])
    pos_full = din("pos_full", [1, SEQ], I32)
    x_own = din("x_own", [NOWN, D])
    pos_own = din("pos_own", [1, NOWN], I32)
    p_own = din("p_own", [NOWN, 256])
    ident_d = din("ident", [128, 128])
    cvec_d = din("cvec", [128, 4])
    mask_d = din("maskT", [128, 8 * 512])
    norm_mix_g = din("norm_mix_g", [1, D])
    w_in = din("w_in", [D, 3328])
    lamv = din("lamv", [1, 256])
    da_subln_g = din("da_subln_g", [128, 1])
    w_out = din("w_out", [D, D])
    norm_ple_g = din("norm_ple_g", [1, D])
    norm_final_g = din("norm_final_g", [1, D])
    ple_gate_w = din("ple_gate_w", [D, D])
    ple_proj_w = din("ple_proj_w", [256, D])
    norm_ffn_g = din("norm_ffn_g", [1, D])
    peer_w_q = din("peer_w_q", [D, 2048])
    peer_keys = din("peer_sub_keys", [16, 128, 128])
    peer_u = din("peer_u", [16384, D])
    peer_v = din("peer_v", [16384, D])
    iota_d = din("iota16", [128, 16])
    rwc_d = din("rwc", [128, 1280])
    sel_d = din("sel", [128, 1])
    rw_mu = din("rw_mu", [1, 1792])
    rw_w0 = din("rw_w0", [1, 512])
    rw_w_up = din("rw_w_up", [64, 512])
    rw_a0 = din("rw_a0", [1, 512])
    rw_a_up = din("rw_a_up", [64, 512])
    rw_g_up = din("rw_g_up", [128, 512])
    rw_k_k = din("rw_k_k", [1, 512])
    rw_k_a = din("rw_k_a", [1, 512])
    rw_r_k = din("rw_r_k", [1, 512])
    rw_ln_g = din("rw_ln_g", [1, 512])
    rw_ln_b = din("rw_ln_b", [1, 512])
    out_d = nc.dram_tensor("out", [NOWN, D], F32, kind="ExternalOutput").ap()
    out_tok = Tok()

    with contextlib.ExitStack() as st0:
        def sb0(name, shape, dt=F32):
            return TT(st0, nc, name, shape, dt)

        ident = sb0("ident", [128, 128])
        identb = sb0("identb", [128, 128], BF16)
        onesb = sb0("onesb", [128, 128], BF16)
        cvec = sb0("cvec", [128, 4])
        mhalf = sb0("mhalf", [128, 1])
        P.dma("sp", lambda e: e.dma_start(out=ident[:], in_=ident_d), [], [ident.k])
        P.dma("sp", lambda e: e.dma_start(out=cvec[:], in_=cvec_d), [], [cvec.k])
        P.op("dve", lambda e: e.tensor_copy(identb[:], ident[:]), [ident.k], [identb.k])
        P.op("pool", lambda e: e.memset(onesb[:], 1.0), [], [onesb.k])
        P.op("pool", lambda e: e.memset(mhalf[:], -0.5), [], [mhalf.k])

        stage = sb0("stage", [128, 1024])

        phase_reads = []

        def load_bf16(dst, dst_view, src_view, eng="dve"):
            a, b_ = src_view.shape[1], src_view.shape[2]
            sv = stage[:, 0:a * b_].rearrange("p (a b) -> p a b", a=a)
            P.dma("sp", lambda e: e.dma_start(out=sv, in_=src_view), [], [stage.k])
            P.op(eng, lambda e: e.tensor_copy(dst_view, sv), [stage.k] + phase_reads, [dst.k])

        def load_w(dst, src2d, ncols):
            kch = src2d.shape[0] // 128
            step = 1024 // kch
            for c0 in range(0, ncols, step):
                load_bf16(dst, dst[:, :, c0:c0 + step],
                          src2d[:, c0:c0 + step].rearrange("(k p) n -> p k n", p=128))

        ps = [TT(st0, nc, f"ps{i}", [128, 512], F32, psum=True) for i in range(7)]
        pst = TT(st0, nc, "pst", [128, 1024], BF16, psum=True)

        o_rw = sb0("o_rw", [128, NOWN // 128, 512], BF16)

        def rmsnorm(src, gtile, dst, jk, ss, col, d=D):
            P.op("dve", lambda e: e.scalar_tensor_tensor(jk[:], src[:], 1.0, src[:], ALU.mult, ALU.mult,
                                                         accum_out=ss[:, col:col + 1]), [src.k], [jk.k, ss.k])
            P.op("dve", lambda e: e.tensor_scalar(ss[:, col:col + 1], ss[:, col:col + 1], 1.0 / d, EPS,
                                                  ALU.mult, ALU.add), [ss.k], [ss.k])
            P.op("act", lambda e: e.activation(out=ss[:, col:col + 1], in_=ss[:, col:col + 1], func=AF.Sqrt),
                 [ss.k], [ss.k])
            P.op("dve", lambda e: e.reciprocal(ss[:, col:col + 1], ss[:, col:col + 1]), [ss.k], [ss.k])
            P.op("dve", lambda e: e.scalar_tensor_tensor(dst[:], src[:], ss[:, col:col + 1], gtile[:],
                                                         ALU.mult, ALU.mult),
                 [src.k, ss.k, gtile.k], [dst.k])

        def transpose_to(src, nchunks, dst_ap_fn, dst_tok, extra_reads=()):
            for c in range(nchunks):
                P.op("pe", lambda e, c=c: e.transpose(pst[:, c * 128:(c + 1) * 128],
                                                      src[:, c * 128:(c + 1) * 128], identb[:]),
                     [src.k, identb.k], [pst.k])
            P.op("act", lambda e: e.copy(dst_ap_fn(), pst[:, 0:nchunks * 128].rearrange("p (k t) -> p k t", k=nchunks)),
                 [pst.k], [dst_tok])

        uvb = nc.dram_tensor("uvb_scratch", [16384, 2 * D], BF16, kind="Internal").ap()
        uvb_tok = Tok()

        def _phase_cvt():
            with contextlib.ExitStack() as st:
                def sb(name, shape, dt=F32):
                    return TT(st, nc, name, shape, dt)
                NCB = 4
                cin = [sb(f"cin{i}", [128, 2, D]) for i in range(NCB)]
                cou = [sb(f"cou{i}", [128, 2, D], BF16) for i in range(NCB)]
                for blk in range(cvt_blocks):
                    i_ = blk % NCB
                    rows = slice(blk * 128, (blk + 1) * 128)
                    P.dma("sp", lambda e, i_=i_, rows=rows: e.dma_start(out=cin[i_][:, 0, :], in_=peer_u[rows, :]), [], [cin[i_].k])
                    P.dma("sp", lambda e, i_=i_, rows=rows: e.dma_start(out=cin[i_][:, 1, :], in_=peer_v[rows, :]), [], [cin[i_].k])
                    if blk % 2 == 0:
                        P.op("act", lambda e, i_=i_: e.copy(cou[i_][:], cin[i_][:]), [cin[i_].k], [cou[i_].k])
                    else:
                        P.op("dve", lambda e, i_=i_: e.tensor_copy(cou[i_][:], cin[i_][:]), [cin[i_].k], [cou[i_].k])
                    P.dma("sp", lambda e, i_=i_, rows=rows: e.dma_start(
                        out=uvb[rows, :].rearrange("r (a d) -> r a d", a=2), in_=cou[i_][:]), [cou[i_].k], [uvb_tok])
            P.barrier()
        if do_peer and not do_rw:
            _phase_cvt()

        def _phase_rw():
            with contextlib.ExitStack() as st:
                def sb(name, shape, dt=F32):
                    return TT(st, nc, name, shape, dt)

                NTILE = rw_tiles if rw_tiles else SEQ // 128
                gmix = sb("gmix", [128, D])
                P.dma("sp", lambda e: e.dma_start(out=gmix[:], in_=norm_mix_g.partition_broadcast(128)), [], [gmix.k])
                rwc = sb("rwc", [128, 1280])
                P.dma("sp", lambda e: e.dma_start(out=rwc[:], in_=rwc_d), [], [rwc.k])
                maskUN = lambda: rwc[:, 0:512]
                maskSL2 = lambda: rwc[:, 512:768]
                triBD = lambda: rwc[:, 768:896]
                onesBD = lambda: rwc[:, 896:1024]
                I2 = lambda: rwc[:, 1024:1088]
                selt = sb("selt", [128, 1])
                P.dma("sp", lambda e: e.dma_start(out=selt[:], in_=sel_d), [], [selt.k])
                prm = sb("prm", [128, 7, 512])
                for n_, src in enumerate((rw_w0, rw_a0, rw_k_k, rw_k_a, rw_r_k, rw_ln_g, rw_ln_b)):
                    P.dma("sp", lambda e, n_=n_, src=src: e.dma_start(out=prm[:, n_, :], in_=src.partition_broadcast(128)),
                          [], [prm.k])
                lup = sb("lup", [128, 512])
                gup = sb("gup", [128, 512])
                P.dma("sp", lambda e: e.dma_start(out=lup[0:64, :], in_=rw_w_up), [], [lup.k])
                P.dma("sp", lambda e: e.dma_start(out=lup[64:128, :], in_=rw_a_up), [], [lup.k])
                P.dma("sp", lambda e: e.dma_start(out=gup[:], in_=rw_g_up), [], [gup.k])
                Wa = sb("Wa", [128, 8, 1792], BF16)
                Wb = sb("Wb", [128, 8, 1792], BF16)
                mut = sb("mut", [128, 128])
                for c0 in range(0, 1792, 128):
                    sv = stage[:, 0:1024].rearrange("p (a b) -> p a b", a=8)
                    P.dma("sp", lambda e, c0=c0: e.dma_start(
                        out=sv, in_=w_in[:, 1536 + c0:1536 + c0 + 128].rearrange("(k p) n -> p k n", p=128)),
                        [], [stage.k])
                    P.dma("sp", lambda e, c0=c0: e.dma_start(out=mut[:], in_=rw_mu[:, c0:c0 + 128].partition_broadcast(128)),
                          [], [mut.k])
                    mub = lambda: mut[:].unsqueeze(1).to_broadcast([128, 8, 128])
                    P.op("dve", lambda e, c0=c0: e.tensor_tensor(Wb[:, :, c0:c0 + 128], sv, mub(), ALU.mult),
                         [stage.k, mut.k], [Wb.k])
                    P.op("dve", lambda e: e.tensor_scalar(mut[:], mut[:], -1.0, 1.0, ALU.mult, ALU.add), [mut.k], [mut.k])
                    P.op("dve", lambda e, c0=c0: e.tensor_tensor(Wa[:, :, c0:c0 + 128], sv, mub(), ALU.mult),
                         [stage.k, mut.k], [Wa.k])

                xt = sb("xt", [128, D])
                jk = stage
                ssx = sb("ssx", [128, 2])
                ub = sb("ub", [128, D], BF16)
                uTe = [sb(f"uTe{i}", [128, 8, 129], BF16) for i in range(2)]
                P.op("pool", lambda e: e.memset(uTe[0][:], 0.0), [], [uTe[0].k])
                P.op("pool", lambda e: e.memset(uTe[1][:], 0.0), [], [uTe[1].k])
                r_t = sb("r_t", [128, 512])
                k_t = sb("k_t", [128, 512])
                v_t = sb("v_t", [128, 512])
                lora = sb("lora", [128, 256])
                loraT = sb("loraT", [128, 2, 128])
                lw = sb("lw", [128, 512])
                a_t = sb("a_t", [128, 512])
                kk = sb("kk", [128, 512])
                km = sb("km", [128, 512])
                ba = sb("ba", [128, 512])
                cl = sb("cl", [128, 512])
                clC = sb("clC", [128, 512])
                ex = sb("ex", [128, 512])
                tmp = sb("tmp", [128, 512])
                st8 = sb("st8", [128, 16])
                Ab = sb("Ab", [128, 512])
                Rb = sb("Rb", [128, 512])
                Bt = sb("Bt", [128, 512])
                Kt = sb("Kt", [128, 512])
                Bh = sb("Bh", [128, 512])
                Kh = sb("Kh", [128, 512])
                PC = sb("PC", [128, 512])
                g_t = sb("g_t", [128, 512])
                Y = sb("Y", [128, 512])
                T4 = sb("T4", [128, 4, 128])
                PCT = sb("PCT", [128, 2])
                UN = sb("UN", [128, 2, 256])
                MN = sb("MN", [128, 2, 256])
                Mq = [sb(f"Mq{i}", [128, 2, 128]) for i in range(2)]
                Uq = [sb(f"Uq{i}", [128, 2, 128]) for i in range(2)]
                TTt = [sb(f"TTt{i}", [128, 2, 128]) for i in range(2)]
                X0 = sb("X0", [128, 128])
                W0 = sb("W0", [128, 128])
                ApPad = sb("ApPad", [128, 2, 128])
                P.op("pool", lambda e: e.memset(ApPad[:], 0.0), [], [ApPad.k])
                W0pad = sb("W0pad", [128, 2, 128])
                Vpad = sb("Vpad", [128, 2, 128])
                P.op("pool", lambda e: e.memset(W0pad[:], 0.0), [], [W0pad.k])
                P.op("pool", lambda e: e.memset(Vpad[:], 0.0), [], [Vpad.k])
                R0 = sb("R0", [128, 128])
                R1 = sb("R1", [128, 128])
                P.op("pool", lambda e: e.memset(R0[:], 0.0), [], [R0.k])
                P.op("pool", lambda e: e.memset(R1[:], 0.0), [], [R1.k])
                GTbd = [sb(f"GTbd{i}", [128, 128]) for i in range(2)]
                for i in range(2):
                    P.op("pool", lambda e, i=i: e.memset(GTbd[i][:], 0.0), [], [GTbd[i].k])
                Sbd = [[sb(f"Sbd{p_}_{i}", [128, 128]) for i in range(2)] for p_ in range(4)]
                for p_ in range(4):
                    for i in range(2):
                        P.op("pool", lambda e, p_=p_, i=i: e.memset(Sbd[p_][i][:], 0.0), [], [Sbd[p_][i].k])
                scur = [0, 0, 0, 0]
                o_t = sb("o_t", [128, 512])
                o_b = sb("o_b", [128, 512], BF16)

                def ev(eng, out_fn, in_fn, reads, writes):
                    if eng == "act":
                        P.op("act", lambda e: e.copy(out_fn(), in_fn()), reads, writes)
                    else:
                        P.op("dve", lambda e: e.tensor_copy(out_fn(), in_fn()), reads, writes)

                if do_peer:
                    cin_ = sb("cin_rw", [128, D])
                    cou_ = sb("cou_rw", [128, D], BF16)
                cvt_per_tile = -(-cvt_blocks // NTILE)

                def cvt_block(blk):
                    rows_ = slice(blk * 128, (blk + 1) * 128)
                    for a_, src_ in enumerate((peer_u, peer_v)):
                        P.dma("sp", lambda e, src_=src_: e.dma_start(out=cin_[:], in_=src_[rows_, :]), [], [cin_.k])
                        if a_ == 0:
                            P.op("act", lambda e: e.copy(cou_[:], cin_[:]), [cin_.k], [cou_.k])
                        else:
                            P.op("dve", lambda e: e.tensor_copy(cou_[:], cin_[:]), [cin_.k], [cou_.k])
                        P.dma("sp", lambda e, a_=a_: e.dma_start(out=uvb[rows_, a_ * D:(a_ + 1) * D], in_=cou_[:]),
                              [cou_.k], [uvb_tok])

                for it in range(NTILE):
                    ue = uTe[it % 2]
                    un = uTe[(it + 1) % 2]
                    rows = slice(it * 128, (it + 1) * 128)
                    P.dma("sp", lambda e, rows=rows: e.dma_start(out=xt[:], in_=x_full[rows, :]), [], [xt.k])
                    rmsnorm(xt, gmix, ub, jk, ssx, 0)
                    transpose_to(ub, 8, lambda ue=ue: ue[:, :, 1:129], ue.k)
                    P.op("dve", lambda e, ue=ue, un=un: e.tensor_copy(un[:, :, 0:1], ue[:, :, 128:129]), [ue.k], [un.k])
                    groups = [(0, 512, r_t, 0), (512, 512, k_t, 0), (1024, 512, v_t, 0), (1536, 256, lora, 0)]
                    for gi, (c0, w_, dst, _) in enumerate(groups):
                        pz = ps[gi % 2]
                        for kc in range(8):
                            P.op("pe", lambda e, kc=kc, c0=c0, w_=w_, pz=pz, ue=ue: e.matmul(
                                pz[:, 0:w_], ue[:, kc, 1:129], Wa[:, kc, c0:c0 + w_], start=(kc == 0), stop=False),
                                [ue.k, Wa.k], [pz.k])
                        for kc in range(8):
                            P.op("pe", lambda e, kc=kc, c0=c0, w_=w_, pz=pz, ue=ue: e.matmul(
                                pz[:, 0:w_], ue[:, kc, 0:128], Wb[:, kc, c0:c0 + w_], start=False, stop=(kc == 7)),
                                [ue.k, Wb.k], [pz.k])
                        P.op("act", lambda e, dst=dst, w_=w_, pz=pz: e.copy(dst[:, 0:w_], pz[:, 0:w_]), [pz.k], [dst.k])
                    if rw_stop == 1:
                        return
                    if do_peer:
                        for blk in range(it * cvt_per_tile, min((it + 1) * cvt_per_tile, cvt_blocks)):
                            cvt_block(blk)
                    for c in range(2):
                        P.op("pe", lambda e, c=c: e.transpose(ps[2][:, c * 128:(c + 1) * 128],
                                                              lora[:, c * 128:(c + 1) * 128], ident[:]),
                             [lora.k, ident.k], [ps[2].k])
                    P.op("act", lambda e: e.activation(out=loraT[0:64, 0, :], in_=ps[2][0:64, 0:128], func=AF.Tanh),
                         [ps[2].k], [loraT.k])
                    P.op("act", lambda e: e.copy(loraT[64:128, 0, :], ps[2][64:128, 0:128]), [ps[2].k], [loraT.k])
                    P.op("act", lambda e: e.activation(out=loraT[:, 1, :], in_=ps[2][:, 128:256], func=AF.Sigmoid),
                         [ps[2].k], [loraT.k])
                    P.op("pe", lambda e: e.matmul(ps[3][:], loraT[0:64, 0, :], lup[0:64, :], start=True, stop=True),
                         [loraT.k, lup.k], [ps[3].k])
                    P.op("pe", lambda e: e.matmul(ps[4][:], loraT[64:128, 0, :], lup[64:128, :], start=True, stop=True),
                         [loraT.k, lup.k], [ps[4].k])
                    P.op("pe", lambda e: e.matmul(ps[5][:], loraT[:, 1, :], gup[:], start=True, stop=True),
                         [loraT.k, gup.k], [ps[5].k])
                    if rw_stop == 2:
                        return
                    P.op("dve", lambda e: e.tensor_tensor(tmp[:], ps[3][:], prm[:, 0, :], ALU.add), [ps[3].k, prm.k], [tmp.k])
                    P.op("act", lambda e: e.activation(out=lw[:], in_=tmp[:], func=AF.Sigmoid), [tmp.k], [lw.k])
                    P.op("dve", lambda e: e.tensor_scalar(lw[:], lw[:], -0.6065306597126334, None, ALU.mult), [lw.k], [lw.k])
                    P.op("dve", lambda e: e.tensor_tensor(tmp[:], ps[4][:], prm[:, 1, :], ALU.add), [ps[4].k, prm.k], [tmp.k])
                    P.op("act", lambda e: e.activation(out=a_t[:], in_=tmp[:], func=AF.Sigmoid), [tmp.k], [a_t.k])
                    P.op("act", lambda e: e.copy(g_t[:], ps[5][:]), [ps[5].k], [g_t.k])
                    P.op("dve", lambda e: e.tensor_tensor(kk[:], k_t[:], prm[:, 2, :], ALU.mult), [k_t.k, prm.k], [kk.k])
                    P.op("dve", lambda e: e.tensor_tensor(tmp[:], kk[:], kk[:], ALU.mult), [kk.k], [tmp.k])
                    P.op("dve", lambda e: e.tensor_reduce(st8[:, 0:8], tmp[:].rearrange("p (h j) -> p h j", h=8), AX.X, ALU.add),
                         [tmp.k], [st8.k])
                    P.op("act", lambda e: e.activation(out=st8[:, 0:8], in_=st8[:, 0:8], func=AF.Sqrt), [st8.k], [st8.k])
                    P.op("dve", lambda e: e.tensor_scalar(st8[:, 0:8], st8[:, 0:8], 1e-12, None, ALU.max), [st8.k], [st8.k])
                    P.op("dve", lambda e: e.reciprocal(st8[:, 0:8], st8[:, 0:8]), [st8.k], [st8.k])
                    P.op("dve", lambda e: e.tensor_tensor(kk[:].rearrange("p (h j) -> p h j", h=8),
                                                          kk[:].rearrange("p (h j) -> p h j", h=8),
                                                          st8[:, 0:8].unsqueeze(2).to_broadcast([128, 8, 64]), ALU.mult),
                         [kk.k, st8.k], [kk.k])
                    P.op("dve", lambda e: e.scalar_tensor_tensor(tmp[:], a_t[:], -1.0, prm[:, 3, :], ALU.add, ALU.mult),
                         [a_t.k, prm.k], [tmp.k])
                    P.op("dve", lambda e: e.scalar_tensor_tensor(km[:], tmp[:], 1.0, k_t[:], ALU.add, ALU.mult),
                         [tmp.k, k_t.k], [km.k])
                    P.op("dve", lambda e: e.tensor_tensor(ba[:], kk[:], a_t[:], ALU.mult), [kk.k, a_t.k], [ba.k])
                    if rw_stop == 3:
                        return
                    P.op("pe", lambda e: e.matmul(ps[3][:], triBD(), lw[:], start=True, stop=True), [rwc.k, lw.k], [ps[3].k])
                    P.op("pe", lambda e: e.matmul(ps[4][:], onesBD(), lw[:], start=True, stop=True), [rwc.k, lw.k], [ps[4].k])
                    P.op("act", lambda e: e.copy(cl[:], ps[3][:]), [ps[3].k], [cl.k])
                    P.op("act", lambda e: e.copy(clC[:], ps[4][:]), [ps[4].k], [clC.k])
                    P.op("dve", lambda e: e.tensor_tensor(tmp[:], cl[:], lw[:], ALU.subtract), [cl.k, lw.k], [tmp.k])
                    P.op("act", lambda e: e.activation(out=ex[:], in_=tmp[:], func=AF.Exp), [tmp.k], [ex.k])
                    P.op("dve", lambda e: e.scalar_tensor_tensor(Ab[:], kk[:], -1.0, ex[:], ALU.mult, ALU.mult),
                         [kk.k, ex.k], [Ab.k])
                    P.op("act", lambda e: e.activation(out=ex[:], in_=cl[:], func=AF.Exp), [cl.k], [ex.k])
                    P.op("dve", lambda e: e.tensor_tensor(Rb[:], r_t[:], ex[:], ALU.mult), [r_t.k, ex.k], [Rb.k])
                    P.op("act", lambda e: e.activation(out=ex[:], in_=cl[:], func=AF.Exp, scale=-1.0), [cl.k], [ex.k])
                    P.op("dve", lambda e: e.tensor_tensor(Bt[:], ba[:], ex[:], ALU.mult), [ba.k, ex.k], [Bt.k])
                    P.op("dve", lambda e: e.tensor_tensor(Kt[:], km[:], ex[:], ALU.mult), [km.k, ex.k], [Kt.k])
                    P.op("dve", lambda e: e.tensor_tensor(tmp[:], clC[:], cl[:], ALU.subtract), [clC.k, cl.k], [tmp.k])
                    P.op("act", lambda e: e.activation(out=ex[:], in_=tmp[:], func=AF.Exp), [tmp.k], [ex.k])
                    P.op("dve", lambda e: e.tensor_tensor(Bh[:], ba[:], ex[:], ALU.mult), [ba.k, ex.k], [Bh.k])
                    P.op("dve", lambda e: e.tensor_tensor(Kh[:], km[:], ex[:], ALU.mult), [km.k, ex.k], [Kh.k])
                    P.op("act", lambda e: e.activation(out=PC[:], in_=clC[:], func=AF.Exp), [clC.k], [PC.k])

                    if rw_stop == 4:
                        return
                    for hp2 in range(4):
                        pc = slice(hp2 * 128, (hp2 + 1) * 128)
                        hc = [slice(hp2 * 128 + h * 64, hp2 * 128 + (h + 1) * 64) for h in range(2)]
                        pb_ = [slice(0, 64), slice(64, 128)]
                        for n_, src in enumerate((Ab, Rb, Bt, Kt)):
                            P.op("pe", lambda e, n_=n_, src=src, pc=pc: e.transpose(ps[2][:, n_ * 128:(n_ + 1) * 128],
                                                                                    src[:, pc], ident[:]),
                                 [src.k, ident.k], [ps[2].k])
                        P.op("pe", lambda e, pc=pc: e.transpose(ps[3][:, 0:128], PC[:, pc], ident[:]),
                             [PC.k, ident.k], [ps[3].k])
                        P.op("act", lambda e: e.copy(T4[:].rearrange("p a t -> p (a t)"), ps[2][:]), [ps[2].k], [T4.k])
                        P.op("dve", lambda e: e.tensor_copy(PCT[:], ps[3][:, 0:128].rearrange("p (c t) -> p c t", c=2)[:, :, 0]),
                             [ps[3].k], [PCT.k])
                        if rw_stop == 5:
                            return
                        for h in range(2):
                            P.op("pe", lambda e, h=h: e.matmul(ps[4][:, h * 256:(h + 1) * 256], T4[pb_[h], 2, :],
                                                               T4[pb_[h], 0:2, :].rearrange("p a t -> p (a t)"), start=True, stop=True),
                                 [T4.k], [ps[4].k])
                            P.op("pe", lambda e, h=h: e.matmul(ps[5][:, h * 256:(h + 1) * 256], T4[pb_[h], 3, :],
                                                               T4[pb_[h], 0:2, :].rearrange("p a t -> p (a t)"), start=True, stop=True),
                                 [T4.k], [ps[5].k])
                            P.op("pe", lambda e, h=h: e.matmul(ps[6][:, h * 128:(h + 1) * 128], T4[pb_[h], 0, :],
                                                               T4[pb_[h], 2, :], start=True, stop=True),
                                 [T4.k], [ps[6].k])
                        P.op("dve", lambda e: e.tensor_tensor(UN[:].rearrange("p h c -> p (h c)"), ps[4][:], maskUN(), ALU.mult),
                             [ps[4].k, rwc.k], [UN.k])
                        P.op("dve", lambda e: e.tensor_tensor(MN[:].rearrange("p h c -> p (h c)"), ps[5][:], maskUN(), ALU.mult),
                             [ps[5].k, rwc.k], [MN.k])
                        P.op("dve", lambda e: e.tensor_tensor(Mq[0][:].rearrange("p h c -> p (h c)"), ps[6][:, 0:256], maskSL2(), ALU.mult),
                             [ps[6].k, rwc.k], [Mq[0].k])
                        if rw_stop == 6:
                            return
                        P.op("act", lambda e: e.copy(Uq[0][:], UN[:, :, 0:128]), [UN.k], [Uq[0].k])
                        for h in range(2):
                            P.op("dve", lambda e, h=h: e.tensor_tensor(TTt[0][:, h, :], UN[:, h, 0:128], ident[:], ALU.add),
                                 [UN.k, ident.k], [TTt[0].k])
                        cu = 0
                        for lev in range(5):
                            nu = 1 - cu
                            for h in range(2):
                                P.op("pe", lambda e, h=h, cu=cu: e.matmul(ps[4][:, h * 128:(h + 1) * 128], Uq[cu][:, h, :], Mq[cu][:, h, :],
                                                                          start=True, stop=True), [Uq[cu].k, Mq[cu].k], [ps[4].k])
                                if lev < 4:
                                    P.op("pe", lambda e, h=h, cu=cu: e.matmul(ps[5][:, h * 128:(h + 1) * 128], Mq[cu][:, h, :], Uq[cu][:, h, :],
                                                                              start=True, stop=True), [Uq[cu].k, Mq[cu].k], [ps[5].k])
                            P.op("act", lambda e, nu=nu: e.copy(Mq[nu][:].rearrange("p h c -> p (h c)"), ps[4][:, 0:256]),
                                 [ps[4].k], [Mq[nu].k])
                            if lev < 4:
                                P.op("dve", lambda e, nu=nu: e.tensor_copy(Uq[nu][:].rearrange("p h c -> p (h c)"), ps[5][:, 0:256]),
                                     [ps[5].k], [Uq[nu].k])
                            for h in range(2):
                                P.op("pe", lambda e, h=h, nu=nu, cu=cu: e.matmul(ps[6][:, h * 128:(h + 1) * 128], Mq[nu][:, h, :], TTt[cu][:, h, :],
                                                                                 start=True, stop=True), [Mq[nu].k, TTt[cu].k], [ps[6].k])
                            P.op("dve", lambda e, nu=nu, cu=cu: e.tensor_tensor(TTt[nu][:].rearrange("p h c -> p (h c)"),
                                                                                TTt[cu][:].rearrange("p h c -> p (h c)"),
                                                                                ps[6][:, 0:256], ALU.add),
                                 [TTt[cu].k, ps[6].k], [TTt[nu].k])
                            cu = nu
                        TTf = TTt[cu]
                        if cu != 0:
                            pass
                        if rw_stop == 7:
                            return
                        for h in range(2):
                            P.op("pe", lambda e, hc=hc, h=h: e.matmul(ps[4][:, h * 64:(h + 1) * 64], MN[:, h, 0:128], v_t[:, hc[h]],
                                                               start=True, stop=True), [MN.k, v_t.k], [ps[4].k])
                        if rw_stop == 70:
                            return
                        P.op("act", lambda e: e.copy(X0[:], ps[4][:, 0:128]), [ps[4].k], [X0.k])
                        for h in range(2):
                            P.op("pe", lambda e, hc=hc, h=h, TTf=TTf: e.matmul(ps[5][:, h * 64:(h + 1) * 64], TTf[:, h, :], X0[:, h * 64:(h + 1) * 64],
                                                                        start=True, stop=True), [TTf.k, X0.k], [ps[5].k])
                            P.op("pe", lambda e, hc=hc, h=h, TTf=TTf: e.matmul(ps[5][:, 128 + h * 64:128 + (h + 1) * 64], TTf[:, h, :], Ab[:, hc[h]],
                                                                        start=True, stop=True), [TTf.k, Ab.k], [ps[5].k])
                        if rw_stop == 72:
                            return
                        P.op("act", lambda e: e.copy(W0[:], ps[5][:, 0:128]), [ps[5].k], [W0.k])
                        if rw_stop == 721:
                            return
                        for h in range(2):
                            P.op("act", lambda e, hc=hc, h=h: e.copy(W0pad[:, h, h * 64:(h + 1) * 64], ps[5][:, h * 64:(h + 1) * 64]),
                                 [ps[5].k], [W0pad.k])
                            P.op("dve", lambda e, hc=hc, h=h: e.tensor_copy(Vpad[:, h, h * 64:(h + 1) * 64], v_t[:, hc[h]]),
                                 [v_t.k], [Vpad.k])
                        if rw_stop == 722:
                            return
                        for h in range(2):
                            P.op("dve", lambda e, h=h: e.tensor_copy(ApPad[:, h, h * 64:(h + 1) * 64], ps[5][:, 128 + h * 64:128 + (h + 1) * 64]),
                                 [ps[5].k], [ApPad.k])
                        if rw_stop == 73:
                            return
                        for h in range(2):
                            P.op("pe", lambda e, h=h: e.matmul(ps[6][:, 0:128], ApPad[:, h, :], UN[:, h, 128:256],
                                                               start=(h == 0), stop=(h == 1)), [ApPad.k, UN.k], [ps[6].k])
                        if rw_stop == 74:
                            return
                        P.op("dve", lambda e: e.tensor_tensor(R0[:, 0:64], ps[6][:, 0:64], T4[:, 1, 0:64], ALU.add),
                             [ps[6].k, T4.k], [R0.k])
                        P.op("dve", lambda e: e.tensor_tensor(R1[:, 64:128], ps[6][:, 64:128], T4[:, 1, 64:128], ALU.add),
                             [ps[6].k, T4.k], [R1.k])
                        if rw_stop == 8:
                            return
                        for c in range(2):
                            cp = slice(c * 64, (c + 1) * 64)
                            for h in range(2):
                                P.op("pe", lambda e, hc=hc, h=h, cp=cp, c=c: e.matmul(ps[4][:, c * 64:(c + 1) * 64], ApPad[cp, h, :], Bh[cp, hc[h]],
                                                                               start=(h == 0), stop=(h == 1)), [ApPad.k, Bh.k], [ps[4].k])
                            for h in range(2):
                                P.op("dve", lambda e, h=h, c=c: e.scalar_tensor_tensor(
                                    GTbd[c][pb_[h], h * 64:(h + 1) * 64], I2()[pb_[h], :], PCT[pb_[h], c:c + 1],
                                    ps[4][pb_[h], c * 64:(c + 1) * 64], ALU.mult, ALU.add),
                                    [rwc.k, PCT.k, ps[4].k], [GTbd[c].k])
                        Sc0 = Sbd[hp2][scur[hp2]]
                        Sc1 = Sbd[hp2][1 - scur[hp2]]
                        for (c, Sin, Sout) in ((0, Sc0, Sc1), (1, Sc1, Sc0)):
                            cp = slice(c * 64, (c + 1) * 64)
                            P.op("pe", lambda e, cp=cp, pc=pc: e.matmul(ps[5][:, 0:128], Bh[cp, pc], W0[cp, :], start=True, stop=False),
                                 [Bh.k, W0.k], [ps[5].k])
                            P.op("pe", lambda e, cp=cp, pc=pc: e.matmul(ps[5][:, 0:128], Kh[cp, pc], v_t[cp, pc], start=False, stop=False),
                                 [Kh.k, v_t.k], [ps[5].k])
                            P.op("pe", lambda e, c=c, Sin=Sin: e.matmul(ps[5][:, 0:128], GTbd[c][:], Sin[:], start=False, stop=True),
                                 [GTbd[c].k, Sin.k], [ps[5].k])
                            if c == 0:
                                for h in range(2):
                                    P.op("pe", lambda e, h=h: e.matmul(ps[6][:, 128:256], UN[:, h, 128:256], W0pad[:, h, :],
                                                                       start=(h == 0), stop=False), [UN.k, W0pad.k], [ps[6].k])
                                    P.op("pe", lambda e, h=h: e.matmul(ps[6][:, 128:256], MN[:, h, 128:256], Vpad[:, h, :],
                                                                       start=False, stop=False), [MN.k, Vpad.k], [ps[6].k])
                                P.op("pe", lambda e, Sin=Sin: e.matmul(ps[6][:, 128:256], R0[:], Sin[:], start=False, stop=False),
                                     [R0.k, Sin.k], [ps[6].k])
                            for h in range(2):
                                P.op("dve" if h == 0 else "act",
                                     (lambda e, h=h, Sout=Sout: e.tensor_copy(Sout[pb_[h], h * 64:(h + 1) * 64], ps[5][pb_[h], h * 64:(h + 1) * 64]))
                                     if h == 0 else
                                     (lambda e, h=h, Sout=Sout: e.copy(Sout[pb_[h], h * 64:(h + 1) * 64], ps[5][pb_[h], h * 64:(h + 1) * 64])),
                                     [ps[5].k], [Sout.k])
                            if c == 0:
                                P.op("pe", lambda e, Sout=Sout: e.matmul(ps[6][:, 128:256], R1[:], Sout[:], start=False, stop=True),
                                     [R1.k, Sout.k], [ps[6].k])
                                P.op("act", lambda e, pc=pc: e.copy(Y[:, pc], ps[6][:, 128:256]), [ps[6].k], [Y.k])

                    if rw_stop == 9:
                        return
                    Y3 = lambda: Y[:].rearrange("p (h j) -> p h j", h=8)
                    P.op("dve", lambda e: e.tensor_reduce(st8[:, 0:8], Y3(), AX.X, ALU.add), [Y.k], [st8.k])
                    P.op("dve", lambda e: e.tensor_scalar(st8[:, 0:8], st8[:, 0:8], 1.0 / 64, None, ALU.mult), [st8.k], [st8.k])
                    P.op("dve", lambda e: e.tensor_tensor(Y3(), Y3(), st8[:, 0:8].unsqueeze(2).to_broadcast([128, 8, 64]), ALU.subtract),
                         [Y.k, st8.k], [Y.k])
                    P.op("dve", lambda e: e.tensor_tensor(tmp[:], Y[:], Y[:], ALU.mult), [Y.k], [tmp.k])
                    P.op("dve", lambda e: e.tensor_reduce(st8[:, 8:16], tmp[:].rearrange("p (h j) -> p h j", h=8), AX.X, ALU.add),
                         [tmp.k], [st8.k])
                    P.op("dve", lambda e: e.tensor_scalar(st8[:, 8:16], st8[:, 8:16], 1.0 / 64, 64e-5, ALU.mult, ALU.add), [st8.k], [st8.k])
                    P.op("act", lambda e: e.activation(out=st8[:, 8:16], in_=st8[:, 8:16], func=AF.Sqrt), [st8.k], [st8.k])
                    P.op("dve", lambda e: e.reciprocal(st8[:, 8:16], st8[:, 8:16]), [st8.k], [st8.k])
                    P.op("dve", lambda e: e.tensor_tensor(Y3(), Y3(), st8[:, 8:16].unsqueeze(2).to_broadcast([128, 8, 64]), ALU.mult),
                         [Y.k, st8.k], [Y.k])
                    P.op("dve", lambda e: e.tensor_tensor(Y[:], Y[:], prm[:, 5, :], ALU.mult), [Y.k, prm.k], [Y.k])
                    P.op("dve", lambda e: e.tensor_tensor(Y[:], Y[:], prm[:, 6, :], ALU.add), [Y.k, prm.k], [Y.k])
                    P.op("dve", lambda e: e.tensor_tensor(tmp[:], r_t[:], km[:], ALU.mult), [r_t.k, km.k], [tmp.k])
                    P.op("dve", lambda e: e.tensor_tensor(tmp[:], tmp[:], prm[:, 4, :], ALU.mult), [tmp.k, prm.k], [tmp.k])
                    P.op("dve", lambda e: e.tensor_reduce(st8[:, 0:8], tmp[:].rearrange("p (h j) -> p h j", h=8), AX.X, ALU.add),
                         [tmp.k], [st8.k])
                    P.op("dve", lambda e: e.tensor_tensor(tmp[:].rearrange("p (h j) -> p h j", h=8),
                                                          v_t[:].rearrange("p (h j) -> p h j", h=8),
                                                          st8[:, 0:8].unsqueeze(2).to_broadcast([128, 8, 64]), ALU.mult),
                         [v_t.k, st8.k], [tmp.k])
                    P.op("dve", lambda e: e.tensor_tensor(Y[:], Y[:], tmp[:], ALU.add), [Y.k, tmp.k], [Y.k])
                    P.op("dve", lambda e: e.tensor_tensor(o_t[:], Y[:], g_t[:], ALU.mult), [Y.k, g_t.k], [o_t.k])
                    sc_i, tin = it // 8, it % 8
                    slot = sc_i * 4 + (tin % 4)
                    if tin < 4:
                        P.op("dve", lambda e, slot=slot: e.tensor_copy(o_rw[:, slot, :], o_t[:]), [o_t.k], [o_rw.k])
                    else:
                        P.op("dve", lambda e, slot=slot: e.tensor_tensor(o_t[:], o_t[:], o_rw[:, slot, :], ALU.subtract),
                             [o_t.k, o_rw.k], [o_t.k])
                        P.op("dve", lambda e, slot=slot: e.scalar_tensor_tensor(o_rw[:, slot, :], o_t[:], selt[:], o_rw[:, slot, :],
                                                                                ALU.mult, ALU.add),
                             [o_t.k, selt.k, o_rw.k], [o_rw.k])
            P.barrier()
        if do_rw:
            _phase_rw()
            P.barrier()
        if DEBUG == 2:
            dbg_d = nc.dram_tensor("dbg", [128, 4 * 512], F32, kind="ExternalOutput").ap()
            for s_ in range(min(4, rw_tiles or 4)):
                P.op("dve", lambda e, s_=s_: e.tensor_copy(stage[:, 0:512], o_rw[:, s_, :]), [o_rw.k, stage.k], [stage.k])
                P.dma("sp", lambda e, s_=s_: e.dma_start(out=dbg_d[:, s_ * 512:(s_ + 1) * 512], in_=stage[:, 0:512]), [stage.k], [DBG_TOK])
        oT_da = sb0("oT_da", [128, 4, NOWN], BF16) if not rw_stop else None
        def _phase_da():
            with contextlib.ExitStack() as st:
                def sb(name, shape, dt=F32):
                    return TT(st, nc, name, shape, dt)

                gmix = sb("gmix", [128, D])
                P.dma("sp", lambda e: e.dma_start(out=gmix[:], in_=norm_mix_g.partition_broadcast(128)), [], [gmix.k])
                maskT = sb("maskT", [128, 8, 512], BF16)
                for mh in range(4):
                    load_bf16(maskT, maskT[:, mh * 2:(mh + 1) * 2, :],
                              mask_d[:, mh * 1024:(mh + 1) * 1024].rearrange("p (j f) -> p j f", j=2))
                lv = sb("lv", [128, 256])
                lt = sb("lt", [128, 128])
                lsc = sb("lsc", [128, 4])
                P.dma("sp", lambda e: e.dma_start(out=lv[:], in_=lamv.partition_broadcast(128)), [], [lv.k])
                P.op("dve", lambda e: e.tensor_tensor(lt[:].rearrange("p (a b) -> p a b", a=2),
                                                      lv[:].rearrange("p (a c b) -> p a c b", a=2, c=2)[:, :, 0, :],
                                                      lv[:].rearrange("p (a c b) -> p a c b", a=2, c=2)[:, :, 1, :],
                                                      ALU.mult), [lv.k], [lt.k])
                P.op("dve", lambda e: e.tensor_reduce(lsc[:, 0:2], lt[:].rearrange("p (a b) -> p a b", a=2),
                                                      AX.X, ALU.add), [lt.k], [lsc.k])
                P.op("act", lambda e: e.activation(out=lsc[:, 0:2], in_=lsc[:, 0:2], func=AF.Exp), [lsc.k], [lsc.k])
                P.op("dve", lambda e: e.tensor_tensor(lsc[:, 2:3], lsc[:, 1:2], lsc[:, 0:1], ALU.subtract), [lsc.k], [lsc.k])
                P.op("dve", lambda e: e.tensor_scalar(lsc[:, 2:3], lsc[:, 2:3], -LAM_INIT, None, ALU.add), [lsc.k], [lsc.k])
                gsub = sb("gsub", [128, 1])
                P.dma("sp", lambda e: e.dma_start(out=gsub[:], in_=da_subln_g), [], [gsub.k])
                P.op("dve", lambda e: e.tensor_scalar(gsub[:], gsub[:], 1.0 - LAM_INIT, None, ALU.mult), [gsub.k], [gsub.k])

                kT = sb("kT", [128, 2, SEQ], BF16)
                Vt = sb("Vt", [128, SEQ // 128, 256], BF16)
                uT = [sb("uT0", [128, 8, 512], BF16)] * 2
                xt = [sb(f"xt{i}", [128, D]) for i in range(2)]
                jk = stage
                ssx = [sb("ssx0", [128, 2])] * 2
                ub = [sb("ub0", [128, D], BF16)] * 2
                wk = sb("wk", [128, 8, 256], BF16)
                wkp = sb("wkp", [128, 8, 256], BF16)
                wq, wqp = wk, wkp
                wv = sb("wv", [128, 8, 256], BF16)
                posi = sb("posi", [128, 512], I32)
                ang = sb("ang", [128, 512])
                kq = sb("kq", [128, 512])
                kqi = sb("kqi", [128, 512], I32)
                msk = sb("msk", [128, 512])
                tC = sb("tC", [128, 512])
                tS = sb("tS", [128, 512])
                t1 = sb("t1", [128, 512])
                t2 = sb("t2", [128, 512])
                qT = sb("qT", [128, 2, 512], BF16)
                pT = [sb(f"pT{i}", [128, 512], BF16) for i in range(3)]
                rr = [sb(f"rr{i}", [128, 512]) for i in range(2)]
                oo = sb("oo", [128, 512])
                sqb = sb("sqb", [128, 512], BF16)

                def sin_table(dst, shift, scale_col, mul):
                    P.op("dve", lambda e: e.tensor_scalar(kq[:], ang[:], shift, 1.0 / TWO_PI, ALU.add, ALU.mult),
                         [ang.k], [kq.k])
                    P.op("dve", lambda e: e.tensor_copy(kqi[:], kq[:]), [kq.k], [kqi.k])
                    P.op("dve", lambda e: e.tensor_copy(kq[:], kqi[:]), [kqi.k], [kq.k])
                    P.op("dve", lambda e: e.scalar_tensor_tensor(msk[:], kq[:], -C1, ang[:], ALU.mult, ALU.add),
                         [kq.k, ang.k], [msk.k])
                    P.op("dve", lambda e: e.scalar_tensor_tensor(msk[:], kq[:], -C2, msk[:], ALU.mult, ALU.add),
                         [kq.k, msk.k], [msk.k])
                    P.op("dve", lambda e: e.tensor_scalar(msk[:], msk[:], shift, None, ALU.add), [msk.k], [msk.k])
                    P.op("dve", lambda e: e.tensor_scalar(kq[:], msk[:], 3.141592653589793, -TWO_PI, ALU.is_gt, ALU.mult),
                         [msk.k], [kq.k])
                    P.op("dve", lambda e: e.tensor_tensor(msk[:], msk[:], kq[:], ALU.add), [msk.k, kq.k], [msk.k])
                    P.op("dve", lambda e: e.tensor_scalar(kq[:], msk[:], -3.141592653589793, TWO_PI, ALU.is_lt, ALU.mult),
                         [msk.k], [kq.k])
                    P.op("dve", lambda e: e.tensor_tensor(msk[:], msk[:], kq[:], ALU.add), [msk.k, kq.k], [msk.k])
                    P.op("act", lambda e: e.activation(out=dst[:], in_=msk[:], func=AF.Sin), [msk.k], [dst.k])
                    if scale_col is not None:
                        P.op("dve", lambda e: e.tensor_scalar(dst[:], dst[:], cvec[:, scale_col:scale_col + 1], mul,
                                                              ALU.mult, ALU.mult), [dst.k, cvec.k], [dst.k])
                    elif mul != 1.0:
                        P.op("dve", lambda e: e.tensor_scalar(dst[:], dst[:], mul, None, ALU.mult), [dst.k], [dst.k])

                def rope_tables(pos_ap, mul):
                    P.dma("sp", lambda e: e.dma_start(out=posi[:], in_=pos_ap.partition_broadcast(128)), [], [posi.k])
                    P.op("dve", lambda e: e.tensor_copy(ang[:], posi[:]), [posi.k], [ang.k])
                    P.op("dve", lambda e: e.tensor_scalar(ang[:], ang[:], cvec[:, 0:1], None, ALU.mult),
                         [ang.k, cvec.k], [ang.k])
                    sin_table(tC, 1.5707963267948966, None, mul)
                    sin_table(tS, 0.0, 1, mul)

                def load_uT(src_dram, row0, dst):
                    for tt in range(4):
                        b = tt % 2
                        rows = slice(row0 + tt * 128, row0 + (tt + 1) * 128)
                        P.dma("sp", lambda e, b=b, rows=rows: e.dma_start(out=xt[b][:], in_=src_dram[rows, :]),
                              [], [xt[b].k])
                        rmsnorm(xt[b], gmix, ub[b], jk, ssx[b], 0)
                        transpose_to(ub[b], 8, lambda tt=tt: dst[:, :, tt * 128:(tt + 1) * 128], dst.k)

                def proj_rope(w_a, w_b, h, src_uT, dst_ap_fn, dst_tok):
                    pa, pb_ = ps[0], ps[1]
                    for kc in range(8):
                        P.op("pe", lambda e, kc=kc: e.matmul(pa[:], w_a[:, kc, h * 128:(h + 1) * 128],
                                                             src_uT[:, kc, :], start=(kc == 0), stop=(kc == 7)),
                             [w_a.k, src_uT.k], [pa.k])
                    for kc in range(8):
                        P.op("pe", lambda e, kc=kc: e.matmul(pb_[:], w_b[:, kc, h * 128:(h + 1) * 128],
                                                             src_uT[:, kc, :], start=(kc == 0), stop=(kc == 7)),
                             [w_b.k, src_uT.k], [pb_.k])
                    P.op("dve", lambda e: e.tensor_tensor(t1[:], pa[:], tC[:], ALU.mult), [pa.k, tC.k], [t1.k])
                    P.op("dve", lambda e: e.tensor_tensor(t2[:], pb_[:], tS[:], ALU.mult), [pb_.k, tS.k], [t2.k])
                    P.op("dve", lambda e: e.tensor_tensor(dst_ap_fn(), t1[:], t2[:], ALU.add), [t1.k, t2.k], [dst_tok])

                for hp in range(1 if mini in (1, 2) else 2):
                    def wload(dst, col0):
                        load_w(dst, w_in[:, col0:col0 + 256], 256)
                    def wperm(wsrc, wdst):
                        P.op("pool", lambda e, wdst=wdst: e.memset(wdst[:], 0.0), [], [wdst.k])
                        P.op("dve", lambda e, wsrc=wsrc, wdst=wdst: e.tensor_copy(
                            wdst[:].rearrange("p k (b d) -> p k b d", d=64)[:, :, :, 0:8],
                            wsrc[:].rearrange("p k (b d) -> p k b d", d=64)[:, :, :, 8:16]), [wsrc.k, wdst.k], [wdst.k])
                        P.op("dve", lambda e, wsrc=wsrc, wdst=wdst: e.tensor_copy(
                            wdst[:].rearrange("p k (b d) -> p k b d", d=64)[:, :, :, 8:16],
                            wsrc[:].rearrange("p k (b d) -> p k b d", d=64)[:, :, :, 0:8]), [wsrc.k, wdst.k], [wdst.k])
                    wload(wk, 512 + hp * 256)
                    wload(wv, 1024 + hp * 256)
                    wperm(wk, wkp)
                    for (wsrc, wdst) in ():
                        P.op("pool", lambda e, wdst=wdst: e.memset(wdst[:], 0.0), [], [wdst.k])
                        P.op("dve", lambda e, wsrc=wsrc, wdst=wdst: e.tensor_copy(
                            wdst[:].rearrange("p k (b d) -> p k b d", d=64)[:, :, :, 0:8],
                            wsrc[:].rearrange("p k (b d) -> p k b d", d=64)[:, :, :, 8:16]), [wsrc.k, wdst.k], [wdst.k])
                        P.op("dve", lambda e, wsrc=wsrc, wdst=wdst: e.tensor_copy(
                            wdst[:].rearrange("p k (b d) -> p k b d", d=64)[:, :, :, 8:16],
                            wsrc[:].rearrange("p k (b d) -> p k b d", d=64)[:, :, :, 0:8]), [wsrc.k, wdst.k], [wdst.k])
                    for g in range(2 if mini else SEQ // 512):
                        u = uT[g % 2]
                        load_uT(x_full, g * 512, u)
                        if mini == 2:
                            P.op('pool', lambda e: e.memset(tC[:], 1.0), [], [tC.k])
                            P.op('pool', lambda e: e.memset(tS[:], 0.0), [], [tS.k])
                        else:
                            rope_tables(pos_full[:, g * 512:(g + 1) * 512], 1.0)
                        for h in range(2):
                            proj_rope(wk, wkp, h, u, lambda h=h, g=g: kT[:, h, g * 512:(g + 1) * 512], kT.k)
                        for tt in range(4):
                            pv = ps[2 + tt % 2]
                            for kc in range(8):
                                P.op("pe", lambda e, kc=kc, tt=tt, pv=pv, u=u: e.matmul(
                                    pv[:, 0:256], u[:, kc, tt * 128:(tt + 1) * 128], wv[:, kc, :],
                                    start=(kc == 0), stop=(kc == 7)), [u.k, wv.k], [pv.k])
                            P.op("act", lambda e, tt=tt, pv=pv, g=g: e.copy(Vt[:, g * 4 + tt, :], pv[:, 0:256]),
                                 [pv.k], [Vt.k])
                    if mini not in (1, 2):
                        wload(wq, hp * 256)
                        wperm(wq, wqp)
                    for i in range({0: 8, 1: 0, 2: 0, 3: 1}[mini]):
                        u = uT[i % 2]
                        load_uT(x_own, i * 512, u)
                        rope_tables(pos_own[:, i * 512:(i + 1) * 512], 0.125)
                        for h in range(2):
                            proj_rope(wq, wqp, h, u, lambda h=h: qT[:, h, :], qT.k)
                        nkb = 8 * i + 8
                        for h in range(2):
                            acc = [ps[2], ps[3], ps[4], ps[5]]
                            pairs = [(j, c) for j in range(nkb) for c in range(2)]

                            def qk(n):
                                j, c = pairs[n]
                                sc = ps[n % 2]
                                pr = slice(c * 64, (c + 1) * 64)
                                P.op("pe", lambda e, sc=sc, pr=pr, j=j, h=h: e.matmul(
                                    sc[:], kT[pr, h, j * 128:(j + 1) * 128], qT[pr, h, :], start=True, stop=True),
                                    [kT.k, qT.k], [sc.k])
                            qk(0)
                            for n in range(len(pairs)):
                                j, c = pairs[n]
                                sc = ps[n % 2]
                                pt = pT[n % 3]
                                P.op("act", lambda e, sc=sc, pt=pt: e.activation(out=pt[:], in_=sc[:], func=AF.Exp),
                                     [sc.k], [pt.k])
                                if n + 1 < len(pairs):
                                    qk(n + 1)
                                if j >= 8 * i:
                                    jj = j - 8 * i
                                    P.op("dve", lambda e, pt=pt, jj=jj: e.tensor_tensor(pt[:], pt[:], maskT[:, jj, :], ALU.mult),
                                         [pt.k, maskT.k], [pt.k])
                                P.op("pe", lambda e, pt=pt, j=j, h=h, c=c, nkb=nkb: e.matmul(
                                    acc[2 * c][:], Vt[:, j, h * 128:(h + 1) * 128], pt[:],
                                    start=(j == 0), stop=(j == nkb - 1)), [Vt.k, pt.k], [acc[2 * c].k])
                                P.op("pe", lambda e, pt=pt, j=j, c=c, nkb=nkb: e.matmul(
                                    acc[2 * c + 1][:], onesb[:], pt[:],
                                    start=(j == 0), stop=(j == nkb - 1)), [onesb.k, pt.k], [acc[2 * c + 1].k])
                            for c in range(2):
                                P.op("dve", lambda e, c=c: e.reciprocal(rr[c][:], acc[2 * c + 1][:]), [acc[2 * c + 1].k], [rr[c].k])
                                P.op("dve", lambda e, c=c: e.tensor_tensor(rr[c][:], rr[c][:], acc[2 * c][:], ALU.mult),
                                     [rr[c].k, acc[2 * c].k], [rr[c].k])
                            P.op("dve", lambda e: e.scalar_tensor_tensor(oo[:], rr[1][:], lsc[:, 2:3], rr[0][:], ALU.mult, ALU.add),
                                 [rr[0].k, rr[1].k, lsc.k], [oo.k])
                            P.op("act", lambda e: e.activation(out=sqb[:], in_=oo[:], func=AF.Square), [oo.k], [sqb.k])
                            P.op("pe", lambda e: e.matmul(ps[6][:], onesb[:], sqb[:], start=True, stop=True),
                                 [onesb.k, sqb.k], [ps[6].k])
                            P.op("dve", lambda e: e.tensor_scalar(rr[0][:], ps[6][:], 1.0 / 128, EPS, ALU.mult, ALU.add),
                                 [ps[6].k], [rr[0].k])
                            P.op("pool", lambda e: e.tensor_tensor(rr[0][:], rr[0][:], mhalf[:].to_broadcast([128, 512]), ALU.pow),
                                 [rr[0].k, mhalf.k], [rr[0].k])
                            P.op("dve", lambda e, h=h, i=i, hp=hp: e.scalar_tensor_tensor(
                                oT_da[:, 2 * hp + h, i * 512:(i + 1) * 512], oo[:], gsub[:], rr[0][:], ALU.mult, ALU.mult),
                                [oo.k, gsub.k, rr[0].k], [oT_da.k])

        if do_da:
            _phase_da()
        P.barrier()
        def _phase_c():
            with contextlib.ExitStack() as st:
                def sb(name, shape, dt=F32):
                    return TT(st, nc, name, shape, dt)

                gple = sb("gple", [128, D])
                gfin = sb("gfin", [128, D])
                P.dma("sp", lambda e: e.dma_start(out=gple[:], in_=norm_ple_g.partition_broadcast(128)), [], [gple.k])
                P.dma("sp", lambda e: e.dma_start(out=gfin[:], in_=norm_final_g.partition_broadcast(128)), [], [gfin.k])
                wo = sb("wo", [128, 8, D], BF16)
                wg = sb("wg", [128, 8, D], BF16)
                wp = sb("wp", [128, 2, D], BF16)
                load_w(wo, w_out, D)
                load_w(wg, ple_gate_w, D)
                load_w(wp, ple_proj_w, D)
                if do_peer:
                    gffn = sb("gffn", [128, D])
                    P.dma("sp", lambda e: e.dma_start(out=gffn[:], in_=norm_ffn_g.partition_broadcast(128)), [], [gffn.k])
                    wq = sb("wq", [128, 8, 2048], BF16)
                    load_w(wq, peer_w_q, 2048)
                    iota16 = sb("iota16", [128, 16])
                    P.dma("sp", lambda e: e.dma_start(out=iota16[:], in_=iota_d), [], [iota16.k])
                    keysT = sb("keysT", [128, 16, 128], BF16)
                    kst = stage
                    for g4 in range(4):
                        P.dma("sp", lambda e, g4=g4: e.dma_start(
                            out=kst[:, 0:512].rearrange("p (a d) -> p a d", a=4), in_=peer_keys[g4 * 4:(g4 + 1) * 4].rearrange("h n d -> n h d")), [], [kst.k])
                        for q in range(4):
                            P.op("pe", lambda e, q=q: e.transpose(ps[0][:, q * 128:(q + 1) * 128], kst[:, q * 128:(q + 1) * 128], ident[:]),
                                 [kst.k, ident.k], [ps[0].k])
                        P.op("act", lambda e, g4=g4: e.copy(keysT[:, g4 * 4:(g4 + 1) * 4, :].rearrange("p a n -> p (a n)"), ps[0][:]),
                             [ps[0].k], [keysT.k])
                    qTg = sb("qTg", [128, 4, 128], BF16)
                    s4 = sb("s4", [128, 4, 128])
                    s4b = sb("s4b", [128, 128])
                    tv = sb("tv", [128, 16, 16])
                    ti = sb("ti", [128, 16, 16], U32)
                    tif = sb("tif", [128, 16, 16])
                    cand = sb("cand", [128, 256])
                    cand2 = sb("cand2", [128, 256])
                    best = sb("best", [128, 8, 16])
                    pos = sb("pos", [128, 8, 16], U32)
                    pij = sb("pij", [128, 2, 128], I32)
                    pijf = sb("pijf", [128, 2, 128])
                    oh = sb("oh", [128, 8, 16, 16], BF16)
                    e01 = sb("e01", [128, 2, 128])
                    idxf = sb("idxf", [128, 128])
                    idxi = sb("idxi", [128, 128], I32)
                    gsm = sb("gsm", [128, 8, 16])
                    gss = sb("gss", [128, 8])
                    hid = sb("hid", [128, 128])
                    wgt = sb("wgt", [128, 128])
                    NG = 4
                    UV = [sb(f"UV{i}", [128, 2, D], BF16) for i in range(NG)]
                    dgs = [sb(f"dg{i}", [128, 128], BF16) for i in range(2)]

                NB = 1
                hbuf = [sb(f"h{i}", [128, D]) for i in range(NB)]
                junk = sb("junk", [128, D])
                ssb = [sb(f"ss{i}", [128, 4]) for i in range(NB)]
                nb = [sb(f"n{i}", [128, D], BF16) for i in range(NB)]
                nT = [sb(f"nT{i}", [128, 8, 128], BF16) for i in range(NB)]
                orT = [sb(f"orT{i}", [128, 4, 128], BF16) for i in range(NB)]
                pin = [sb(f"pin{i}", [128, 256]) for i in range(NB)]
                pb = [sb(f"pb{i}", [128, 256], BF16) for i in range(NB)]
                pT2 = [sb(f"pT2{i}", [128, 2, 128], BF16) for i in range(NB)]
                gate = [sb(f"gate{i}", [128, D]) for i in range(NB)]
                h3 = gate
                ob = hbuf
                prod, xnb, xnT, xn = junk, nb[0], nT[0], gate[0]

                NT = c_tiles if c_tiles else {0: NOWN // 128, 1: 0, 2: 0, 3: 2}[mini]
                for it in range(NT):
                    b = it % NB
                    rows = slice(it * 128, (it + 1) * 128)
                    h = hbuf[b]
                    P.dma("sp", lambda e, h=h, rows=rows: e.dma_start(out=h[:], in_=x_own[rows, :]), [], [h.k])
                    P.dma("sp", lambda e, b=b, rows=rows: e.dma_start(out=pin[b][:], in_=p_own[rows, :]), [], [pin[b].k])
                    if do_da or do_rw:
                        if do_rw:
                            for c in range(4):
                                P.op("pe", lambda e, c=c, it=it: e.transpose(pst[:, c * 128:(c + 1) * 128],
                                                                             o_rw[:, it, c * 128:(c + 1) * 128], identb[:]),
                                     [o_rw.k, identb.k], [pst.k])
                            P.op("act", lambda e, b=b: e.copy(orT[b][:], pst[:, 0:512].rearrange("p (k t) -> p k t", k=4)),
                                 [pst.k], [orT[b].k])
                        for half in range(2):
                            cs = slice(half * 512, (half + 1) * 512)
                            pg = ps[4 + half]
                            kcs = ([0, 1, 2, 3] if do_da else []) + ([4, 5, 6, 7] if do_rw else [])
                            for n_, kc in enumerate(kcs):
                                if kc < 4:
                                    lhs = (lambda kc=kc, it=it: oT_da[:, kc, it * 128:(it + 1) * 128])
                                    rk = oT_da.k
                                else:
                                    lhs = (lambda kc=kc, b=b: orT[b][:, kc - 4, :])
                                    rk = orT[b].k
                                P.op("pe", lambda e, lhs=lhs, kc=kc, cs=cs, pg=pg, n_=n_, kcs=kcs: e.matmul(
                                    pg[:], lhs(), wo[:, kc, cs], start=(n_ == 0), stop=(n_ == len(kcs) - 1)),
                                    [rk, wo.k], [pg.k])
                            P.op("dve", lambda e, h=h, cs=cs, pg=pg: e.tensor_tensor(h[:, cs], h[:, cs], pg[:], ALU.add),
                                 [h.k, pg.k], [h.k])
                    if do_peer:
                        rmsnorm(h, gffn, xn, junk, ssb[b], 2)
                        P.op("act", lambda e: e.copy(xnb[:], xn[:]), [xn.k], [xnb.k])
                        transpose_to(xnb, 8, lambda: xnT[:], xnT.k)
                        for g4 in range(4):
                            for q in range(4):
                                hp_ = g4 * 4 + q
                                for kc in range(8):
                                    P.op("pe", lambda e, q=q, kc=kc, hp_=hp_: e.matmul(
                                        ps[0][:, q * 128:(q + 1) * 128], wq[:, kc, hp_ * 128:(hp_ + 1) * 128], xnT[:, kc, :],
                                        start=(kc == 0), stop=(kc == 7)), [wq.k, xnT.k], [ps[0].k])
                            P.op("act", lambda e: e.copy(qTg[:].rearrange("p a t -> p (a t)"), ps[0][:]), [ps[0].k], [qTg.k])
                            for q in range(4):
                                hp_ = g4 * 4 + q
                                P.op("pe", lambda e, q=q, hp_=hp_: e.matmul(ps[1][:, q * 128:(q + 1) * 128], qTg[:, q, :], keysT[:, hp_, :],
                                                                          start=True, stop=True), [qTg.k, keysT.k], [ps[1].k])
                            P.op("act", lambda e: e.copy(s4[:].rearrange("p a n -> p (a n)"), ps[1][:]), [ps[1].k], [s4.k])
                            for q in range(4):
                                hp_ = g4 * 4 + q
                                P.op("dve", lambda e, q=q, hp_=hp_: e.max(out=tv[:, hp_, 0:8], in_=s4[:, q, :]), [s4.k], [tv.k])
                                P.op("dve", lambda e, q=q, hp_=hp_: e.max_index(out=ti[:, hp_, 0:8], in_max=tv[:, hp_, 0:8], in_values=s4[:, q, :]),
                                     [s4.k, tv.k], [ti.k])
                                P.op("dve", lambda e, q=q, hp_=hp_: e.match_replace(out=s4b[:], in_to_replace=tv[:, hp_, 0:8], in_values=s4[:, q, :],
                                                                                    imm_value=-1e30), [s4.k, tv.k], [s4b.k])
                                P.op("dve", lambda e, hp_=hp_: e.max(out=tv[:, hp_, 8:16], in_=s4b[:]), [s4b.k], [tv.k])
                                P.op("dve", lambda e, hp_=hp_: e.max_index(out=ti[:, hp_, 8:16], in_max=tv[:, hp_, 8:16], in_values=s4b[:]),
                                     [s4b.k, tv.k], [ti.k])
                        P.op("dve", lambda e: e.tensor_copy(tif[:], ti[:]), [ti.k], [tif.k])
                        for hh_ in range(8):
                            c3 = lambda: cand[:].rearrange("p (i j) -> p i j", i=16)
                            P.op("dve", lambda e, hh_=hh_, c3=c3: e.tensor_tensor(
                                c3(), tv[:, 2 * hh_, :].unsqueeze(2).to_broadcast([128, 16, 16]),
                                tv[:, 2 * hh_ + 1, :].unsqueeze(1).to_broadcast([128, 16, 16]), ALU.add), [tv.k], [cand.k])
                            P.op("dve", lambda e, hh_=hh_: e.max(out=best[:, hh_, 0:8], in_=cand[:]), [cand.k], [best.k])
                            P.op("dve", lambda e, hh_=hh_: e.max_index(out=pos[:, hh_, 0:8], in_max=best[:, hh_, 0:8], in_values=cand[:]),
                                 [cand.k, best.k], [pos.k])
                            P.op("dve", lambda e, hh_=hh_: e.match_replace(out=cand2[:], in_to_replace=best[:, hh_, 0:8], in_values=cand[:],
                                                                           imm_value=-1e30), [cand.k, best.k], [cand2.k])
                            P.op("dve", lambda e, hh_=hh_: e.max(out=best[:, hh_, 8:16], in_=cand2[:]), [cand2.k], [best.k])
                            P.op("dve", lambda e, hh_=hh_: e.max_index(out=pos[:, hh_, 8:16], in_max=best[:, hh_, 8:16], in_values=cand2[:]),
                                 [cand2.k, best.k], [pos.k])
                        posf = lambda: pos[:].rearrange("p h k -> p (h k)")
                        P.op("dve", lambda e: e.tensor_copy(idxf[:], posf()), [pos.k], [idxf.k])
                        P.op("dve", lambda e: e.tensor_scalar(pijf[:, 1, :], idxf[:], 0.0625, None, ALU.mult), [idxf.k], [pijf.k])
                        P.op("dve", lambda e: e.tensor_copy(pij[:, 0, :], pijf[:, 1, :]), [pijf.k], [pij.k])
                        P.op("dve", lambda e: e.tensor_copy(pijf[:, 0, :], pij[:, 0, :]), [pij.k], [pijf.k])
                        P.op("dve", lambda e: e.tensor_scalar(pijf[:, 1, :], pijf[:, 0, :], 16.0, None, ALU.mult), [pijf.k], [pijf.k])
                        P.op("dve", lambda e: e.tensor_tensor(pijf[:, 1, :], pijf[:, 1, :], idxf[:], ALU.is_gt), [pijf.k, idxf.k], [pijf.k])
                        P.op("dve", lambda e: e.tensor_tensor(pijf[:, 0, :], pijf[:, 0, :], pijf[:, 1, :], ALU.subtract), [pijf.k], [pijf.k])
                        P.op("dve", lambda e: e.scalar_tensor_tensor(pijf[:, 1, :], pijf[:, 0, :], -16.0, idxf[:], ALU.mult, ALU.add),
                             [pijf.k, idxf.k], [pijf.k])
                        tif4 = lambda: tif[:].rearrange("p (h two) k -> p h two k", two=2)
                        for pp_ in range(2):
                            P.op("dve", lambda e, pp_=pp_: e.tensor_tensor(
                                oh[:], pijf[:, pp_, :].rearrange("p (h k) -> p h k", h=8).unsqueeze(3).to_broadcast([128, 8, 16, 16]),
                                iota16[:].unsqueeze(1).unsqueeze(1).to_broadcast([128, 8, 16, 16]), ALU.is_equal),
                                [pijf.k, iota16.k], [oh.k])
                            P.op("dve", lambda e, pp_=pp_: e.tensor_tensor(
                                oh[:], oh[:], tif4()[:, :, pp_, :].unsqueeze(2).to_broadcast([128, 8, 16, 16]), ALU.mult),
                                [oh.k, tif.k], [oh.k])
                            P.op("dve", lambda e, pp_=pp_: e.tensor_reduce(e01[:, pp_, :], oh[:].rearrange("p h k i -> p (h k) i"), AX.X, ALU.add),
                                 [oh.k], [e01.k])
                        P.op("dve", lambda e: e.scalar_tensor_tensor(idxf[:], e01[:, 0, :], 128.0, e01[:, 1, :], ALU.mult, ALU.add),
                             [e01.k], [idxf.k])
                        P.op("dve", lambda e: e.tensor_copy(idxi[:], idxf[:]), [idxf.k], [idxi.k])
                        P.op("dve", lambda e: e.tensor_tensor(gsm[:], best[:], best[:, :, 0:1].to_broadcast([128, 8, 16]), ALU.subtract),
                             [best.k], [gsm.k])
                        P.op("act", lambda e: e.activation(out=gsm[:], in_=gsm[:], func=AF.Exp), [gsm.k], [gsm.k])
                        P.op("dve", lambda e: e.tensor_reduce(gss[:], gsm[:], AX.X, ALU.add), [gsm.k], [gss.k])
                        P.op("dve", lambda e: e.reciprocal(gss[:], gss[:]), [gss.k], [gss.k])
                        P.op("dve", lambda e: e.tensor_tensor(gsm[:], gsm[:], gss[:].unsqueeze(2).to_broadcast([128, 8, 16]), ALU.mult),
                             [gsm.k, gss.k], [gsm.k])
                        NS = peer_slots
                        gflat = lambda: gsm[:].rearrange("p h k -> p (h k)")
                        for s_ in range(NS + 1):
                            if s_ < NS:
                                uv = UV[s_ % NG]
                                P.dma("pool", lambda e, s_=s_, uv=uv: e.indirect_dma_start(
                                    out=uv[:].rearrange("p a d -> p (a d)"), out_offset=None, in_=uvb,
                                    in_offset=bass.IndirectOffsetOnAxis(ap=idxi[:, s_:s_ + 1], axis=0)), [idxi.k, uvb_tok], [uv.k])
                                P.op("dve", lambda e, s_=s_, uv=uv: e.scalar_tensor_tensor(
                                    prod[:], uv[:, 0, :], 1.0, xn[:], ALU.mult, ALU.mult, accum_out=hid[:, s_:s_ + 1]),
                                    [uv.k, xn.k], [prod.k, hid.k])
                            if s_ >= 1:
                                t_ = s_ - 1
                                uvp = UV[t_ % NG]
                                P.op("act", lambda e, t_=t_: e.activation(out=wgt[:, t_:t_ + 1], in_=hid[:, t_:t_ + 1], func=AF.Gelu),
                                     [hid.k], [wgt.k])
                                P.op("dve", lambda e, t_=t_: e.tensor_tensor(wgt[:, t_:t_ + 1], wgt[:, t_:t_ + 1], gflat()[:, t_:t_ + 1], ALU.mult),
                                     [wgt.k, gsm.k], [wgt.k])
                                dg = dgs[t_ % 2]
                                P.op("act", lambda e, t_=t_, dg=dg: e.activation(out=dg[:], in_=identb[:], func=AF.Copy, scale=wgt[:, t_:t_ + 1]),
                                     [identb.k, wgt.k], [dg.k])
                                for hf in range(2):
                                    P.op("pe", lambda e, t_=t_, dg=dg, uvp=uvp, hf=hf, NS=NS: e.matmul(
                                        ps[2 + hf][:], dg[:], uvp[:, 1, hf * 512:(hf + 1) * 512], start=(t_ == 0), stop=(t_ == NS - 1)),
                                        [dg.k, uvp.k], [ps[2 + hf].k])
                        for hf in range(2):
                            P.op("dve", lambda e, hf=hf, h=h: e.tensor_tensor(h[:, hf * 512:(hf + 1) * 512], h[:, hf * 512:(hf + 1) * 512], ps[2 + hf][:], ALU.add),
                                 [h.k, ps[2 + hf].k], [h.k])
                    rmsnorm(h, gple, nb[b], junk, ssb[b], 0)
                    transpose_to(nb[b], 8, lambda b=b: nT[b][:], nT[b].k)
                    P.op("dve", lambda e, b=b: e.tensor_copy(pb[b][:], pin[b][:]), [pin[b].k], [pb[b].k])
                    transpose_to(pb[b], 2, lambda b=b: pT2[b][:], pT2[b].k)
                    for half in range(2):
                        cs = slice(half * 512, (half + 1) * 512)
                        pg = ps[half]
                        for kc in range(8):
                            P.op("pe", lambda e, b=b, kc=kc, cs=cs, pg=pg: e.matmul(
                                pg[:], nT[b][:, kc, :], wg[:, kc, cs], start=(kc == 0), stop=(kc == 7)),
                                [nT[b].k, wg.k], [pg.k])
                        P.op("act", lambda e, b=b, cs=cs, pg=pg: e.activation(out=gate[b][:, cs], in_=pg[:],
                                                                              func=AF.Sigmoid),
                             [pg.k], [gate[b].k])
                        pp = ps[2 + half]
                        for kc in range(2):
                            P.op("pe", lambda e, b=b, kc=kc, cs=cs, pp=pp: e.matmul(
                                pp[:], pT2[b][:, kc, :], wp[:, kc, cs], start=(kc == 0), stop=(kc == 1)),
                                [pT2[b].k, wp.k], [pp.k])
                        P.op("dve", lambda e, b=b, cs=cs, pp=pp: e.tensor_tensor(gate[b][:, cs], gate[b][:, cs], pp[:],
                                                                                 ALU.mult),
                             [gate[b].k, pp.k], [gate[b].k])
                    P.op("dve", lambda e, b=b, h=h: e.tensor_tensor(h3[b][:], h[:], gate[b][:], ALU.add),
                         [h.k, gate[b].k], [h3[b].k])
                    rmsnorm(h3[b], gfin, ob[b], junk, ssb[b], 1)
                    P.dma("sp", lambda e, b=b, rows=rows: e.dma_start(out=out_d[rows, :], in_=ob[b][:]),
                          [ob[b].k], [out_tok])
        if not rw_stop:
            _phase_c()
        P.finish([out_tok, DBG_TOK])
        P.emit()
    return nc


_NC_CACHE = {}


def _masks(hh):
    p = np.arange(128)[:, None, None]
    jj = np.arange(8)[None, :, None]
    f = np.arange(512)[None, None, :]
    return ((128 * jj + p) <= (512 * hh + f)).astype(np.float32).reshape(128, 8 * 512)


def _rw_consts():
    s = np.arange(128)[:, None]
    t = np.arange(128)[None, :]
    same = (s // 64) == (t // 64)
    su = (same & (s < t)).astype(np.float32)
    ui = (same & (s <= t)).astype(np.float32)
    sl = (same & (s > t)).astype(np.float32)
    c = np.zeros((128, 1280), np.float32)
    c[:, 0:128] = su; c[:, 128:256] = ui; c[:, 256:384] = su; c[:, 384:512] = ui
    c[:, 512:640] = sl; c[:, 640:768] = sl
    c[:, 768:896] = ui
    c[:, 896:1024] = same.astype(np.float32)
    c[:, 1024:1088] = (np.arange(128)[:, None] % 64 == np.arange(64)[None, :]).astype(np.float32)
    return c


def kernel(**inputs):
    f32 = lambda a: np.ascontiguousarray(np.asarray(a, dtype=np.float32))
    x = f32(inputs["x"])
    p = f32(inputs["p"])[0]
    pos = np.ascontiguousarray(np.asarray(inputs["positions"], dtype=np.int32))
    B = x.shape[0]
    key = "nc"
    if key not in _NC_CACHE:
        _NC_CACHE[key] = build(**_BUILD_FLAGS)
    nc = _NC_CACHE[key]
    ident = np.eye(128, dtype=np.float32)
    cvec = np.zeros((128, 4), np.float32)
    for q in range(128):
        d = q % 64
        cvec[q, 0] = INVF[d % 8] if d < 16 else 0.0
        cvec[q, 1] = -1.0 if d < 8 else 1.0
    lamv = np.concatenate([f32(inputs["lam_q1"])[0], f32(inputs["lam_k1"])[0],
                           f32(inputs["lam_q2"])[0], f32(inputs["lam_k2"])[0]]).reshape(1, 256)
    shared = {
        "ident": ident, "cvec": cvec,
        "norm_mix_g": f32(inputs["norm_mix_g"]).reshape(1, D),
        "w_in": f32(inputs["w_in"][0]),
        "lamv": lamv,
        "da_subln_g": f32(inputs["da_subln_g"]).reshape(128, 1),
        "w_out": f32(inputs["w_out"][0]),
        "norm_ple_g": f32(inputs["norm_ple_g"]).reshape(1, D),
        "norm_final_g": f32(inputs["norm_final_g"]).reshape(1, D),
        "ple_gate_w": f32(inputs["ple_gate_w"][0]),
        "ple_proj_w": f32(inputs["ple_proj_w"][0]),
    }
    masks = [_masks(0), _masks(1)]
    for nm in ("rw_mu", "rw_w0", "rw_a0", "rw_k_k", "rw_k_a", "rw_ln_g", "rw_ln_b"):
        shared[nm] = f32(inputs[nm]).reshape(1, -1)
    shared["rw_r_k"] = f32(inputs["rw_r_k"]).reshape(1, 512)
    shared["rw_w_up"] = f32(inputs["rw_w_up"][0])
    shared["rw_a_up"] = f32(inputs["rw_a_up"][0])
    shared["rw_g_up"] = f32(inputs["rw_g_up"][0])
    shared["rwc"] = _rw_consts()
    shared["norm_ffn_g"] = f32(inputs["norm_ffn_g"]).reshape(1, D)
    shared["peer_w_q"] = f32(inputs["peer_w_q"][0])
    shared["peer_sub_keys"] = f32(inputs["peer_sub_keys"][0]).reshape(16, 128, 128)
    shared["peer_u"] = f32(inputs["peer_u"][0])
    shared["peer_v"] = f32(inputs["peer_v"][0])
    shared["iota16"] = np.tile(np.arange(16, dtype=np.float32)[None, :], (128, 1))
    in_maps = []
    for c in range(8):
        b, hh = c // 2, c % 2
        m = dict(shared)
        m["x_full"] = x[b]
        m["pos_full"] = pos[b].reshape(1, SEQ)
        m["x_own"] = np.ascontiguousarray(x[b].reshape(8, 2, 512, D)[:, hh].reshape(NOWN, D))
        m["pos_own"] = np.ascontiguousarray(pos[b].reshape(8, 2, 512)[:, hh].reshape(1, NOWN))
        m["p_own"] = np.ascontiguousarray(p[b].reshape(8, 2, 512, 256)[:, hh].reshape(NOWN, 256))
        m["maskT"] = masks[hh]
        m["sel"] = np.full((128, 1), float(hh), np.float32)
        in_maps.append(m)
    res = run_bass_kernel_spmd(nc, in_maps, core_ids=list(range(8)))
    out = np.empty((B, SEQ, D), dtype=np.float32)
    for c in range(8):
        b, hh = c // 2, c % 2
        out[b].reshape(8, 2, 512, D)[:, hh] = np.asarray(res.results[c]["out"]).reshape(8, 512, D)
    return out


_BUILD_FLAGS = dict(do_da=True, do_rw=True, do_peer=True)
```

```python
import contextlib
import numpy as np
import concourse.bass as bass
import concourse.mybir as mybir
from concourse.bass_utils import run_bass_kernel_spmd

F32 = mybir.dt.float32
BF16 = mybir.dt.bfloat16
I32 = mybir.dt.int32
U32 = mybir.dt.uint32
AF = mybir.ActivationFunctionType
ALU = mybir.AluOpType
AX = mybir.AxisListType

D = 1024
SEQ = 8192
NOWN = 4096
EPS = 1e-6
ENGS = ("pe", "act", "dve", "pool", "sp")
NDSEM = 8


class Tok:
    __slots__ = ("w", "r", "psum")

    def __init__(self):
        self.w = None
        self.r = []
        self.psum = False


class Prog:
    def __init__(self, nc):
        self.nc = nc
        self.ops = {e: [] for e in ENGS}
        self.ndma = {e: 0 for e in ENGS}
        self.barrier_deps = set()

    def barrier(self):
        deps = set()
        for e in ENGS:
            n = len(self.ops[e])
            for i in range(n - 1, -1, -1):
                if self.ops[e][i]["me"][0] == "eng":
                    deps.add(("eng", e, i))
                    break
            for j in range(max(0, self.ndma[e] - NDSEM), self.ndma[e]):
                deps.add(("dma", e, j))
        self.barrier_deps = deps

    def _add(self, eng, fn, reads, writes, dma):
        deps = set()
        for t in reads:
            if t.w is not None:
                deps.add(t.w)
            if t.psum:
                for d in t.r:
                    if d[1] != eng:
                        deps.add(d)
        for t in writes:
            if t.w is not None:
                deps.add(t.w)
            for d in t.r:
                deps.add(d)
        deps |= self.barrier_deps
        idx = len(self.ops[eng])
        if dma:
            j = self.ndma[eng]
            self.ndma[eng] += 1
            me = ("dma", eng, j)
            if j >= NDSEM:
                deps.add(("dma", eng, j - NDSEM))
        else:
            me = ("eng", eng, idx)
        if eng == "pe":
            deps = {d for d in deps if not (d[0] == "eng" and d[1] == "pe")}
        deps.discard(me)
        self.ops[eng].append(dict(fn=fn, deps=deps, me=me))
        for t in reads:
            t.r.append(me)
        for t in writes:
            t.w = me
            t.r = []
        return me

    def op(self, eng, fn, reads=(), writes=()):
        return self._add(eng, fn, list(reads), list(writes), False)

    def dma(self, eng, fn, reads=(), writes=()):
        return self._add(eng, fn, list(reads), list(writes), True)

    def finish(self, toks):
        deps = set()
        for t in toks:
            if t.w is not None:
                deps.add(t.w)
        self.ops["sp"].append(dict(fn=lambda e: e.nop(), deps=deps,
                                   me=("eng", "sp", len(self.ops["sp"]))))

    def emit(self):
        nc = self.nc
        needed = {e: set() for e in ENGS}
        for e in ENGS:
            for o in self.ops[e]:
                for d in o["deps"]:
                    if d[0] == "eng":
                        needed[d[1]].add(d[2])
        sigcount = {}
        for e in ENGS:
            c = 0
            for i, o in enumerate(self.ops[e]):
                if i in needed[e] and o["me"][0] == "eng":
                    c += 1
                    sigcount[(e, i)] = c
        with contextlib.ExitStack() as st:
            esem = {e: st.enter_context(nc.semaphore(f"s_{e}")) for e in ENGS}
            dsem = {e: [st.enter_context(nc.semaphore(f"d_{e}{k}")) for k in range(NDSEM)]
                    for e in ("sp", "act", "pool")}
            block = st.enter_context(nc.Block())

            def run(eng_name, eng):
                waited = {}
                for i, o in enumerate(self.ops[eng_name]):
                    for d in sorted(o["deps"]):
                        if d[0] == "eng":
                            sem = esem[d[1]]
                            val = sigcount[(d[1], d[2])]
                            key = ("e", d[1])
                        else:
                            sem = dsem[d[1]][d[2] % NDSEM]
                            val = 16 * (d[2] // NDSEM + 1)
                            key = ("d", d[1], d[2] % NDSEM)
                        if waited.get(key, 0) >= val:
                            continue
                        eng.wait_ge(sem, val)
                        waited[key] = val
                    ins = o["fn"](eng)
                    me = o["me"]
                    if me[0] == "dma":
                        ins.then_inc(dsem[me[1]][me[2] % NDSEM], 16)
                    elif (eng_name, i) in sigcount:
                        ins.then_inc(esem[eng_name], 1)

            @block.tensor
            def _(e):
                run("pe", e)

            @block.scalar
            def _(e):
                run("act", e)

            @block.vector
            def _(e):
                run("dve", e)

            @block.gpsimd
            def _(e):
                run("pool", e)

            @block.sync
            def _(e):
                run("sp", e)


class TT:
    _used = {}

    def __init__(self, st, nc, name, shape, dtype, psum=False):
        k_ = (id(nc), name)
        n_ = TT._used.get(k_, 0)
        TT._used[k_] = n_ + 1
        if n_:
            name = f"{name}_v{n_}"
        if psum:
            self.t = st.enter_context(nc.psum_tensor("P_" + name, shape, dtype))
        else:
            self.t = st.enter_context(nc.sbuf_tensor("S_" + name, shape, dtype))
        self.k = Tok()
        self.k.psum = psum

    def __getitem__(self, idx):
        return self.t[idx]


INVF = [float(500000.0 ** (-(i * 2.0) / 16.0)) for i in range(8)]
TWO_PI = 6.283185307179586
C1 = 6.28125
C2 = TWO_PI - C1
LAM_INIT = 0.2
DEBUG = False


class _Stop(Exception):
    pass

DBG_DONE = []
DBG_TOK = Tok()


def build(do_da=True, do_rw=True, do_peer=True, mini=0, rw_tiles=0, rw_stop=0, c_tiles=0, peer_slots=128, cvt_blocks=128):
    nc = bass.Bass("TRN2", target_bir_lowering=False)
    P = Prog(nc)

    def din(name, shape, dt=F32):
        return nc.dram_tensor(name, list(shape), dt, kind="ExternalInput").ap()

    x_full = din("x_full", [SEQ, D])
    pos_full = din("pos_full", [1, SEQ], I32)
    x_own = din("x_own", [NOWN, D])
    pos_own = din("pos_own", [1, NOWN], I32)
    p_own = din("p_own", [NOWN, 256])
    ident_d = din("ident", [128, 128])
    cvec_d = din("cvec", [128, 4])
    mask_d = din("maskT", [128, 8 * 512])
    norm_mix_g = din("norm_mix_g", [1, D])
    w_in = din("w_in", [D, 3328])
    lamv = din("lamv", [1, 256])
    da_subln_g = din("da_subln_g", [128, 1])
    w_out = din("w_out", [D, D])
    norm_ple_g = din("norm_ple_g", [1, D])
    norm_final_g = din("norm_final_g", [1, D])
    ple_gate_w = din("ple_gate_w", [D, D])
    ple_proj_w = din("ple_proj_w", [256, D])
    norm_ffn_g = din("norm_ffn_g", [1, D])
    peer_w_q = din("peer_w_q", [D, 2048])
    peer_keys = din("peer_sub_keys", [16, 128, 128])
    peer_u = din("peer_u", [16384, D])
    peer_v = din("peer_v", [16384, D])
    iota_d = din("iota16", [128, 16])
    rwc_d = din("rwc", [128, 1280])
    sel_d = din("sel", [128, 1])
    rw_mu = din("rw_mu", [1, 1792])
    rw_w0 = din("rw_w0", [1, 512])
    rw_w_up = din("rw_w_up", [64, 512])
    rw_a0 = din("rw_a0", [1, 512])
    rw_a_up = din("rw_a_up", [64, 512])
    rw_g_up = din("rw_g_up", [128, 512])
    rw_k_k = din("rw_k_k", [1, 512])
    rw_k_a = din("rw_k_a", [1, 512])
    rw_r_k = din("rw_r_k", [1, 512])
    rw_ln_g = din("rw_ln_g", [1, 512])
    rw_ln_b = din("rw_ln_b", [1, 512])
    out_d = nc.dram_tensor("out", [NOWN, D], F32, kind="ExternalOutput").ap()
    out_tok = Tok()

    with contextlib.ExitStack() as st0:
        def sb0(name, shape, dt=F32):
            return TT(st0, nc, name, shape, dt)

        ident = sb0("ident", [128, 128])
        identb = sb0("identb", [128, 128], BF16)
        onesb = sb0("onesb", [128, 128], BF16)
        cvec = sb0("cvec", [128, 4])
        mhalf = sb0("mhalf", [128, 1])
        P.dma("sp", lambda e: e.dma_start(out=ident[:], in_=ident_d), [], [ident.k])
        P.dma("sp", lambda e: e.dma_start(out=cvec[:], in_=cvec_d), [], [cvec.k])
        P.op("dve", lambda e: e.tensor_copy(identb[:], ident[:]), [ident.k], [identb.k])
        P.op("pool", lambda e: e.memset(onesb[:], 1.0), [], [onesb.k])
        P.op("pool", lambda e: e.memset(mhalf[:], -0.5), [], [mhalf.k])

        stage = sb0("stage", [128, 1024])

        phase_reads = []

        def load_bf16(dst, dst_view, src_view, eng="dve"):
            a, b_ = src_view.shape[1], src_view.shape[2]
            sv = stage[:, 0:a * b_].rearrange("p (a b) -> p a b", a=a)
            P.dma("sp", lambda e: e.dma_start(out=sv, in_=src_view), [], [stage.k])
            P.op(eng, lambda e: e.tensor_copy(dst_view, sv), [stage.k] + phase_reads, [dst.k])

        def load_w(dst, src2d, ncols):
            kch = src2d.shape[0] // 128
            step = 1024 // kch
            for c0 in range(0, ncols, step):
                load_bf16(dst, dst[:, :, c0:c0 + step],
                          src2d[:, c0:c0 + step].rearrange("(k p) n -> p k n", p=128))

        ps = [TT(st0, nc, f"ps{i}", [128, 512], F32, psum=True) for i in range(7)]
        pst = TT(st0, nc, "pst", [128, 1024], BF16, psum=True)

        o_rw = sb0("o_rw", [128, NOWN // 128, 512], BF16)

        def rmsnorm(src, gtile, dst, jk, ss, col, d=D):
            P.op("dve", lambda e: e.scalar_tensor_tensor(jk[:], src[:], 1.0, src[:], ALU.mult, ALU.mult,
                                                         accum_out=ss[:, col:col + 1]), [src.k], [jk.k, ss.k])
            P.op("dve", lambda e: e.tensor_scalar(ss[:, col:col + 1], ss[:, col:col + 1], 1.0 / d, EPS,
                                                  ALU.mult, ALU.add), [ss.k], [ss.k])
            P.op("pool", lambda e: e.tensor_tensor(ss[:, col:col + 1], ss[:, col:col + 1], mhalf[:], ALU.pow),
                 [ss.k, mhalf.k], [ss.k])
            P.op("dve", lambda e: e.scalar_tensor_tensor(dst[:], src[:], ss[:, col:col + 1], gtile[:],
                                                         ALU.mult, ALU.mult),
                 [src.k, ss.k, gtile.k], [dst.k])

        def transpose_to(src, nchunks, dst_ap_fn, dst_tok, extra_reads=()):
            for c in range(nchunks):
                P.op("pe", lambda e, c=c: e.transpose(pst[:, c * 128:(c + 1) * 128],
                                                      src[:, c * 128:(c + 1) * 128], identb[:]),
                     [src.k, identb.k], [pst.k])
            P.op("act", lambda e: e.copy(dst_ap_fn(), pst[:, 0:nchunks * 128].rearrange("p (k t) -> p k t", k=nchunks)),
                 [pst.k], [dst_tok])

        uvb = nc.dram_tensor("uvb_scratch", [16384, 2 * D], BF16, kind="Internal").ap()
        uvb_tok = Tok()

        def _phase_cvt():
            with contextlib.ExitStack() as st:
                def sb(name, shape, dt=F32):
                    return TT(st, nc, name, shape, dt)
                NCB = 4
                cin = [sb(f"cin{i}", [128, 2, D]) for i in range(NCB)]
                cou = [sb(f"cou{i}", [128, 2, D], BF16) for i in range(NCB)]
                for blk in range(cvt_blocks):
                    i_ = blk % NCB
                    rows = slice(blk * 128, (blk + 1) * 128)
                    P.dma("sp", lambda e, i_=i_, rows=rows: e.dma_start(out=cin[i_][:, 0, :], in_=peer_u[rows, :]), [], [cin[i_].k])
                    P.dma("sp", lambda e, i_=i_, rows=rows: e.dma_start(out=cin[i_][:, 1, :], in_=peer_v[rows, :]), [], [cin[i_].k])
                    if blk % 2 == 0:
                        P.op("act", lambda e, i_=i_: e.copy(cou[i_][:], cin[i_][:]), [cin[i_].k], [cou[i_].k])
                    else:
                        P.op("dve", lambda e, i_=i_: e.tensor_copy(cou[i_][:], cin[i_][:]), [cin[i_].k], [cou[i_].k])
                    P.dma("sp", lambda e, i_=i_, rows=rows: e.dma_start(
                        out=uvb[rows, :].rearrange("r (a d) -> r a d", a=2), in_=cou[i_][:]), [cou[i_].k], [uvb_tok])
            P.barrier()
        if do_peer and not do_rw:
            _phase_cvt()

        def _phase_rw():
            with contextlib.ExitStack() as st:
                def sb(name, shape, dt=F32):
                    return TT(st, nc, name, shape, dt)

                NTILE = rw_tiles if rw_tiles else SEQ // 128
                gmix = sb("gmix", [128, D])
                P.dma("sp", lambda e: e.dma_start(out=gmix[:], in_=norm_mix_g.partition_broadcast(128)), [], [gmix.k])
                rwc = sb("rwc", [128, 1280])
                P.dma("sp", lambda e: e.dma_start(out=rwc[:], in_=rwc_d), [], [rwc.k])
                maskUN = lambda: rwc[:, 0:512]
                maskSL2 = lambda: rwc[:, 512:768]
                triBD = lambda: rwc[:, 768:896]
                onesBD = lambda: rwc[:, 896:1024]
                I2 = lambda: rwc[:, 1024:1088]
                selt = sb("selt", [128, 1])
                P.dma("sp", lambda e: e.dma_start(out=selt[:], in_=sel_d), [], [selt.k])
                prm = sb("prm", [128, 7, 512])
                for n_, src in enumerate((rw_w0, rw_a0, rw_k_k, rw_k_a, rw_r_k, rw_ln_g, rw_ln_b)):
                    P.dma("sp", lambda e, n_=n_, src=src: e.dma_start(out=prm[:, n_, :], in_=src.partition_broadcast(128)),
                          [], [prm.k])
                lup = sb("lup", [128, 512])
                gup = sb("gup", [128, 512])
                P.dma("sp", lambda e: e.dma_start(out=lup[0:64, :], in_=rw_w_up), [], [lup.k])
                P.dma("sp", lambda e: e.dma_start(out=lup[64:128, :], in_=rw_a_up), [], [lup.k])
                P.dma("sp", lambda e: e.dma_start(out=gup[:], in_=rw_g_up), [], [gup.k])
                Wa = sb("Wa", [128, 8, 1792], BF16)
                Wb = sb("Wb", [128, 8, 1792], BF16)
                mut = sb("mut", [128, 128])
                for c0 in range(0, 1792, 128):
                    sv = stage[:, 0:1024].rearrange("p (a b) -> p a b", a=8)
                    P.dma("sp", lambda e, c0=c0: e.dma_start(
                        out=sv, in_=w_in[:, 1536 + c0:1536 + c0 + 128].rearrange("(k p) n -> p k n", p=128)),
                        [], [stage.k])
                    P.dma("sp", lambda e, c0=c0: e.dma_start(out=mut[:], in_=rw_mu[:, c0:c0 + 128].partition_broadcast(128)),
                          [], [mut.k])
                    mub = lambda: mut[:].unsqueeze(1).to_broadcast([128, 8, 128])
                    P.op("dve", lambda e, c0=c0: e.tensor_tensor(Wb[:, :, c0:c0 + 128], sv, mub(), ALU.mult),
                         [stage.k, mut.k], [Wb.k])
                    P.op("dve", lambda e: e.tensor_scalar(mut[:], mut[:], -1.0, 1.0, ALU.mult, ALU.add), [mut.k], [mut.k])
                    P.op("dve", lambda e, c0=c0: e.tensor_tensor(Wa[:, :, c0:c0 + 128], sv, mub(), ALU.mult),
                         [stage.k, mut.k], [Wa.k])

                xt = sb("xt", [128, D])
                jk = stage
                ssx = sb("ssx", [128, 2])
                ub = sb("ub", [128, D], BF16)
                uTe = [sb(f"uTe{i}", [128, 8, 129], BF16) for i in range(2)]
                P.op("pool", lambda e: e.memset(uTe[0][:], 0.0), [], [uTe[0].k])
                P.op("pool", lambda e: e.memset(uTe[1][:], 0.0), [], [uTe[1].k])
                r_t = sb("r_t", [128, 512])
                k_t = sb("k_t", [128, 512])
                v_t = sb("v_t", [128, 512])
                lora = sb("lora", [128, 256])
                loraT = sb("loraT", [128, 2, 128])
                lw = sb("lw", [128, 512])
                a_t = sb("a_t", [128, 512])
                kk = sb("kk", [128, 512])
                km = sb("km", [128, 512])
                ba = sb("ba", [128, 512])
                cl = sb("cl", [128, 512])
                clC = sb("clC", [128, 512])
                ex = sb("ex", [128, 512])
                tmp = sb("tmp", [128, 512])
                st8 = sb("st8", [128, 16])
                Ab = sb("Ab", [128, 512])
                Rb = sb("Rb", [128, 512])
                Bt = sb("Bt", [128, 512])
                Kt = sb("Kt", [128, 512])
                Bh = sb("Bh", [128, 512])
                Kh = sb("Kh", [128, 512])
                PC = sb("PC", [128, 512])
                g_t = sb("g_t", [128, 512])
                Y = sb("Y", [128, 512])
                T4 = sb("T4", [128, 4, 128])
                PCT = sb("PCT", [128, 2])
                UN = sb("UN", [128, 2, 256])
                MN = sb("MN", [128, 2, 256])
                Mq = [sb(f"Mq{i}", [128, 2, 128]) for i in range(2)]
                Uq = [sb(f"Uq{i}", [128, 2, 128]) for i in range(2)]
                TTt = [sb(f"TTt{i}", [128, 2, 128]) for i in range(2)]
                X0 = sb("X0", [128, 128])
                W0 = sb("W0", [128, 128])
                ApPad = sb("ApPad", [128, 2, 128])
                P.op("pool", lambda e: e.memset(ApPad[:], 0.0), [], [ApPad.k])
                W0pad = sb("W0pad", [128, 2, 128])
                Vpad = sb("Vpad", [128, 2, 128])
                P.op("pool", lambda e: e.memset(W0pad[:], 0.0), [], [W0pad.k])
                P.op("pool", lambda e: e.memset(Vpad[:], 0.0), [], [Vpad.k])
                R0 = sb("R0", [128, 128])
                R1 = sb("R1", [128, 128])
                P.op("pool", lambda e: e.memset(R0[:], 0.0), [], [R0.k])
                P.op("pool", lambda e: e.memset(R1[:], 0.0), [], [R1.k])
                GTbd = [sb(f"GTbd{i}", [128, 128]) for i in range(2)]
                for i in range(2):
                    P.op("pool", lambda e, i=i: e.memset(GTbd[i][:], 0.0), [], [GTbd[i].k])
                Sbd = [[sb(f"Sbd{p_}_{i}", [128, 128]) for i in range(2)] for p_ in range(4)]
                for p_ in range(4):
                    for i in range(2):
                        P.op("pool", lambda e, p_=p_, i=i: e.memset(Sbd[p_][i][:], 0.0), [], [Sbd[p_][i].k])
                scur = [0, 0, 0, 0]
                o_t = sb("o_t", [128, 512])
                o_b = sb("o_b", [128, 512], BF16)

                def ev(eng, out_fn, in_fn, reads, writes):
                    if eng == "act":
                        P.op("act", lambda e: e.copy(out_fn(), in_fn()), reads, writes)
                    else:
                        P.op("dve", lambda e: e.tensor_copy(out_fn(), in_fn()), reads, writes)

                if do_peer:
                    cin_ = sb("cin_rw", [128, D])
                    cou_ = sb("cou_rw", [128, D], BF16)
                cvt_per_tile = -(-cvt_blocks // NTILE)

                def cvt_block(blk):
                    rows_ = slice(blk * 128, (blk + 1) * 128)
                    for a_, src_ in enumerate((peer_u, peer_v)):
                        P.dma("sp", lambda e, src_=src_: e.dma_start(out=cin_[:], in_=src_[rows_, :]), [], [cin_.k])
                        if a_ == 0:
                            P.op("act", lambda e: e.copy(cou_[:], cin_[:]), [cin_.k], [cou_.k])
                        else:
                            P.op("dve", lambda e: e.tensor_copy(cou_[:], cin_[:]), [cin_.k], [cou_.k])
                        P.dma("sp", lambda e, a_=a_: e.dma_start(out=uvb[rows_, a_ * D:(a_ + 1) * D], in_=cou_[:]),
                              [cou_.k], [uvb_tok])

                for it in range(NTILE):
                    if do_peer:
                        for blk in range(it * cvt_per_tile, min((it + 1) * cvt_per_tile, cvt_blocks)):
                            cvt_block(blk)
                    ue = uTe[it % 2]
                    un = uTe[(it + 1) % 2]
                    rows = slice(it * 128, (it + 1) * 128)
                    P.dma("sp", lambda e, rows=rows: e.dma_start(out=xt[:], in_=x_full[rows, :]), [], [xt.k])
                    rmsnorm(xt, gmix, ub, jk, ssx, 0)
                    transpose_to(ub, 8, lambda ue=ue: ue[:, :, 1:129], ue.k)
                    P.op("dve", lambda e, ue=ue, un=un: e.tensor_copy(un[:, :, 0:1], ue[:, :, 128:129]), [ue.k], [un.k])
                    groups = [(0, 512, r_t, 0), (512, 512, k_t, 0), (1024, 512, v_t, 0), (1536, 256, lora, 0)]
                    for gi, (c0, w_, dst, _) in enumerate(groups):
                        pz = ps[gi % 2]
                        for kc in range(8):
                            P.op("pe", lambda e, kc=kc, c0=c0, w_=w_, pz=pz, ue=ue: e.matmul(
                                pz[:, 0:w_], ue[:, kc, 1:129], Wa[:, kc, c0:c0 + w_], start=(kc == 0), stop=False),
                                [ue.k, Wa.k], [pz.k])
                        for kc in range(8):
                            P.op("pe", lambda e, kc=kc, c0=c0, w_=w_, pz=pz, ue=ue: e.matmul(
                                pz[:, 0:w_], ue[:, kc, 0:128], Wb[:, kc, c0:c0 + w_], start=False, stop=(kc == 7)),
                                [ue.k, Wb.k], [pz.k])
                        P.op("act", lambda e, dst=dst, w_=w_, pz=pz: e.copy(dst[:, 0:w_], pz[:, 0:w_]), [pz.k], [dst.k])
                    if rw_stop == 1:
                        return
                    for c in range(2):
                        P.op("pe", lambda e, c=c: e.transpose(ps[2][:, c * 128:(c + 1) * 128],
                                                              lora[:, c * 128:(c + 1) * 128], ident[:]),
                             [lora.k, ident.k], [ps[2].k])
                    P.op("act", lambda e: e.activation(out=loraT[0:64, 0, :], in_=ps[2][0:64, 0:128], func=AF.Tanh),
                         [ps[2].k], [loraT.k])
                    P.op("act", lambda e: e.copy(loraT[64:128, 0, :], ps[2][64:128, 0:128]), [ps[2].k], [loraT.k])
                    P.op("act", lambda e: e.activation(out=loraT[:, 1, :], in_=ps[2][:, 128:256], func=AF.Sigmoid),
                         [ps[2].k], [loraT.k])
                    P.op("pe", lambda e: e.matmul(ps[3][:], loraT[0:64, 0, :], lup[0:64, :], start=True, stop=True),
                         [loraT.k, lup.k], [ps[3].k])
                    P.op("pe", lambda e: e.matmul(ps[4][:], loraT[64:128, 0, :], lup[64:128, :], start=True, stop=True),
                         [loraT.k, lup.k], [ps[4].k])
                    P.op("pe", lambda e: e.matmul(ps[5][:], loraT[:, 1, :], gup[:], start=True, stop=True),
                         [loraT.k, gup.k], [ps[5].k])
                    if rw_stop == 2:
                        return
                    P.op("dve", lambda e: e.tensor_tensor(tmp[:], ps[3][:], prm[:, 0, :], ALU.add), [ps[3].k, prm.k], [tmp.k])
                    P.op("act", lambda e: e.activation(out=lw[:], in_=tmp[:], func=AF.Sigmoid), [tmp.k], [lw.k])
                    P.op("dve", lambda e: e.tensor_scalar(lw[:], lw[:], -0.6065306597126334, None, ALU.mult), [lw.k], [lw.k])
                    P.op("dve", lambda e: e.tensor_tensor(tmp[:], ps[4][:], prm[:, 1, :], ALU.add), [ps[4].k, prm.k], [tmp.k])
                    P.op("act", lambda e: e.activation(out=a_t[:], in_=tmp[:], func=AF.Sigmoid), [tmp.k], [a_t.k])
                    P.op("act", lambda e: e.copy(g_t[:], ps[5][:]), [ps[5].k], [g_t.k])
                    P.op("dve", lambda e: e.tensor_tensor(kk[:], k_t[:], prm[:, 2, :], ALU.mult), [k_t.k, prm.k], [kk.k])
                    P.op("dve", lambda e: e.tensor_tensor(tmp[:], kk[:], kk[:], ALU.mult), [kk.k], [tmp.k])
                    P.op("dve", lambda e: e.tensor_reduce(st8[:, 0:8], tmp[:].rearrange("p (h j) -> p h j", h=8), AX.X, ALU.add),
                         [tmp.k], [st8.k])
                    P.op("act", lambda e: e.activation(out=st8[:, 0:8], in_=st8[:, 0:8], func=AF.Sqrt), [st8.k], [st8.k])
                    P.op("dve", lambda e: e.tensor_scalar(st8[:, 0:8], st8[:, 0:8], 1e-12, None, ALU.max), [st8.k], [st8.k])
                    P.op("dve", lambda e: e.reciprocal(st8[:, 0:8], st8[:, 0:8]), [st8.k], [st8.k])
                    P.op("dve", lambda e: e.tensor_tensor(kk[:].rearrange("p (h j) -> p h j", h=8),
                                                          kk[:].rearrange("p (h j) -> p h j", h=8),
                                                          st8[:, 0:8].unsqueeze(2).to_broadcast([128, 8, 64]), ALU.mult),
                         [kk.k, st8.k], [kk.k])
                    P.op("dve", lambda e: e.scalar_tensor_tensor(tmp[:], a_t[:], -1.0, prm[:, 3, :], ALU.add, ALU.mult),
                         [a_t.k, prm.k], [tmp.k])
                    P.op("dve", lambda e: e.scalar_tensor_tensor(km[:], tmp[:], 1.0, k_t[:], ALU.add, ALU.mult),
                         [tmp.k, k_t.k], [km.k])
                    P.op("dve", lambda e: e.tensor_tensor(ba[:], kk[:], a_t[:], ALU.mult), [kk.k, a_t.k], [ba.k])
                    if rw_stop == 3:
                        return
                    P.op("pe", lambda e: e.matmul(ps[3][:], triBD(), lw[:], start=True, stop=True), [rwc.k, lw.k], [ps[3].k])
                    P.op("pe", lambda e: e.matmul(ps[4][:], onesBD(), lw[:], start=True, stop=True), [rwc.k, lw.k], [ps[4].k])
                    P.op("act", lambda e: e.copy(cl[:], ps[3][:]), [ps[3].k], [cl.k])
                    P.op("act", lambda e: e.copy(clC[:], ps[4][:]), [ps[4].k], [clC.k])
                    P.op("dve", lambda e: e.tensor_tensor(tmp[:], cl[:], lw[:], ALU.subtract), [cl.k, lw.k], [tmp.k])
                    P.op("act", lambda e: e.activation(out=ex[:], in_=tmp[:], func=AF.Exp), [tmp.k], [ex.k])
                    P.op("dve", lambda e: e.scalar_tensor_tensor(Ab[:], kk[:], -1.0, ex[:], ALU.mult, ALU.mult),
                         [kk.k, ex.k], [Ab.k])
                    P.op("act", lambda e: e.activation(out=ex[:], in_=cl[:], func=AF.Exp), [cl.k], [ex.k])
                    P.op("dve", lambda e: e.tensor_tensor(Rb[:], r_t[:], ex[:], ALU.mult), [r_t.k, ex.k], [Rb.k])
                    P.op("act", lambda e: e.activation(out=ex[:], in_=cl[:], func=AF.Exp, scale=-1.0), [cl.k], [ex.k])
                    P.op("dve", lambda e: e.tensor_tensor(Bt[:], ba[:], ex[:], ALU.mult), [ba.k, ex.k], [Bt.k])
                    P.op("dve", lambda e: e.tensor_tensor(Kt[:], km[:], ex[:], ALU.mult), [km.k, ex.k], [Kt.k])
                    P.op("dve", lambda e: e.tensor_tensor(tmp[:], clC[:], cl[:], ALU.subtract), [clC.k, cl.k], [tmp.k])
                    P.op("act", lambda e: e.activation(out=ex[:], in_=tmp[:], func=AF.Exp), [tmp.k], [ex.k])
                    P.op("dve", lambda e: e.tensor_tensor(Bh[:], ba[:], ex[:], ALU.mult), [ba.k, ex.k], [Bh.k])
                    P.op("dve", lambda e: e.tensor_tensor(Kh[:], km[:], ex[:], ALU.mult), [km.k, ex.k], [Kh.k])
                    P.op("act", lambda e: e.activation(out=PC[:], in_=clC[:], func=AF.Exp), [clC.k], [PC.k])

                    if rw_stop == 4:
                        return
                    for hp2 in range(4):
                        pc = slice(hp2 * 128, (hp2 + 1) * 128)
                        hc = [slice(hp2 * 128 + h * 64, hp2 * 128 + (h + 1) * 64) for h in range(2)]
                        pb_ = [slice(0, 64), slice(64, 128)]
                        for n_, src in enumerate((Ab, Rb, Bt, Kt)):
                            P.op("pe", lambda e, n_=n_, src=src, pc=pc: e.transpose(ps[2][:, n_ * 128:(n_ + 1) * 128],
                                                                                    src[:, pc], ident[:]),
                                 [src.k, ident.k], [ps[2].k])
                        P.op("pe", lambda e, pc=pc: e.transpose(ps[3][:, 0:128], PC[:, pc], ident[:]),
                             [PC.k, ident.k], [ps[3].k])
                        P.op("act", lambda e: e.copy(T4[:].rearrange("p a t -> p (a t)"), ps[2][:]), [ps[2].k], [T4.k])
                        P.op("dve", lambda e: e.tensor_copy(PCT[:], ps[3][:, 0:128].rearrange("p (c t) -> p c t", c=2)[:, :, 0]),
                             [ps[3].k], [PCT.k])
                        if rw_stop == 5:
                            return
                        for h in range(2):
                            P.op("pe", lambda e, h=h: e.matmul(ps[4][:, h * 256:(h + 1) * 256], T4[pb_[h], 2, :],
                                                               T4[pb_[h], 0:2, :].rearrange("p a t -> p (a t)"), start=True, stop=True),
                                 [T4.k], [ps[4].k])
                            P.op("pe", lambda e, h=h: e.matmul(ps[5][:, h * 256:(h + 1) * 256], T4[pb_[h], 3, :],
                                                               T4[pb_[h], 0:2, :].rearrange("p a t -> p (a t)"), start=True, stop=True),
                                 [T4.k], [ps[5].k])
                            P.op("pe", lambda e, h=h: e.matmul(ps[6][:, h * 128:(h + 1) * 128], T4[pb_[h], 0, :],
                                                               T4[pb_[h], 2, :], start=True, stop=True),
                                 [T4.k], [ps[6].k])
                        P.op("dve", lambda e: e.tensor_tensor(UN[:].rearrange("p h c -> p (h c)"), ps[4][:], maskUN(), ALU.mult),
                             [ps[4].k, rwc.k], [UN.k])
                        P.op("dve", lambda e: e.tensor_tensor(MN[:].rearrange("p h c -> p (h c)"), ps[5][:], maskUN(), ALU.mult),
                             [ps[5].k, rwc.k], [MN.k])
                        P.op("dve", lambda e: e.tensor_tensor(Mq[0][:].rearrange("p h c -> p (h c)"), ps[6][:, 0:256], maskSL2(), ALU.mult),
                             [ps[6].k, rwc.k], [Mq[0].k])
                        if rw_stop == 6:
                            return
                        P.op("act", lambda e: e.copy(Uq[0][:], UN[:, :, 0:128]), [UN.k], [Uq[0].k])
                        for h in range(2):
                            P.op("dve", lambda e, h=h: e.tensor_tensor(TTt[0][:, h, :], UN[:, h, 0:128], ident[:], ALU.add),
                                 [UN.k, ident.k], [TTt[0].k])
                        cu = 0
                        for lev in range(5):
                            nu = 1 - cu
                            for h in range(2):
                                P.op("pe", lambda e, h=h, cu=cu: e.matmul(ps[4][:, h * 128:(h + 1) * 128], Uq[cu][:, h, :], Mq[cu][:, h, :],
                                                                          start=True, stop=True), [Uq[cu].k, Mq[cu].k], [ps[4].k])
                                if lev < 4:
                                    P.op("pe", lambda e, h=h, cu=cu: e.matmul(ps[5][:, h * 128:(h + 1) * 128], Mq[cu][:, h, :], Uq[cu][:, h, :],
                                                                              start=True, stop=True), [Uq[cu].k, Mq[cu].k], [ps[5].k])
                            P.op("act", lambda e, nu=nu: e.copy(Mq[nu][:].rearrange("p h c -> p (h c)"), ps[4][:, 0:256]),
                                 [ps[4].k], [Mq[nu].k])
                            if lev < 4:
                                P.op("dve", lambda e, nu=nu: e.tensor_copy(Uq[nu][:].rearrange("p h c -> p (h c)"), ps[5][:, 0:256]),
                                     [ps[5].k], [Uq[nu].k])
                            for h in range(2):
                                P.op("pe", lambda e, h=h, nu=nu, cu=cu: e.matmul(ps[6][:, h * 128:(h + 1) * 128], Mq[nu][:, h, :], TTt[cu][:, h, :],
                                                                                 start=True, stop=True), [Mq[nu].k, TTt[cu].k], [ps[6].k])
                            P.op("dve", lambda e, nu=nu, cu=cu: e.tensor_tensor(TTt[nu][:].rearrange("p h c -> p (h c)"),
                                                                                TTt[cu][:].rearrange("p h c -> p (h c)"),
                                                                                ps[6][:, 0:256], ALU.add),
                                 [TTt[cu].k, ps[6].k], [TTt[nu].k])
                            cu = nu
                        TTf = TTt[cu]
                        if cu != 0:
                            pass
                        if rw_stop == 7:
                            return
                        for h in range(2):
                            P.op("pe", lambda e, hc=hc, h=h: e.matmul(ps[4][:, h * 64:(h + 1) * 64], MN[:, h, 0:128], v_t[:, hc[h]],
                                                               start=True, stop=True), [MN.k, v_t.k], [ps[4].k])
                        if rw_stop == 70:
                            return
                        P.op("act", lambda e: e.copy(X0[:], ps[4][:, 0:128]), [ps[4].k], [X0.k])
                        for h in range(2):
                            P.op("pe", lambda e, hc=hc, h=h, TTf=TTf: e.matmul(ps[5][:, h * 64:(h + 1) * 64], TTf[:, h, :], X0[:, h * 64:(h + 1) * 64],
                                                                        start=True, stop=True), [TTf.k, X0.k], [ps[5].k])
                            P.op("pe", lambda e, hc=hc, h=h, TTf=TTf: e.matmul(ps[5][:, 128 + h * 64:128 + (h + 1) * 64], TTf[:, h, :], Ab[:, hc[h]],
                                                                        start=True, stop=True), [TTf.k, Ab.k], [ps[5].k])
                        if rw_stop == 72:
                            return
                        P.op("act", lambda e: e.copy(W0[:], ps[5][:, 0:128]), [ps[5].k], [W0.k])
                        if rw_stop == 721:
                            return
                        for h in range(2):
                            P.op("act", lambda e, hc=hc, h=h: e.copy(W0pad[:, h, h * 64:(h + 1) * 64], ps[5][:, h * 64:(h + 1) * 64]),
                                 [ps[5].k], [W0pad.k])
                            P.op("dve", lambda e, hc=hc, h=h: e.tensor_copy(Vpad[:, h, h * 64:(h + 1) * 64], v_t[:, hc[h]]),
                                 [v_t.k], [Vpad.k])
                        if rw_stop == 722:
                            return
                        for h in range(2):
                            P.op("dve", lambda e, h=h: e.tensor_copy(ApPad[:, h, h * 64:(h + 1) * 64], ps[5][:, 128 + h * 64:128 + (h + 1) * 64]),
                                 [ps[5].k], [ApPad.k])
                        if rw_stop == 73:
                            return
                        for h in range(2):
                            P.op("pe", lambda e, h=h: e.matmul(ps[6][:, 0:128], ApPad[:, h, :], UN[:, h, 128:256],
                                                               start=(h == 0), stop=(h == 1)), [ApPad.k, UN.k], [ps[6].k])
                        if rw_stop == 74:
                            return
                        P.op("dve", lambda e: e.tensor_tensor(R0[:, 0:64], ps[6][:, 0:64], T4[:, 1, 0:64], ALU.add),
                             [ps[6].k, T4.k], [R0.k])
                        P.op("dve", lambda e: e.tensor_tensor(R1[:, 64:128], ps[6][:, 64:128], T4[:, 1, 64:128], ALU.add),
                             [ps[6].k, T4.k], [R1.k])
                        if rw_stop == 8:
                            return
                        for c in range(2):
                            cp = slice(c * 64, (c + 1) * 64)
                            for h in range(2):
                                P.op("pe", lambda e, hc=hc, h=h, cp=cp, c=c: e.matmul(ps[4][:, c * 64:(c + 1) * 64], ApPad[cp, h, :], Bh[cp, hc[h]],
                                                                               start=(h == 0), stop=(h == 1)), [ApPad.k, Bh.k], [ps[4].k])
                            for h in range(2):
                                P.op("dve", lambda e, h=h, c=c: e.scalar_tensor_tensor(
                                    GTbd[c][pb_[h], h * 64:(h + 1) * 64], I2()[pb_[h], :], PCT[pb_[h], c:c + 1],
                                    ps[4][pb_[h], c * 64:(c + 1) * 64], ALU.mult, ALU.add),
                                    [rwc.k, PCT.k, ps[4].k], [GTbd[c].k])
                        Sc0 = Sbd[hp2][scur[hp2]]
                        Sc1 = Sbd[hp2][1 - scur[hp2]]
                        for (c, Sin, Sout) in ((0, Sc0, Sc1), (1, Sc1, Sc0)):
                            cp = slice(c * 64, (c + 1) * 64)
                            P.op("pe", lambda e, cp=cp, pc=pc: e.matmul(ps[5][:, 0:128], Bh[cp, pc], W0[cp, :], start=True, stop=False),
                                 [Bh.k, W0.k], [ps[5].k])
                            P.op("pe", lambda e, cp=cp, pc=pc: e.matmul(ps[5][:, 0:128], Kh[cp, pc], v_t[cp, pc], start=False, stop=False),
                                 [Kh.k, v_t.k], [ps[5].k])
                            P.op("pe", lambda e, c=c, Sin=Sin: e.matmul(ps[5][:, 0:128], GTbd[c][:], Sin[:], start=False, stop=True),
                                 [GTbd[c].k, Sin.k], [ps[5].k])
                            if c == 0:
                                for h in range(2):
                                    P.op("pe", lambda e, h=h: e.matmul(ps[6][:, 128:256], UN[:, h, 128:256], W0pad[:, h, :],
                                                                       start=(h == 0), stop=False), [UN.k, W0pad.k], [ps[6].k])
                                    P.op("pe", lambda e, h=h: e.matmul(ps[6][:, 128:256], MN[:, h, 128:256], Vpad[:, h, :],
                                                                       start=False, stop=False), [MN.k, Vpad.k], [ps[6].k])
                                P.op("pe", lambda e, Sin=Sin: e.matmul(ps[6][:, 128:256], R0[:], Sin[:], start=False, stop=False),
                                     [R0.k, Sin.k], [ps[6].k])
                            for h in range(2):
                                P.op("dve" if h == 0 else "act",
                                     (lambda e, h=h, Sout=Sout: e.tensor_copy(Sout[pb_[h], h * 64:(h + 1) * 64], ps[5][pb_[h], h * 64:(h + 1) * 64]))
                                     if h == 0 else
                                     (lambda e, h=h, Sout=Sout: e.copy(Sout[pb_[h], h * 64:(h + 1) * 64], ps[5][pb_[h], h * 64:(h + 1) * 64])),
                                     [ps[5].k], [Sout.k])
                            if c == 0:
                                P.op("pe", lambda e, Sout=Sout: e.matmul(ps[6][:, 128:256], R1[:], Sout[:], start=False, stop=True),
                                     [R1.k, Sout.k], [ps[6].k])
                                P.op("act", lambda e, pc=pc: e.copy(Y[:, pc], ps[6][:, 128:256]), [ps[6].k], [Y.k])

                    if rw_stop == 9:
                        return
                    Y3 = lambda: Y[:].rearrange("p (h j) -> p h j", h=8)
                    P.op("dve", lambda e: e.tensor_reduce(st8[:, 0:8], Y3(), AX.X, ALU.add), [Y.k], [st8.k])
                    P.op("dve", lambda e: e.tensor_scalar(st8[:, 0:8], st8[:, 0:8], 1.0 / 64, None, ALU.mult), [st8.k], [st8.k])
                    P.op("dve", lambda e: e.tensor_tensor(Y3(), Y3(), st8[:, 0:8].unsqueeze(2).to_broadcast([128, 8, 64]), ALU.subtract),
                         [Y.k, st8.k], [Y.k])
                    P.op("dve", lambda e: e.tensor_tensor(tmp[:], Y[:], Y[:], ALU.mult), [Y.k], [tmp.k])
                    P.op("dve", lambda e: e.tensor_reduce(st8[:, 8:16], tmp[:].rearrange("p (h j) -> p h j", h=8), AX.X, ALU.add),
                         [tmp.k], [st8.k])
                    P.op("dve", lambda e: e.tensor_scalar(st8[:, 8:16], st8[:, 8:16], 1.0 / 64, 64e-5, ALU.mult, ALU.add), [st8.k], [st8.k])
                    P.op("act", lambda e: e.activation(out=st8[:, 8:16], in_=st8[:, 8:16], func=AF.Sqrt), [st8.k], [st8.k])
                    P.op("dve", lambda e: e.reciprocal(st8[:, 8:16], st8[:, 8:16]), [st8.k], [st8.k])
                    P.op("dve", lambda e: e.tensor_tensor(Y3(), Y3(), st8[:, 8:16].unsqueeze(2).to_broadcast([128, 8, 64]), ALU.mult),
                         [Y.k, st8.k], [Y.k])
                    P.op("dve", lambda e: e.tensor_tensor(Y[:], Y[:], prm[:, 5, :], ALU.mult), [Y.k, prm.k], [Y.k])
                    P.op("dve", lambda e: e.tensor_tensor(Y[:], Y[:], prm[:, 6, :], ALU.add), [Y.k, prm.k], [Y.k])
                    P.op("dve", lambda e: e.tensor_tensor(tmp[:], r_t[:], km[:], ALU.mult), [r_t.k, km.k], [tmp.k])
                    P.op("dve", lambda e: e.tensor_tensor(tmp[:], tmp[:], prm[:, 4, :], ALU.mult), [tmp.k, prm.k], [tmp.k])
                    P.op("dve", lambda e: e.tensor_reduce(st8[:, 0:8], tmp[:].rearrange("p (h j) -> p h j", h=8), AX.X, ALU.add),
                         [tmp.k], [st8.k])
                    P.op("dve", lambda e: e.tensor_tensor(tmp[:].rearrange("p (h j) -> p h j", h=8),
                                                          v_t[:].rearrange("p (h j) -> p h j", h=8),
                                                          st8[:, 0:8].unsqueeze(2).to_broadcast([128, 8, 64]), ALU.mult),
                         [v_t.k, st8.k], [tmp.k])
                    P.op("dve", lambda e: e.tensor_tensor(Y[:], Y[:], tmp[:], ALU.add), [Y.k, tmp.k], [Y.k])
                    P.op("dve", lambda e: e.tensor_tensor(o_t[:], Y[:], g_t[:], ALU.mult), [Y.k, g_t.k], [o_t.k])
                    sc_i, tin = it // 8, it % 8
                    slot = sc_i * 4 + (tin % 4)
                    if tin < 4:
                        P.op("dve", lambda e, slot=slot: e.tensor_copy(o_rw[:, slot, :], o_t[:]), [o_t.k], [o_rw.k])
                    else:
                        P.op("dve", lambda e, slot=slot: e.tensor_tensor(o_t[:], o_t[:], o_rw[:, slot, :], ALU.subtract),
                             [o_t.k, o_rw.k], [o_t.k])
                        P.op("dve", lambda e, slot=slot: e.scalar_tensor_tensor(o_rw[:, slot, :], o_t[:], selt[:], o_rw[:, slot, :],
                                                                                ALU.mult, ALU.add),
                             [o_t.k, selt.k, o_rw.k], [o_rw.k])
            P.barrier()
        if do_rw:
            _phase_rw()
            P.barrier()
        if DEBUG == 2:
            dbg_d = nc.dram_tensor("dbg", [128, 4 * 512], F32, kind="ExternalOutput").ap()
            for s_ in range(min(4, rw_tiles or 4)):
                P.op("dve", lambda e, s_=s_: e.tensor_copy(stage[:, 0:512], o_rw[:, s_, :]), [o_rw.k, stage.k], [stage.k])
                P.dma("sp", lambda e, s_=s_: e.dma_start(out=dbg_d[:, s_ * 512:(s_ + 1) * 512], in_=stage[:, 0:512]), [stage.k], [DBG_TOK])
        oT_da = sb0("oT_da", [128, 4, NOWN], BF16) if not rw_stop else None
        def _phase_da():
            with contextlib.ExitStack() as st:
                def sb(name, shape, dt=F32):
                    return TT(st, nc, name, shape, dt)

                gmix = sb("gmix", [128, D])
                P.dma("sp", lambda e: e.dma_start(out=gmix[:], in_=norm_mix_g.partition_broadcast(128)), [], [gmix.k])
                maskT = sb("maskT", [128, 8, 512], BF16)
                for mh in range(4):
                    load_bf16(maskT, maskT[:, mh * 2:(mh + 1) * 2, :],
                              mask_d[:, mh * 1024:(mh + 1) * 1024].rearrange("p (j f) -> p j f", j=2))
                lv = sb("lv", [128, 256])
                lt = sb("lt", [128, 128])
                lsc = sb("lsc", [128, 4])
                P.dma("sp", lambda e: e.dma_start(out=lv[:], in_=lamv.partition_broadcast(128)), [], [lv.k])
                P.op("dve", lambda e: e.tensor_tensor(lt[:].rearrange("p (a b) -> p a b", a=2),
                                                      lv[:].rearrange("p (a c b) -> p a c b", a=2, c=2)[:, :, 0, :],
                                                      lv[:].rearrange("p (a c b) -> p a c b", a=2, c=2)[:, :, 1, :],
                                                      ALU.mult), [lv.k], [lt.k])
                P.op("dve", lambda e: e.tensor_reduce(lsc[:, 0:2], lt[:].rearrange("p (a b) -> p a b", a=2),
                                                      AX.X, ALU.add), [lt.k], [lsc.k])
                P.op("act", lambda e: e.activation(out=lsc[:, 0:2], in_=lsc[:, 0:2], func=AF.Exp), [lsc.k], [lsc.k])
                P.op("dve", lambda e: e.tensor_tensor(lsc[:, 2:3], lsc[:, 1:2], lsc[:, 0:1], ALU.subtract), [lsc.k], [lsc.k])
                P.op("dve", lambda e: e.tensor_scalar(lsc[:, 2:3], lsc[:, 2:3], -LAM_INIT, None, ALU.add), [lsc.k], [lsc.k])
                gsub = sb("gsub", [128, 1])
                P.dma("sp", lambda e: e.dma_start(out=gsub[:], in_=da_subln_g), [], [gsub.k])
                P.op("dve", lambda e: e.tensor_scalar(gsub[:], gsub[:], 1.0 - LAM_INIT, None, ALU.mult), [gsub.k], [gsub.k])

                kT = sb("kT", [128, 2, SEQ], BF16)
                Vt = sb("Vt", [128, SEQ // 128, 256], BF16)
                uT = [sb("uT0", [128, 8, 512], BF16)] * 2
                xt = [sb(f"xt{i}", [128, D]) for i in range(2)]
                jk = stage
                ssx = [sb("ssx0", [128, 2])] * 2
                ub = [sb("ub0", [128, D], BF16)] * 2
                wk = sb("wk", [128, 8, 256], BF16)
                wkp = sb("wkp", [128, 8, 256], BF16)
                wq, wqp = wk, wkp
                wv = sb("wv", [128, 8, 256], BF16)
                posi = sb("posi", [128, 512], I32)
                ang = sb("ang", [128, 512])
                kq = sb("kq", [128, 512])
                kqi = sb("kqi", [128, 512], I32)
                msk = sb("msk", [128, 512])
                tC = sb("tC", [128, 512])
                tS = sb("tS", [128, 512])
                t1 = sb("t1", [128, 512])
                t2 = sb("t2", [128, 512])
                qT = sb("qT", [128, 2, 512], BF16)
                pT = [sb(f"pT{i}", [128, 512], BF16) for i in range(3)]
                rr = [sb(f"rr{i}", [128, 512]) for i in range(2)]
                oo = sb("oo", [128, 512])
                sqb = sb("sqb", [128, 512], BF16)

                def sin_table(dst, shift, scale_col, mul):
                    P.op("dve", lambda e: e.tensor_scalar(kq[:], ang[:], shift, 1.0 / TWO_PI, ALU.add, ALU.mult),
                         [ang.k], [kq.k])
                    P.op("dve", lambda e: e.tensor_copy(kqi[:], kq[:]), [kq.k], [kqi.k])
                    P.op("dve", lambda e: e.tensor_copy(kq[:], kqi[:]), [kqi.k], [kq.k])
                    P.op("dve", lambda e: e.scalar_tensor_tensor(msk[:], kq[:], -C1, ang[:], ALU.mult, ALU.add),
                         [kq.k, ang.k], [msk.k])
                    P.op("dve", lambda e: e.scalar_tensor_tensor(msk[:], kq[:], -C2, msk[:], ALU.mult, ALU.add),
                         [kq.k, msk.k], [msk.k])
                    P.op("dve", lambda e: e.tensor_scalar(msk[:], msk[:], shift, None, ALU.add), [msk.k], [msk.k])
                    P.op("dve", lambda e: e.tensor_scalar(kq[:], msk[:], 3.141592653589793, -TWO_PI, ALU.is_gt, ALU.mult),
                         [msk.k], [kq.k])
                    P.op("dve", lambda e: e.tensor_tensor(msk[:], msk[:], kq[:], ALU.add), [msk.k, kq.k], [msk.k])
                    P.op("dve", lambda e: e.tensor_scalar(kq[:], msk[:], -3.141592653589793, TWO_PI, ALU.is_lt, ALU.mult),
                         [msk.k], [kq.k])
                    P.op("dve", lambda e: e.tensor_tensor(msk[:], msk[:], kq[:], ALU.add), [msk.k, kq.k], [msk.k])
                    P.op("act", lambda e: e.activation(out=dst[:], in_=msk[:], func=AF.Sin), [msk.k], [dst.k])
                    if scale_col is not None:
                        P.op("dve", lambda e: e.tensor_scalar(dst[:], dst[:], cvec[:, scale_col:scale_col + 1], mul,
                                                              ALU.mult, ALU.mult), [dst.k, cvec.k], [dst.k])
                    elif mul != 1.0:
                        P.op("dve", lambda e: e.tensor_scalar(dst[:], dst[:], mul, None, ALU.mult), [dst.k], [dst.k])

                def rope_tables(pos_ap, mul):
                    P.dma("sp", lambda e: e.dma_start(out=posi[:], in_=pos_ap.partition_broadcast(128)), [], [posi.k])
                    P.op("dve", lambda e: e.tensor_copy(ang[:], posi[:]), [posi.k], [ang.k])
                    P.op("dve", lambda e: e.tensor_scalar(ang[:], ang[:], cvec[:, 0:1], None, ALU.mult),
                         [ang.k, cvec.k], [ang.k])
                    sin_table(tC, 1.5707963267948966, None, mul)
                    sin_table(tS, 0.0, 1, mul)

                def load_uT(src_dram, row0, dst):
                    for tt in range(4):
                        b = tt % 2
                        rows = slice(row0 + tt * 128, row0 + (tt + 1) * 128)
                        P.dma("sp", lambda e, b=b, rows=rows: e.dma_start(out=xt[b][:], in_=src_dram[rows, :]),
                              [], [xt[b].k])
                        rmsnorm(xt[b], gmix, ub[b], jk, ssx[b], 0)
                        transpose_to(ub[b], 8, lambda tt=tt: dst[:, :, tt * 128:(tt + 1) * 128], dst.k)

                def proj_rope(w_a, w_b, h, src_uT, dst_ap_fn, dst_tok):
                    pa, pb_ = ps[0], ps[1]
                    for kc in range(8):
                        P.op("pe", lambda e, kc=kc: e.matmul(pa[:], w_a[:, kc, h * 128:(h + 1) * 128],
                                                             src_uT[:, kc, :], start=(kc == 0), stop=(kc == 7)),
                             [w_a.k, src_uT.k], [pa.k])
                    for kc in range(8):
                        P.op("pe", lambda e, kc=kc: e.matmul(pb_[:], w_b[:, kc, h * 128:(h + 1) * 128],
                                                             src_uT[:, kc, :], start=(kc == 0), stop=(kc == 7)),
                             [w_b.k, src_uT.k], [pb_.k])
                    P.op("dve", lambda e: e.tensor_tensor(t1[:], pa[:], tC[:], ALU.mult), [pa.k, tC.k], [t1.k])
                    P.op("dve", lambda e: e.tensor_tensor(t2[:], pb_[:], tS[:], ALU.mult), [pb_.k, tS.k], [t2.k])
                    P.op("dve", lambda e: e.tensor_tensor(dst_ap_fn(), t1[:], t2[:], ALU.add), [t1.k, t2.k], [dst_tok])

                for hp in range(1 if mini in (1, 2) else 2):
                    def wload(dst, col0):
                        load_w(dst, w_in[:, col0:col0 + 256], 256)
                    def wperm(wsrc, wdst):
                        P.op("pool", lambda e, wdst=wdst: e.memset(wdst[:], 0.0), [], [wdst.k])
                        P.op("dve", lambda e, wsrc=wsrc, wdst=wdst: e.tensor_copy(
                            wdst[:].rearrange("p k (b d) -> p k b d", d=64)[:, :, :, 0:8],
                            wsrc[:].rearrange("p k (b d) -> p k b d", d=64)[:, :, :, 8:16]), [wsrc.k, wdst.k], [wdst.k])
                        P.op("dve", lambda e, wsrc=wsrc, wdst=wdst: e.tensor_copy(
                            wdst[:].rearrange("p k (b d) -> p k b d", d=64)[:, :, :, 8:16],
                            wsrc[:].rearrange("p k (b d) -> p k b d", d=64)[:, :, :, 0:8]), [wsrc.k, wdst.k], [wdst.k])
                    wload(wk, 512 + hp * 256)
                    wload(wv, 1024 + hp * 256)
                    wperm(wk, wkp)
                    for (wsrc, wdst) in ():
                        P.op("pool", lambda e, wdst=wdst: e.memset(wdst[:], 0.0), [], [wdst.k])
                        P.op("dve", lambda e, wsrc=wsrc, wdst=wdst: e.tensor_copy(
                            wdst[:].rearrange("p k (b d) -> p k b d", d=64)[:, :, :, 0:8],
                            wsrc[:].rearrange("p k (b d) -> p k b d", d=64)[:, :, :, 8:16]), [wsrc.k, wdst.k], [wdst.k])
                        P.op("dve", lambda e, wsrc=wsrc, wdst=wdst: e.tensor_copy(
                            wdst[:].rearrange("p k (b d) -> p k b d", d=64)[:, :, :, 8:16],
                            wsrc[:].rearrange("p k (b d) -> p k b d", d=64)[:, :, :, 0:8]), [wsrc.k, wdst.k], [wdst.k])
                    for g in range(2 if mini else SEQ // 512):
                        u = uT[g % 2]
                        load_uT(x_full, g * 512, u)
                        if mini == 2:
                            P.op('pool', lambda e: e.memset(tC[:], 1.0), [], [tC.k])
                            P.op('pool', lambda e: e.memset(tS[:], 0.0), [], [tS.k])
                        else:
                            rope_tables(pos_full[:, g * 512:(g + 1) * 512], 1.0)
                        for h in range(2):
                            proj_rope(wk, wkp, h, u, lambda h=h, g=g: kT[:, h, g * 512:(g + 1) * 512], kT.k)
                        for tt in range(4):
                            pv = ps[2 + tt % 2]
                            for kc in range(8):
                                P.op("pe", lambda e, kc=kc, tt=tt, pv=pv, u=u: e.matmul(
                                    pv[:, 0:256], u[:, kc, tt * 128:(tt + 1) * 128], wv[:, kc, :],
                                    start=(kc == 0), stop=(kc == 7)), [u.k, wv.k], [pv.k])
                            P.op("act", lambda e, tt=tt, pv=pv, g=g: e.copy(Vt[:, g * 4 + tt, :], pv[:, 0:256]),
                                 [pv.k], [Vt.k])
                    if mini not in (1, 2):
                        wload(wq, hp * 256)
                        wperm(wq, wqp)
                    for i in range({0: 8, 1: 0, 2: 0, 3: 1}[mini]):
                        u = uT[i % 2]
                        load_uT(x_own, i * 512, u)
                        rope_tables(pos_own[:, i * 512:(i + 1) * 512], 0.125)
                        for h in range(2):
                            proj_rope(wq, wqp, h, u, lambda h=h: qT[:, h, :], qT.k)
                        nkb = 8 * i + 8
                        for h in range(2):
                            acc = [ps[2], ps[3], ps[4], ps[5]]
                            pairs = [(j, c) for j in range(nkb) for c in range(2)]

                            def qk(n):
                                j, c = pairs[n]
                                sc = ps[n % 2]
                                pr = slice(c * 64, (c + 1) * 64)
                                P.op("pe", lambda e, sc=sc, pr=pr, j=j, h=h: e.matmul(
                                    sc[:], kT[pr, h, j * 128:(j + 1) * 128], qT[pr, h, :], start=True, stop=True),
                                    [kT.k, qT.k], [sc.k])
                            qk(0)
                            for n in range(len(pairs)):
                                j, c = pairs[n]
                                sc = ps[n % 2]
                                pt = pT[n % 3]
                                P.op("act", lambda e, sc=sc, pt=pt: e.activation(out=pt[:], in_=sc[:], func=AF.Exp),
                                     [sc.k], [pt.k])
                                if n + 1 < len(pairs):
                                    qk(n + 1)
                                if j >= 8 * i:
                                    jj = j - 8 * i
                                    P.op("dve", lambda e, pt=pt, jj=jj: e.tensor_tensor(pt[:], pt[:], maskT[:, jj, :], ALU.mult),
                                         [pt.k, maskT.k], [pt.k])
                                P.op("pe", lambda e, pt=pt, j=j, h=h, c=c, nkb=nkb: e.matmul(
                                    acc[2 * c][:], Vt[:, j, h * 128:(h + 1) * 128], pt[:],
                                    start=(j == 0), stop=(j == nkb - 1)), [Vt.k, pt.k], [acc[2 * c].k])
                                P.op("pe", lambda e, pt=pt, j=j, c=c, nkb=nkb: e.matmul(
                                    acc[2 * c + 1][:], onesb[:], pt[:],
                                    start=(j == 0), stop=(j == nkb - 1)), [onesb.k, pt.k], [acc[2 * c + 1].k])
                            for c in range(2):
                                P.op("dve", lambda e, c=c: e.reciprocal(rr[c][:], acc[2 * c + 1][:]), [acc[2 * c + 1].k], [rr[c].k])
                                P.op("dve", lambda e, c=c: e.tensor_tensor(rr[c][:], rr[c][:], acc[2 * c][:], ALU.mult),
                                     [rr[c].k, acc[2 * c].k], [rr[c].k])
                            P.op("dve", lambda e: e.scalar_tensor_tensor(oo[:], rr[1][:], lsc[:, 2:3], rr[0][:], ALU.mult, ALU.add),
                                 [rr[0].k, rr[1].k, lsc.k], [oo.k])
                            P.op("act", lambda e: e.activation(out=sqb[:], in_=oo[:], func=AF.Square), [oo.k], [sqb.k])
                            P.op("pe", lambda e: e.matmul(ps[6][:], onesb[:], sqb[:], start=True, stop=True),
                                 [onesb.k, sqb.k], [ps[6].k])
                            P.op("dve", lambda e: e.tensor_scalar(rr[0][:], ps[6][:], 1.0 / 128, EPS, ALU.mult, ALU.add),
                                 [ps[6].k], [rr[0].k])
                            P.op("pool", lambda e: e.tensor_tensor(rr[0][:], rr[0][:], mhalf[:].to_broadcast([128, 512]), ALU.pow),
                                 [rr[0].k, mhalf.k], [rr[0].k])
                            P.op("dve", lambda e, h=h, i=i, hp=hp: e.scalar_tensor_tensor(
                                oT_da[:, 2 * hp + h, i * 512:(i + 1) * 512], oo[:], gsub[:], rr[0][:], ALU.mult, ALU.mult),
                                [oo.k, gsub.k, rr[0].k], [oT_da.k])

        if do_da:
            _phase_da()
        P.barrier()
        def _phase_c():
            with contextlib.ExitStack() as st:
                def sb(name, shape, dt=F32):
                    return TT(st, nc, name, shape, dt)

                gple = sb("gple", [128, D])
                gfin = sb("gfin", [128, D])
                P.dma("sp", lambda e: e.dma_start(out=gple[:], in_=norm_ple_g.partition_broadcast(128)), [], [gple.k])
                P.dma("sp", lambda e: e.dma_start(out=gfin[:], in_=norm_final_g.partition_broadcast(128)), [], [gfin.k])
                wo = sb("wo", [128, 8, D], BF16)
                wg = sb("wg", [128, 8, D], BF16)
                wp = sb("wp", [128, 2, D], BF16)
                load_w(wo, w_out, D)
                load_w(wg, ple_gate_w, D)
                load_w(wp, ple_proj_w, D)
                if do_peer:
                    gffn = sb("gffn", [128, D])
                    P.dma("sp", lambda e: e.dma_start(out=gffn[:], in_=norm_ffn_g.partition_broadcast(128)), [], [gffn.k])
                    wq = sb("wq", [128, 8, 2048], BF16)
                    load_w(wq, peer_w_q, 2048)
                    iota16 = sb("iota16", [128, 16])
                    P.dma("sp", lambda e: e.dma_start(out=iota16[:], in_=iota_d), [], [iota16.k])
                    keysT = sb("keysT", [128, 16, 128], BF16)
                    kst = stage
                    for g4 in range(4):
                        P.dma("sp", lambda e, g4=g4: e.dma_start(
                            out=kst[:, 0:512].rearrange("p (a d) -> p a d", a=4), in_=peer_keys[g4 * 4:(g4 + 1) * 4].rearrange("h n d -> n h d")), [], [kst.k])
                        for q in range(4):
                            P.op("pe", lambda e, q=q: e.transpose(ps[0][:, q * 128:(q + 1) * 128], kst[:, q * 128:(q + 1) * 128], ident[:]),
                                 [kst.k, ident.k], [ps[0].k])
                        P.op("act", lambda e, g4=g4: e.copy(keysT[:, g4 * 4:(g4 + 1) * 4, :].rearrange("p a n -> p (a n)"), ps[0][:]),
                             [ps[0].k], [keysT.k])
                    qTg = sb("qTg", [128, 4, 128], BF16)
                    s4 = sb("s4", [128, 4, 128])
                    s4b = sb("s4b", [128, 128])
                    tv = sb("tv", [128, 16, 16])
                    ti = sb("ti", [128, 16, 16], U32)
                    tif = sb("tif", [128, 16, 16])
                    cand = sb("cand", [128, 256])
                    cand2 = sb("cand2", [128, 256])
                    best = sb("best", [128, 8, 16])
                    pos = sb("pos", [128, 8, 16], U32)
                    pij = sb("pij", [128, 2, 128], I32)
                    pijf = sb("pijf", [128, 2, 128])
                    oh = sb("oh", [128, 8, 16, 16], BF16)
                    e01 = sb("e01", [128, 2, 128])
                    idxf = sb("idxf", [128, 128])
                    idxi = sb("idxi", [128, 128], I32)
                    gsm = sb("gsm", [128, 8, 16])
                    gss = sb("gss", [128, 8])
                    hid = sb("hid", [128, 128])
                    wgt = sb("wgt", [128, 128])
                    NG = 4
                    UV = [sb(f"UV{i}", [128, 2, D], BF16) for i in range(NG)]
                    dgs = [sb(f"dg{i}", [128, 128], BF16) for i in range(2)]

                NB = 1
                hbuf = [sb(f"h{i}", [128, D]) for i in range(NB)]
                junk = sb("junk", [128, D])
                ssb = [sb(f"ss{i}", [128, 4]) for i in range(NB)]
                nb = [sb(f"n{i}", [128, D], BF16) for i in range(NB)]
                nT = [sb(f"nT{i}", [128, 8, 128], BF16) for i in range(NB)]
                orT = [sb(f"orT{i}", [128, 4, 128], BF16) for i in range(NB)]
                pin = [sb(f"pin{i}", [128, 256]) for i in range(NB)]
                pb = [sb(f"pb{i}", [128, 256], BF16) for i in range(NB)]
                pT2 = [sb(f"pT2{i}", [128, 2, 128], BF16) for i in range(NB)]
                gate = [sb(f"gate{i}", [128, D]) for i in range(NB)]
                h3 = gate
                ob = hbuf
                prod, xnb, xnT, xn = junk, nb[0], nT[0], gate[0]

                NT = c_tiles if c_tiles else {0: NOWN // 128, 1: 0, 2: 0, 3: 2}[mini]
                for it in range(NT):
                    b = it % NB
                    rows = slice(it * 128, (it + 1) * 128)
                    h = hbuf[b]
                    P.dma("sp", lambda e, h=h, rows=rows: e.dma_start(out=h[:], in_=x_own[rows, :]), [], [h.k])
                    P.dma("sp", lambda e, b=b, rows=rows: e.dma_start(out=pin[b][:], in_=p_own[rows, :]), [], [pin[b].k])
                    if do_da or do_rw:
                        if do_rw:
                            for c in range(4):
                                P.op("pe", lambda e, c=c, it=it: e.transpose(pst[:, c * 128:(c + 1) * 128],
                                                                             o_rw[:, it, c * 128:(c + 1) * 128], identb[:]),
                                     [o_rw.k, identb.k], [pst.k])
                            P.op("act", lambda e, b=b: e.copy(orT[b][:], pst[:, 0:512].rearrange("p (k t) -> p k t", k=4)),
                                 [pst.k], [orT[b].k])
                        for half in range(2):
                            cs = slice(half * 512, (half + 1) * 512)
                            pg = ps[4 + half]
                            kcs = ([0, 1, 2, 3] if do_da else []) + ([4, 5, 6, 7] if do_rw else [])
                            for n_, kc in enumerate(kcs):
                                if kc < 4:
                                    lhs = (lambda kc=kc, it=it: oT_da[:, kc, it * 128:(it + 1) * 128])
                                    rk = oT_da.k
                                else:
                                    lhs = (lambda kc=kc, b=b: orT[b][:, kc - 4, :])
                                    rk = orT[b].k
                                P.op("pe", lambda e, lhs=lhs, kc=kc, cs=cs, pg=pg, n_=n_, kcs=kcs: e.matmul(
                                    pg[:], lhs(), wo[:, kc, cs], start=(n_ == 0), stop=(n_ == len(kcs) - 1)),
                                    [rk, wo.k], [pg.k])
                            P.op("dve", lambda e, h=h, cs=cs, pg=pg: e.tensor_tensor(h[:, cs], h[:, cs], pg[:], ALU.add),
                                 [h.k, pg.k], [h.k])
                    if do_peer:
                        rmsnorm(h, gffn, xn, junk, ssb[b], 2)
                        P.op("act", lambda e: e.copy(xnb[:], xn[:]), [xn.k], [xnb.k])
                        transpose_to(xnb, 8, lambda: xnT[:], xnT.k)
                        for g4 in range(4):
                            for q in range(4):
                                hp_ = g4 * 4 + q
                                for kc in range(8):
                                    P.op("pe", lambda e, q=q, kc=kc, hp_=hp_: e.matmul(
                                        ps[0][:, q * 128:(q + 1) * 128], wq[:, kc, hp_ * 128:(hp_ + 1) * 128], xnT[:, kc, :],
                                        start=(kc == 0), stop=(kc == 7)), [wq.k, xnT.k], [ps[0].k])
                            P.op("act", lambda e: e.copy(qTg[:].rearrange("p a t -> p (a t)"), ps[0][:]), [ps[0].k], [qTg.k])
                            for q in range(4):
                                hp_ = g4 * 4 + q
                                P.op("pe", lambda e, q=q, hp_=hp_: e.matmul(ps[1][:, q * 128:(q + 1) * 128], qTg[:, q, :], keysT[:, hp_, :],
                                                                          start=True, stop=True), [qTg.k, keysT.k], [ps[1].k])
                            P.op("act", lambda e: e.copy(s4[:].rearrange("p a n -> p (a n)"), ps[1][:]), [ps[1].k], [s4.k])
                            for q in range(4):
                                hp_ = g4 * 4 + q
                                P.op("dve", lambda e, q=q, hp_=hp_: e.max(out=tv[:, hp_, 0:8], in_=s4[:, q, :]), [s4.k], [tv.k])
                                P.op("dve", lambda e, q=q, hp_=hp_: e.max_index(out=ti[:, hp_, 0:8], in_max=tv[:, hp_, 0:8], in_values=s4[:, q, :]),
                                     [s4.k, tv.k], [ti.k])
                                P.op("dve", lambda e, q=q, hp_=hp_: e.match_replace(out=s4b[:], in_to_replace=tv[:, hp_, 0:8], in_values=s4[:, q, :],
                                                                                    imm_value=-1e30), [s4.k, tv.k], [s4b.k])
                                P.op("dve", lambda e, hp_=hp_: e.max(out=tv[:, hp_, 8:16], in_=s4b[:]), [s4b.k], [tv.k])
                                P.op("dve", lambda e, hp_=hp_: e.max_index(out=ti[:, hp_, 8:16], in_max=tv[:, hp_, 8:16], in_values=s4b[:]),
                                     [s4b.k, tv.k], [ti.k])
                        P.op("dve", lambda e: e.tensor_copy(tif[:], ti[:]), [ti.k], [tif.k])
                        for hh_ in range(8):
                            c3 = lambda: cand[:].rearrange("p (i j) -> p i j", i=16)
                            P.op("dve", lambda e, hh_=hh_, c3=c3: e.tensor_tensor(
                                c3(), tv[:, 2 * hh_, :].unsqueeze(2).to_broadcast([128, 16, 16]),
                                tv[:, 2 * hh_ + 1, :].unsqueeze(1).to_broadcast([128, 16, 16]), ALU.add), [tv.k], [cand.k])
                            P.op("dve", lambda e, hh_=hh_: e.max(out=best[:, hh_, 0:8], in_=cand[:]), [cand.k], [best.k])
                            P.op("dve", lambda e, hh_=hh_: e.max_index(out=pos[:, hh_, 0:8], in_max=best[:, hh_, 0:8], in_values=cand[:]),
                                 [cand.k, best.k], [pos.k])
                            P.op("dve", lambda e, hh_=hh_: e.match_replace(out=cand2[:], in_to_replace=best[:, hh_, 0:8], in_values=cand[:],
                                                                           imm_value=-1e30), [cand.k, best.k], [cand2.k])
                            P.op("dve", lambda e, hh_=hh_: e.max(out=best[:, hh_, 8:16], in_=cand2[:]), [cand2.k], [best.k])
                            P.op("dve", lambda e, hh_=hh_: e.max_index(out=pos[:, hh_, 8:16], in_max=best[:, hh_, 8:16], in_values=cand2[:]),
                                 [cand2.k, best.k], [pos.k])
                        posf = lambda: pos[:].rearrange("p h k -> p (h k)")
                        P.op("dve", lambda e: e.tensor_copy(idxf[:], posf()), [pos.k], [idxf.k])
                        P.op("dve", lambda e: e.tensor_scalar(pijf[:, 1, :], idxf[:], 0.0625, None, ALU.mult), [idxf.k], [pijf.k])
                        P.op("dve", lambda e: e.tensor_copy(pij[:, 0, :], pijf[:, 1, :]), [pijf.k], [pij.k])
                        P.op("dve", lambda e: e.tensor_copy(pijf[:, 0, :], pij[:, 0, :]), [pij.k], [pijf.k])
                        P.op("dve", lambda e: e.tensor_scalar(pijf[:, 1, :], pijf[:, 0, :], 16.0, None, ALU.mult), [pijf.k], [pijf.k])
                        P.op("dve", lambda e: e.tensor_tensor(pijf[:, 1, :], pijf[:, 1, :], idxf[:], ALU.is_gt), [pijf.k, idxf.k], [pijf.k])
                        P.op("dve", lambda e: e.tensor_tensor(pijf[:, 0, :], pijf[:, 0, :], pijf[:, 1, :], ALU.subtract), [pijf.k], [pijf.k])
                        P.op("dve", lambda e: e.scalar_tensor_tensor(pijf[:, 1, :], pijf[:, 0, :], -16.0, idxf[:], ALU.mult, ALU.add),
                             [pijf.k, idxf.k], [pijf.k])
                        tif4 = lambda: tif[:].rearrange("p (h two) k -> p h two k", two=2)
                        for pp_ in range(2):
                            P.op("dve", lambda e, pp_=pp_: e.tensor_tensor(
                                oh[:], pijf[:, pp_, :].rearrange("p (h k) -> p h k", h=8).unsqueeze(3).to_broadcast([128, 8, 16, 16]),
                                iota16[:].unsqueeze(1).unsqueeze(1).to_broadcast([128, 8, 16, 16]), ALU.is_equal),
                                [pijf.k, iota16.k], [oh.k])
                            P.op("dve", lambda e, pp_=pp_: e.tensor_tensor(
                                oh[:], oh[:], tif4()[:, :, pp_, :].unsqueeze(2).to_broadcast([128, 8, 16, 16]), ALU.mult),
                                [oh.k, tif.k], [oh.k])
                            P.op("dve", lambda e, pp_=pp_: e.tensor_reduce(e01[:, pp_, :], oh[:].rearrange("p h k i -> p (h k) i"), AX.X, ALU.add),
                                 [oh.k], [e01.k])
                        P.op("dve", lambda e: e.scalar_tensor_tensor(idxf[:], e01[:, 0, :], 128.0, e01[:, 1, :], ALU.mult, ALU.add),
                             [e01.k], [idxf.k])
                        P.op("dve", lambda e: e.tensor_copy(idxi[:], idxf[:]), [idxf.k], [idxi.k])
                        P.op("dve", lambda e: e.tensor_tensor(gsm[:], best[:], best[:, :, 0:1].to_broadcast([128, 8, 16]), ALU.subtract),
                             [best.k], [gsm.k])
                        P.op("act", lambda e: e.activation(out=gsm[:], in_=gsm[:], func=AF.Exp), [gsm.k], [gsm.k])
                        P.op("dve", lambda e: e.tensor_reduce(gss[:], gsm[:], AX.X, ALU.add), [gsm.k], [gss.k])
                        P.op("dve", lambda e: e.reciprocal(gss[:], gss[:]), [gss.k], [gss.k])
                        P.op("dve", lambda e: e.tensor_tensor(gsm[:], gsm[:], gss[:].unsqueeze(2).to_broadcast([128, 8, 16]), ALU.mult),
                             [gsm.k, gss.k], [gsm.k])
                        NS = peer_slots
                        gflat = lambda: gsm[:].rearrange("p h k -> p (h k)")
                        for s_ in range(NS + 1):
                            if s_ < NS:
                                uv = UV[s_ % NG]
                                P.dma("pool", lambda e, s_=s_, uv=uv: e.indirect_dma_start(
                                    out=uv[:].rearrange("p a d -> p (a d)"), out_offset=None, in_=uvb,
                                    in_offset=bass.IndirectOffsetOnAxis(ap=idxi[:, s_:s_ + 1], axis=0)), [idxi.k, uvb_tok], [uv.k])
                                P.op("dve", lambda e, s_=s_, uv=uv: e.scalar_tensor_tensor(
                                    prod[:], uv[:, 0, :], 1.0, xn[:], ALU.mult, ALU.mult, accum_out=hid[:, s_:s_ + 1]),
                                    [uv.k, xn.k], [prod.k, hid.k])
                            if s_ >= 1:
                                t_ = s_ - 1
                                uvp = UV[t_ % NG]
                                P.op("act", lambda e, t_=t_: e.activation(out=wgt[:, t_:t_ + 1], in_=hid[:, t_:t_ + 1], func=AF.Gelu),
                                     [hid.k], [wgt.k])
                                P.op("dve", lambda e, t_=t_: e.tensor_tensor(wgt[:, t_:t_ + 1], wgt[:, t_:t_ + 1], gflat()[:, t_:t_ + 1], ALU.mult),
                                     [wgt.k, gsm.k], [wgt.k])
                                dg = dgs[t_ % 2]
                                P.op("act", lambda e, t_=t_, dg=dg: e.activation(out=dg[:], in_=identb[:], func=AF.Copy, scale=wgt[:, t_:t_ + 1]),
                                     [identb.k, wgt.k], [dg.k])
                                for hf in range(2):
                                    P.op("pe", lambda e, t_=t_, dg=dg, uvp=uvp, hf=hf, NS=NS: e.matmul(
                                        ps[2 + hf][:], dg[:], uvp[:, 1, hf * 512:(hf + 1) * 512], start=(t_ == 0), stop=(t_ == NS - 1)),
                                        [dg.k, uvp.k], [ps[2 + hf].k])
                        for hf in range(2):
                            P.op("dve", lambda e, hf=hf, h=h: e.tensor_tensor(h[:, hf * 512:(hf + 1) * 512], h[:, hf * 512:(hf + 1) * 512], ps[2 + hf][:], ALU.add),
                                 [h.k, ps[2 + hf].k], [h.k])
                    rmsnorm(h, gple, nb[b], junk, ssb[b], 0)
                    transpose_to(nb[b], 8, lambda b=b: nT[b][:], nT[b].k)
                    P.op("dve", lambda e, b=b: e.tensor_copy(pb[b][:], pin[b][:]), [pin[b].k], [pb[b].k])
                    transpose_to(pb[b], 2, lambda b=b: pT2[b][:], pT2[b].k)
                    for half in range(2):
                        cs = slice(half * 512, (half + 1) * 512)
                        pg = ps[half]
                        for kc in range(8):
                            P.op("pe", lambda e, b=b, kc=kc, cs=cs, pg=pg: e.matmul(
                                pg[:], nT[b][:, kc, :], wg[:, kc, cs], start=(kc == 0), stop=(kc == 7)),
                                [nT[b].k, wg.k], [pg.k])
                        P.op("act", lambda e, b=b, cs=cs, pg=pg: e.activation(out=gate[b][:, cs], in_=pg[:],
                                                                              func=AF.Sigmoid),
                             [pg.k], [gate[b].k])
                        pp = ps[2 + half]
                        for kc in range(2):
                            P.op("pe", lambda e, b=b, kc=kc, cs=cs, pp=pp: e.matmul(
                                pp[:], pT2[b][:, kc, :], wp[:, kc, cs], start=(kc == 0), stop=(kc == 1)),
                                [pT2[b].k, wp.k], [pp.k])
                        P.op("dve", lambda e, b=b, cs=cs, pp=pp: e.tensor_tensor(gate[b][:, cs], gate[b][:, cs], pp[:],
                                                                                 ALU.mult),
                             [gate[b].k, pp.k], [gate[b].k])
                    P.op("dve", lambda e, b=b, h=h: e.tensor_tensor(h3[b][:], h[:], gate[b][:], ALU.add),
                         [h.k, gate[b].k], [h3[b].k])
                    rmsnorm(h3[b], gfin, ob[b], junk, ssb[b], 1)
                    P.dma("sp", lambda e, b=b, rows=rows: e.dma_start(out=out_d[rows, :], in_=ob[b][:]),
                          [ob[b].k], [out_tok])
        if not rw_stop:
            _phase_c()
        P.finish([out_tok, DBG_TOK])
        P.emit()
    return nc


_NC_CACHE = {}


def _masks(hh):
    p = np.arange(128)[:, None, None]
    jj = np.arange(8)[None, :, None]
    f = np.arange(512)[None, None, :]
    return ((128 * jj + p) <= (512 * hh + f)).astype(np.float32).reshape(128, 8 * 512)


def _rw_consts():
    s = np.arange(128)[:, None]
    t = np.arange(128)[None, :]
    same = (s // 64) == (t // 64)
    su = (same & (s < t)).astype(np.float32)
    ui = (same & (s <= t)).astype(np.float32)
    sl = (same & (s > t)).astype(np.float32)
    c = np.zeros((128, 1280), np.float32)
    c[:, 0:128] = su; c[:, 128:256] = ui; c[:, 256:384] = su; c[:, 384:512] = ui
    c[:, 512:640] = sl; c[:, 640:768] = sl
    c[:, 768:896] = ui
    c[:, 896:1024] = same.astype(np.float32)
    c[:, 1024:1088] = (np.arange(128)[:, None] % 64 == np.arange(64)[None, :]).astype(np.float32)
    return c


def kernel(**inputs):
    f32 = lambda a: np.ascontiguousarray(np.asarray(a, dtype=np.float32))
    x = f32(inputs["x"])
    p = f32(inputs["p"])[0]
    pos = np.ascontiguousarray(np.asarray(inputs["positions"], dtype=np.int32))
    B = x.shape[0]
    key = "nc"
    if key not in _NC_CACHE:
        _NC_CACHE[key] = build(**_BUILD_FLAGS)
    nc = _NC_CACHE[key]
    ident = np.eye(128, dtype=np.float32)
    cvec = np.zeros((128, 4), np.float32)
    for q in range(128):
        d = q % 64
        cvec[q, 0] = INVF[d % 8] if d < 16 else 0.0
        cvec[q, 1] = -1.0 if d < 8 else 1.0
    lamv = np.concatenate([f32(inputs["lam_q1"])[0], f32(inputs["lam_k1"])[0],
                           f32(inputs["lam_q2"])[0], f32(inputs["lam_k2"])[0]]).reshape(1, 256)
    shared = {
        "ident": ident, "cvec": cvec,
        "norm_mix_g": f32(inputs["norm_mix_g"]).reshape(1, D),
        "w_in": f32(inputs["w_in"][0]),
        "lamv": lamv,
        "da_subln_g": f32(inputs["da_subln_g"]).reshape(128, 1),
        "w_out": f32(inputs["w_out"][0]),
        "norm_ple_g": f32(inputs["norm_ple_g"]).reshape(1, D),
        "norm_final_g": f32(inputs["norm_final_g"]).reshape(1, D),
        "ple_gate_w": f32(inputs["ple_gate_w"][0]),
        "ple_proj_w": f32(inputs["ple_proj_w"][0]),
    }
    masks = [_masks(0), _masks(1)]
    for nm in ("rw_mu", "rw_w0", "rw_a0", "rw_k_k", "rw_k_a", "rw_ln_g", "rw_ln_b"):
        shared[nm] = f32(inputs[nm]).reshape(1, -1)
    shared["rw_r_k"] = f32(inputs["rw_r_k"]).reshape(1, 512)
    shared["rw_w_up"] = f32(inputs["rw_w_up"][0])
    shared["rw_a_up"] = f32(inputs["rw_a_up"][0])
    shared["rw_g_up"] = f32(inputs["rw_g_up"][0])
    shared["rwc"] = _rw_consts()
    shared["norm_ffn_g"] = f32(inputs["norm_ffn_g"]).reshape(1, D)
    shared["peer_w_q"] = f32(inputs["peer_w_q"][0])
    shared["peer_sub_keys"] = f32(inputs["peer_sub_keys"][0]).reshape(16, 128, 128)
    shared["peer_u"] = f32(inputs["peer_u"][0])
    shared["peer_v"] = f32(inputs["peer_v"][0])
    shared["iota16"] = np.tile(np.arange(16, dtype=np.float32)[None, :], (128, 1))
    in_maps = []
    for c in range(8):
        b, hh = c // 2, c % 2
        m = dict(shared)
        m["x_full"] = x[b]
        m["pos_full"] = pos[b].reshape(1, SEQ)
        m["x_own"] = np.ascontiguousarray(x[b].reshape(8, 2, 512, D)[:, hh].reshape(NOWN, D))
        m["pos_own"] = np.ascontiguousarray(pos[b].reshape(8, 2, 512)[:, hh].reshape(1, NOWN))
        m["p_own"] = np.ascontiguousarray(p[b].reshape(8, 2, 512, 256)[:, hh].reshape(NOWN, 256))
        m["maskT"] = masks[hh]
        m["sel"] = np.full((128, 1), float(hh), np.float32)
        in_maps.append(m)
    res = run_bass_kernel_spmd(nc, in_maps, core_ids=list(range(8)))
    out = np.empty((B, SEQ, D), dtype=np.float32)
    for c in range(8):
        b, hh = c // 2, c % 2
        out[b].reshape(8, 2, 512, D)[:, hh] = np.asarray(res.results[c]["out"]).reshape(8, 512, D)
    return out


_BUILD_FLAGS = dict(do_da=True, do_rw=True, do_peer=True)
```
